# Optimizing a Trainium2 kernel written in Bass

```python
import jax, jax.numpy as jnp
from jax import lax
import numpy as np

D_MODEL = 2048
BATCH = 8
SEQ = 2048
DEPTH = 2

GRID_W = 64
CTX_LEN = 256
Q_BLOCK = 128
HEAD_DIM = 128
BRANCH_W = 512
N_BRANCHES = 4
MIX_W = N_BRANCHES * BRANCH_W
A_HEADS = 4
A_KV_HEADS = 2
B_HEADS = 4
NA_KH = 8
NA_KW = 16
C_HEADS = 4
C_KV_HEADS = 2
WINDOW = 128
D_HEADS = 4
MLA_KV_RANK = 512
MLA_NOPE = 128
MLA_ROPE = 64
MLA_V = 128
MLA_QK = MLA_NOPE + MLA_ROPE
ROPE_THETA = 10000.0
EPS = 1e-6
IN_SPLITS = (
    A_HEADS * HEAD_DIM, A_KV_HEADS * HEAD_DIM, A_KV_HEADS * HEAD_DIM, BRANCH_W,
    B_HEADS * HEAD_DIM, B_HEADS * HEAD_DIM, B_HEADS * HEAD_DIM, BRANCH_W,
    C_HEADS * HEAD_DIM, C_KV_HEADS * HEAD_DIM, C_KV_HEADS * HEAD_DIM, BRANCH_W,
    D_HEADS * MLA_QK, MLA_KV_RANK, MLA_ROPE, BRANCH_W,
)
IN_COLS = sum(IN_SPLITS)

kernel_name = 'hybrid_parallel_heads_flow_block'


def rms_norm(x, g):
    xf = x.astype(jnp.float32)
    y = xf * lax.rsqrt(jnp.mean(xf * xf, axis=-1, keepdims=True) + EPS)
    return (y * g.astype(jnp.float32)).astype(x.dtype)


def heads(t, n):
    return t.reshape(t.shape[:-1] + (n, t.shape[-1] // n))


def split_cols(y):
    return jnp.split(y, np.cumsum(IN_SPLITS)[:-1].tolist(), axis=-1)


def axial_rope_tables(n_tokens, rot_dim):
    t = jnp.arange(n_tokens)
    row = (t // GRID_W).astype(jnp.float32)
    col = (t % GRID_W).astype(jnp.float32)
    n_freq = rot_dim // 4
    inv_freq = ROPE_THETA ** (-jnp.arange(n_freq, dtype=jnp.float32) / n_freq)
    ang = jnp.concatenate([row[:, None] * inv_freq, col[:, None] * inv_freq], axis=-1)
    return jnp.cos(ang), jnp.sin(ang)


def apply_rope(x, cos, sin):
    x1, x2 = jnp.split(x.astype(jnp.float32), 2, axis=-1)
    cs, sn = cos[None, :, None, :], sin[None, :, None, :]
    return jnp.concatenate([x1 * cs - x2 * sn, x1 * sn + x2 * cs], axis=-1).astype(x.dtype)


def softmax_with_sink(s, sink):
    if sink is None:
        return jax.nn.softmax(s, axis=-1)
    sk = sink.astype(jnp.float32).reshape(s.shape[1], s.shape[2], 1, 1)
    m = jnp.maximum(jnp.max(s, axis=-1, keepdims=True), sk)
    e = jnp.exp(s - m)
    return e / (jnp.sum(e, axis=-1, keepdims=True) + jnp.exp(sk - m))


def to_blocks(q):
    B, S = q.shape[:2]
    return q.reshape((B, S // Q_BLOCK, Q_BLOCK) + q.shape[2:]).swapaxes(0, 1)


def from_blocks(o):
    o = o.swapaxes(0, 1)
    return o.reshape(o.shape[0], o.shape[1] * o.shape[2], -1)


def ctx_attn(q, k, v, sink=None):
    Hq, dq = q.shape[2:]
    Hkv = k.shape[2]
    qg = q.reshape(q.shape[:2] + (Hkv, Hq // Hkv, dq))
    s = jnp.einsum('bqhgd,bkhd->bhgqk', qg, k, preferred_element_type=jnp.float32) * dq ** -0.5
    p = softmax_with_sink(s, sink).astype(v.dtype)
    o = jnp.einsum('bhgqk,bkhd->bqhgd', p, v)
    return o.reshape(o.shape[:2] + (-1,))


def global_attn_blocks(q, k, v):
    Hq, dq = q.shape[2:]
    Hkv = k.shape[2]
    qb = to_blocks(q.reshape(q.shape[:2] + (Hkv, Hq // Hkv, dq)))

    def block(qi):
        s = jnp.einsum('bqhgd,bkhd->bhgqk', qi, k, preferred_element_type=jnp.float32) * dq ** -0.5
        p = jax.nn.softmax(s, axis=-1).astype(v.dtype)
        return jnp.einsum('bhgqk,bkhd->bqhgd', p, v)

    return from_blocks(lax.map(block, qb))


def window_attn_blocks(q, k, v, kc, vc, sink):
    S = q.shape[1]
    Hq, dq = q.shape[2:]
    Hkv = k.shape[2]
    L = kc.shape[1]
    span = Q_BLOCK + 2 * WINDOW
    pad = ((0, 0), (WINDOW, WINDOW), (0, 0), (0, 0))
    kp, vp = jnp.pad(k, pad), jnp.pad(v, pad)
    qb = to_blocks(q.reshape(q.shape[:2] + (Hkv, Hq // Hkv, dq)))
    scale = dq ** -0.5

    def block(args):
        i, qi = args
        kb = lax.dynamic_slice_in_dim(kp, i * Q_BLOCK, span, axis=1)
        vb = lax.dynamic_slice_in_dim(vp, i * Q_BLOCK, span, axis=1)
        qpos = i * Q_BLOCK + jnp.arange(Q_BLOCK)
        kpos = i * Q_BLOCK - WINDOW + jnp.arange(span)
        valid = (jnp.abs(qpos[:, None] - kpos[None, :]) <= WINDOW) & (kpos >= 0)[None, :] & (kpos < S)[None, :]
        s_lat = jnp.einsum('bqhgd,bkhd->bhgqk', qi, kb, preferred_element_type=jnp.float32) * scale
        s_lat = jnp.where(valid, s_lat, -jnp.inf)
        s_ctx = jnp.einsum('bqhgd,bkhd->bhgqk', qi, kc, preferred_element_type=jnp.float32) * scale
        p = softmax_with_sink(jnp.concatenate([s_ctx, s_lat], axis=-1), sink).astype(v.dtype)
        return (jnp.einsum('bhgqk,bkhd->bqhgd', p[..., :L], vc)
                + jnp.einsum('bhgqk,bkhd->bqhgd', p[..., L:], vb))

    return from_blocks(lax.map(block, (jnp.arange(S // Q_BLOCK), qb)))


def neighbourhood_attn_blocks(q, k, v, kc, vc, rpb):
    S = q.shape[1]
    dq = q.shape[3]
    L = kc.shape[1]
    rows = S // GRID_W
    kh = min(NA_KH, rows)
    q_rows = Q_BLOCK // GRID_W
    strip_rows = min(kh + q_rows - 1, rows)
    strip = strip_rows * GRID_W
    qb = to_blocks(q)
    scale = dq ** -0.5
    q_local = jnp.arange(Q_BLOCK)
    k_local = jnp.arange(strip)

    def block(args):
        i, qi = args
        r0 = i * q_rows
        strip_start = jnp.minimum(jnp.clip(r0 - kh // 2, 0, rows - kh), rows - strip_rows)
        kb = lax.dynamic_slice_in_dim(k, strip_start * GRID_W, strip, axis=1)
        vb = lax.dynamic_slice_in_dim(v, strip_start * GRID_W, strip, axis=1)
        qr, qcol = r0 + q_local // GRID_W, q_local % GRID_W
        kr, kcol = strip_start + k_local // GRID_W, k_local % GRID_W
        rs = jnp.clip(qr - kh // 2, 0, rows - kh)
        cs = jnp.clip(qcol - NA_KW // 2, 0, GRID_W - NA_KW)
        valid = ((kr[None, :] >= rs[:, None]) & (kr[None, :] < rs[:, None] + kh)
                 & (kcol[None, :] >= cs[:, None]) & (kcol[None, :] < cs[:, None] + NA_KW))
        dr = jnp.clip(kr[None, :] - qr[:, None] + NA_KH - 1, 0, 2 * NA_KH - 2)
        dc = jnp.clip(kcol[None, :] - qcol[:, None] + NA_KW - 1, 0, 2 * NA_KW - 2)
        bias = rpb[:, dr, dc].astype(jnp.float32)
        s_lat = jnp.einsum('bqhd,bkhd->bhqk', qi, kb, preferred_element_type=jnp.float32) * scale + bias[None]
        s_lat = jnp.where(valid, s_lat, -jnp.inf)
        s_ctx = jnp.einsum('bqhd,bkhd->bhqk', qi, kc, preferred_element_type=jnp.float32) * scale
        p = jax.nn.softmax(jnp.concatenate([s_ctx, s_lat], axis=-1), axis=-1).astype(v.dtype)
        return (jnp.einsum('bhqk,bkhd->bqhd', p[..., :L], vc)
                + jnp.einsum('bhqk,bkhd->bqhd', p[..., L:], vb))

    return from_blocks(lax.map(block, (jnp.arange(S // Q_BLOCK), qb)))


def mixer_a(p, pc, q_g, k_g, cos, sin, with_ctx):
    q, k, v, g = p
    qc, kc, vc, gc = pc
    q = apply_rope(rms_norm(heads(q, A_HEADS), q_g), cos, sin)
    k = apply_rope(rms_norm(heads(k, A_KV_HEADS), k_g), cos, sin)
    kc = rms_norm(heads(kc, A_KV_HEADS), k_g)
    v, vc = heads(v, A_KV_HEADS), heads(vc, A_KV_HEADS)
    o = global_attn_blocks(q, jnp.concatenate([kc, k], axis=1), jnp.concatenate([vc, v], axis=1)) * jax.nn.silu(g)
    if not with_ctx:
        return o, None
    oc = ctx_attn(rms_norm(heads(qc, A_HEADS), q_g), kc, vc) * jax.nn.silu(gc)
    return o, oc


def mixer_b(p, pc, q_g, k_g, rpb, with_ctx):
    q, k, v, g = p
    qc, kc, vc, gc = pc
    q = rms_norm(heads(q, B_HEADS), q_g)
    k = rms_norm(heads(k, B_HEADS), k_g)
    kc = rms_norm(heads(kc, B_HEADS), k_g)
    v, vc = heads(v, B_HEADS), heads(vc, B_HEADS)
    o = neighbourhood_attn_blocks(q, k, v, kc, vc, rpb) * jax.nn.silu(g)
    if not with_ctx:
        return o, None
    oc = ctx_attn(rms_norm(heads(qc, B_HEADS), q_g), kc, vc) * jax.nn.silu(gc)
    return o, oc


def mixer_c(p, pc, q_g, k_g, sink, cos, sin, with_ctx):
    q, k, v, g = p
    qc, kc, vc, gc = pc
    q = apply_rope(rms_norm(heads(q, C_HEADS), q_g), cos, sin)
    k = apply_rope(rms_norm(heads(k, C_KV_HEADS), k_g), cos, sin)
    kc = rms_norm(heads(kc, C_KV_HEADS), k_g)
    v, vc = heads(v, C_KV_HEADS), heads(vc, C_KV_HEADS)
    o = window_attn_blocks(q, k, v, kc, vc, sink) * jax.nn.silu(g)
    if not with_ctx:
        return o, None
    oc = ctx_attn(rms_norm(heads(qc, C_HEADS), q_g), kc, vc, sink) * jax.nn.silu(gc)
    return o, oc


def mixer_d(p, pc, q_g, k_g, kv_g, w_uk, w_uv, cos, sin, with_ctx):
    def keys_values(ckv, k_rope):
        ckv = rms_norm(ckv, kv_g)
        k_nope = heads(ckv @ w_uk, D_HEADS)
        v = heads(ckv @ w_uv, D_HEADS)
        k_rope = jnp.broadcast_to(k_rope[:, :, None, :], k_nope.shape[:-1] + (MLA_ROPE,))
        return rms_norm(jnp.concatenate([k_nope, k_rope], axis=-1), k_g), v

    def rope_tail(t):
        return jnp.concatenate([t[..., :MLA_NOPE], apply_rope(t[..., MLA_NOPE:], cos, sin)], axis=-1)

    q, ckv, k_rope, g = p
    qc, ckv_c, k_rope_c, gc = pc
    q = rope_tail(rms_norm(heads(q, D_HEADS), q_g))
    k, v = keys_values(ckv, k_rope)
    k = rope_tail(k)
    kc, vc = keys_values(ckv_c, k_rope_c)
    o = global_attn_blocks(q, jnp.concatenate([kc, k], axis=1), jnp.concatenate([vc, v], axis=1)) * jax.nn.silu(g)
    if not with_ctx:
        return o, None
    oc = ctx_attn(rms_norm(heads(qc, D_HEADS), q_g), kc, vc) * jax.nn.silu(gc)
    return o, oc


def setup_inputs(seed: int = 0) -> dict:
    key = jax.random.key(seed)
    ks = jax.random.split(key, 22)
    f32 = jnp.float32

    def normal(k, shape, scale=1.0):
        return scale * jax.random.normal(k, shape, f32)

    def gain(k, shape):
        return 1.0 + 0.05 * jax.random.normal(k, shape, f32)

    D = D_MODEL
    return {
        'x': normal(ks[0], (BATCH, SEQ, D)),
        'c': normal(ks[1], (BATCH, D)),
        'ctx': normal(ks[2], (BATCH, CTX_LEN, D)),
        'c_ctx': normal(ks[3], (D,)),
        'norm_g': gain(ks[4], (DEPTH, D)),
        'w_ada': normal(ks[5], (DEPTH, D, 3 * D), 0.5 * D ** -0.5),
        'b_ada': normal(ks[6], (DEPTH, 3 * D), 0.01),
        'w_in': normal(ks[7], (DEPTH, D, IN_COLS), D ** -0.5),
        'w_out': normal(ks[8], (DEPTH, MIX_W, D), MIX_W ** -0.5),
        'a_q_g': gain(ks[9], (DEPTH, HEAD_DIM)),
        'a_k_g': gain(ks[10], (DEPTH, HEAD_DIM)),
        'b_q_g': gain(ks[11], (DEPTH, HEAD_DIM)),
        'b_k_g': gain(ks[12], (DEPTH, HEAD_DIM)),
        'b_rpb': normal(ks[13], (DEPTH, B_HEADS, 2 * NA_KH - 1, 2 * NA_KW - 1), 0.1),
        'c_q_g': gain(ks[14], (DEPTH, HEAD_DIM)),
        'c_k_g': gain(ks[15], (DEPTH, HEAD_DIM)),
        'c_sink': normal(ks[16], (DEPTH, C_HEADS), 0.5),
        'd_q_g': gain(ks[17], (DEPTH, MLA_QK)),
        'd_k_g': gain(ks[18], (DEPTH, MLA_QK)),
        'd_kv_g': gain(ks[19], (DEPTH, MLA_KV_RANK)),
        'd_w_uk': normal(ks[20], (DEPTH, MLA_KV_RANK, D_HEADS * MLA_NOPE), MLA_KV_RANK ** -0.5),
        'd_w_uv': normal(ks[21], (DEPTH, MLA_KV_RANK, D_HEADS * MLA_V), MLA_KV_RANK ** -0.5),
    }


def reference(x, c, ctx, c_ctx, norm_g, w_ada, b_ada, w_in, w_out,
              a_q_g, a_k_g, b_q_g, b_k_g, b_rpb, c_q_g, c_k_g, c_sink,
              d_q_g, d_k_g, d_kv_g, d_w_uk, d_w_uv):
    S = x.shape[1]
    cos_h, sin_h = axial_rope_tables(S, HEAD_DIM)
    cos_r, sin_r = axial_rope_tables(S, MLA_ROPE)
    hc = ctx.astype(x.dtype)
    for l in range(DEPTH):
        with_ctx = l < DEPTH - 1
        sh_x, sc_x, gt_x = jnp.split((jax.nn.silu(c) @ w_ada[l] + b_ada[l])[:, None, :], 3, axis=-1)
        sh_c, sc_c, gt_c = jnp.split(jax.nn.silu(c_ctx) @ w_ada[l] + b_ada[l], 3, axis=-1)
        h_x = rms_norm(x, norm_g[l]) * (1 + sc_x) + sh_x
        h_c = rms_norm(hc, norm_g[l]) * (1 + sc_c) + sh_c
        px = split_cols(h_x @ w_in[l])
        pc = split_cols(h_c @ w_in[l])
        oa, oa_c = mixer_a(px[0:4], pc[0:4], a_q_g[l], a_k_g[l], cos_h, sin_h, with_ctx)
        ob, ob_c = mixer_b(px[4:8], pc[4:8], b_q_g[l], b_k_g[l], b_rpb[l], with_ctx)
        oc, oc_c = mixer_c(px[8:12], pc[8:12], c_q_g[l], c_k_g[l], c_sink[l], cos_h, sin_h, with_ctx)
        od, od_c = mixer_d(px[12:16], pc[12:16], d_q_g[l], d_k_g[l], d_kv_g[l], d_w_uk[l], d_w_uv[l],
                           cos_r, sin_r, with_ctx)
        x = x + gt_x * (jnp.concatenate([oa, ob, oc, od], axis=-1) @ w_out[l])
        if with_ctx:
            hc = hc + gt_c * (jnp.concatenate([oa_c, ob_c, oc_c, od_c], axis=-1) @ w_out[l])
    return x
```

```python
import numpy as np
from contextlib import ExitStack
import concourse.bass as bass
import concourse.mybir as mybir
from concourse.bass_utils import run_bass_kernel_spmd

F32 = mybir.dt.float32
BF16 = mybir.dt.bfloat16
AF = mybir.ActivationFunctionType
ALU = mybir.AluOpType

D = 2048
SEQ = 2048
CTXL = 256
T = SEQ + CTXL
NT = T // 128
NXT = SEQ // 128
DEPTH = 2
INC = 6976
EPS = 1e-6
NEG = -30000.0
NCORES = 8

COL = dict(a_q=0, a_k=512, a_v=768, a_g=1024, b_q=1536, b_k=2048, b_v=2560, b_g=3072,
           c_q=3584, c_k=4096, c_v=4352, c_g=4608, d_q=5120, d_ckv=5888, d_kr=6400, d_g=6464)
SQR = {}
_r = 0
for _n, _rows in (("a_q", 512), ("a_k", 256), ("b_q", 512), ("b_k", 512), ("c_q", 512), ("c_k", 256),
                  ("d_qn", 512), ("d_qr", 256), ("d_kn", 512), ("d_kr", 256), ("gate", 2048)):
    SQR[_n] = _r
    _r += _rows
SQ_ROWS = _r
SVC = dict(a_v=0, b_v=256, c_v=768, d_v=1024)
SV_COLS = 1536
NGAIN = 14
NPT = 19


class Buf:
    __slots__ = ("name", "w", "r", "sem", "cnt")

    def __init__(self, name):
        self.name = name
        self.w = None
        self.r = []
        self.sem = None
        self.cnt = 0


class Sched:
    ENG = ("pe", "act", "dve", "pool", "sp")

    def __init__(self, nc, stack):
        self.nc = nc
        self.stack = stack
        self.eng = {"pe": nc.tensor, "act": nc.scalar, "dve": nc.vector, "pool": nc.gpsimd, "sp": nc.sync}
        self.sems = {}
        self.count = {}
        for e in ("pe", "act", "dve", "pool"):
            self.sems[e] = stack.enter_context(nc.semaphore("prog_" + e))
            self.count[e] = 0
        self.seen = {e: {} for e in self.ENG}
        self.semcnt = {}
        self.freek = []
        self.stagek = []
        self.persist = True
        self.ninst = 0
        self.nwait = 0

    def _bufsem(self, b):
        if b.sem is None:
            if self.freek:
                key = self.freek.pop()
            else:
                key = "dsem%d" % len(self.semcnt)
                self.sems[key] = self.stack.enter_context(self.nc.semaphore(key))
                self.semcnt[key] = 0
            b.sem = key
            if not self.persist:
                self.stagek.append(key)
        return b.sem

    def end_stage(self):
        self.barrier()
        self.freek.extend(self.stagek)
        self.stagek = []

    def _wait(self, engine, deps):
        need = {}
        for t in deps:
            if t is None:
                continue
            k, v = t
            if need.get(k, 0) < v:
                need[k] = v
        seen = self.seen[engine]
        for k, v in need.items():
            if seen.get(k, 0) >= v:
                continue
            self.eng[engine].wait_ge(self.sems[k], v)
            self.nwait += 1
            seen[k] = v

    def op(self, engine, fn, reads=(), writes=()):
        deps = []
        own = set()
        for b in reads:
            deps.append(b.w)
            if b.w is not None and b.w[0] == engine:
                own.add(b.w)
        for b in writes:
            deps.append(b.w)
            deps.extend(b.r)
        deps = [t for t in deps if t is not None and (t[0] != engine or t in own)]
        self._wait(engine, deps)
        inst = fn(self.eng[engine])
        self.count[engine] += 1
        inst.then_inc(self.sems[engine], 1)
        tok = (engine, self.count[engine])
        for b in writes:
            b.w = tok
            b.r = []
        for b in reads:
            if b not in writes:
                b.r.append(tok)
        self.ninst += 1
        return inst

    def dma(self, fns, owner, reads=(), writes=(), queue="sp"):
        deps = []
        for b in reads:
            deps.append(b.w)
        for b in writes:
            deps.append(b.w)
            deps.extend(b.r)
        self._wait(queue, deps)
        key = self._bufsem(owner)
        for fn in fns:
            inst = fn(self.eng[queue])
            self.semcnt[key] += 16
            inst.then_inc(self.sems[key], 16)
            self.ninst += 1
        tok = (key, self.semcnt[key])
        for b in writes:
            b.w = tok
            b.r = []
        for b in reads:
            if b not in writes:
                b.r.append(tok)
        return tok

    def barrier(self):
        toks = [(e, self.count[e]) for e in ("pe", "act", "dve", "pool")]
        toks += [(k, v) for k, v in self.semcnt.items()]
        for e in self.ENG:
            self._wait(e, toks)


_UID = [0]


def _uniq(name):
    _UID[0] += 1
    return "%s_u%d" % (name, _UID[0])


class Ring:
    def __init__(self, nc, st, name, shape, dtype, n, psum=False):
        self.t = []
        for i in range(n):
            nm = _uniq("%s%d" % (name, i))
            if psum:
                t = st.enter_context(nc.psum_tensor(nm, shape, dtype))
            else:
                t = st.enter_context(nc.sbuf_tensor(nm, shape, dtype))
            self.t.append((t, Buf(nm)))
        self.i = 0

    def next(self):
        r = self.t[self.i % len(self.t)]
        self.i += 1
        return r


class Prog:
    def __init__(self, depth=DEPTH, debug=None):
        self.depth = depth
        self.debug = debug
        nc = bass.Bass("TRN2", target_bir_lowering=False)
        self.nc = nc

        def din(name, shape, dt=F32):
            return nc.dram_tensor(name, list(shape), dt, kind="ExternalInput").ap()

        self.x = din("x", [SEQ, D])
        self.ctx = din("ctx", [CTXL, D])
        self.cfm = din("cfm", [128, 32])
        self.norm_g = din("norm_g", [DEPTH, D])
        self.b_ada = din("b_ada", [DEPTH, 3 * D])
        self.w_ada = din("w_ada", [DEPTH, D, 3 * D])
        self.w_in = din("w_in", [DEPTH, D, INC])
        self.w_out = din("w_out", [DEPTH, D, D])
        self.w_uk = din("w_uk", [DEPTH, 512, 512])
        self.w_uv = din("w_uv", [DEPTH, 512, 512])
        self.gains = din("gains", [128, DEPTH * NGAIN])
        self.sink = din("sink", [1, DEPTH * 4])
        self.rpbx = din("rpbx", [DEPTH, 128, 4 * NPT * 64])
        self.maskb = din("maskb", [128, NPT * 64])
        self.maskc = din("maskc", [128, 256])
        self.ident = din("ident", [128, 128])
        self.rmat = din("rmat", [128, 192])
        self.rope_h = din("rope_h", [128, 2 * SEQ])
        self.rope_r = din("rope_r", [64, 2 * SEQ])
        self.out = nc.dram_tensor("out", [SEQ, D], F32, kind="ExternalOutput").ap()
        self.modv = nc.dram_tensor("modv", [DEPTH, 2, 3 * D], F32).ap()
        self.sq = nc.dram_tensor("sq", [SQ_ROWS, T], BF16).ap()
        self.sv = nc.dram_tensor("sv", [T, SV_COLS], BF16).ap()
        self.x1 = nc.dram_tensor("x1", [SEQ, D], F32).ap()
        self.hc1 = nc.dram_tensor("hc1", [CTXL, D], F32).ap()
        if debug:
            self.dbg = nc.dram_tensor("dbg", list(debug[1]), debug[2], kind="ExternalOutput").ap()

        with ExitStack() as st:
            self.st = st
            self.S = Sched(nc, st)
            self.build()

    def sb(self, st, name, shape, dt):
        return st.enter_context(self.nc.sbuf_tensor(_uniq(name), list(shape), dt))

    def load(self, st, name, shape, dt, src):
        t = self.sb(st, name, shape, dt)
        b = Buf(name)
        self.S.dma([lambda e: e.dma_start(out=t[:], in_=src)], b, writes=[b])
        return t, b

    def build(self):
        S, nc, st = self.S, self.nc, self.st
        self.big = self.sb(st, "big", [128, 16, T], BF16)
        self.bigb = [Buf("big%d" % i) for i in range(NT)]
        idf, idfb = self.load(st, "idf", [128, 128], F32, self.ident)
        rmf, rmfb = self.load(st, "rmf", [128, 192], F32, self.rmat)
        self.idb = self.sb(st, "idb", [128, 128], BF16)
        self.rmb = self.sb(st, "rmb", [128, 192], BF16)
        self.ones = self.sb(st, "ones", [128, 128], BF16)
        self.cb = Buf("consts")
        S.op("dve", lambda e: e.tensor_copy(self.idb[:], idf[:]), reads=[idfb], writes=[self.cb])
        S.op("dve", lambda e: e.tensor_copy(self.rmb[:], rmf[:]), reads=[rmfb], writes=[self.cb])
        S.op("dve", lambda e: e.memset(self.ones[:], 1.0), writes=[self.cb])
        self.gn, self.gnb = self.load(st, "gn", [128, DEPTH * NGAIN], F32, self.gains)
        S.barrier()
        S.persist = False
        for l in range(self.depth):
            with nc.named_scope("ada%d" % l):
                self.stage_ada(l)
        for l in range(self.depth):
            last = (l == DEPTH - 1)
            with nc.named_scope("norm%d" % l):
                self.stage_norm(l)
            if self.debug and self.debug[0] == "hT%d" % l:
                self.dump_big()
                return
            with nc.named_scope("proj%d" % l):
                self.stage_proj(l, last)
            if self.debug and self.debug[0] == "sq%d" % l:
                return self.dump_dram(self.sq)
            if self.debug and self.debug[0] == "sv%d" % l:
                return self.dump_dram(self.sv)
            with nc.named_scope("attn%d" % l):
                self.stage_attn(l, last)
            if self.debug and self.debug[0] == "mix%d" % l:
                self.dump_big()
                return
            with nc.named_scope("out%d" % l):
                self.stage_out(l, last)
            if self.debug and self.debug[0] == "xo%d" % l:
                return self.dump_dram(self.x1 if not last else self.out)

    def dump_big(self):
        S = self.S
        b = Buf("dump")
        S.dma([lambda e: e.dma_start(out=self.dbg, in_=self.big[:])], b)
        S.barrier()

    def dump_dram(self, src):
        S = self.S
        b = Buf("dump")
        S.dma([lambda e: e.dma_start(out=self.dbg, in_=src)], b)
        S.barrier()

    class WS:
        def __init__(self, P, st, kch, nwb=3, conv=("pool",)):
            self.P = P
            self.kch = kch
            self.stage = Ring(P.nc, st, "wst", [128, kch, 128], F32, 2)
            self.wb = Ring(P.nc, st, "wbb", [128, kch, 128], BF16, nwb)
            self.conv = conv
            self.n = 0

        def fetch(self, w2d, c0, ncols, kch=None):
            P, S = self.P, self.P.S
            kch = kch or self.kch
            stg, bs = self.stage.next()
            wb, bw = self.wb.next()
            src = w2d[:, c0:c0 + ncols].rearrange("(kc p) n -> p kc n", p=128)
            S.dma([lambda e: e.dma_start(out=stg[:, :kch, :ncols], in_=src)], bs, writes=[bs])
            eng = self.conv[self.n % len(self.conv)]
            self.n += 1
            if eng == "act":
                S.op("act", lambda e: e.activation(out=wb[:, :kch, :ncols], in_=stg[:, :kch, :ncols], func=AF.Copy),
                     reads=[bs], writes=[bw])
            else:
                S.op(eng, lambda e: e.tensor_copy(wb[:, :kch, :ncols], stg[:, :kch, :ncols]), reads=[bs], writes=[bw])
            return wb, bw

    def stage_ada(self, l):
        S, nc = self.S, self.nc
        with ExitStack() as st:
            cf, cfb = self.load(st, "cf", [128, 32], F32, self.cfm)
            s2 = self.sb(st, "s2", [128, 16, 2], BF16)
            s2b = Buf("s2")
            sil = self.sb(st, "sil", [128, 32], F32)
            silb = Buf("sil")
            S.op("act", lambda e: e.activation(out=sil[:], in_=cf[:], func=AF.Silu), reads=[cfb], writes=[silb])
            S.op("dve", lambda e: e.tensor_copy(s2[:, :, 0], sil[:, 0:16]), reads=[silb], writes=[s2b])
            S.op("dve", lambda e: e.tensor_copy(s2[:, :, 1], sil[:, 16:32]), reads=[silb], writes=[s2b])
            modrow = self.sb(st, "modrow", [2, 3 * D], F32)
            mrb = Buf("modrow")
            badd, baddb = self.load(st, "badd", [2, 3 * D], F32, self.b_ada[l].partition_broadcast(2))
            g2, g2b = self.load(st, "g2", [2, D], F32, self.norm_g[l].partition_broadcast(2))
            ps = Ring(nc, st, "psA", [128, 512], F32, 2, psum=True)
            stg = Ring(nc, st, "adst", [128, 16, 256], F32, 2)
            wbr = Ring(nc, st, "adwb", [128, 16, 256], BF16, 2)
            w2d = self.w_ada[l]
            nsl = 3 * D // 256

            def fetch(i):
                t, b = stg.next()
                wb, bw = wbr.next()
                src = w2d[:, i * 256:(i + 1) * 256].rearrange("(kc p) n -> p kc n", p=128)
                S.dma([lambda e: e.dma_start(out=t[:, 0:8, :], in_=src[:, 0:8, :]),
                       lambda e: e.dma_start(out=t[:, 8:16, :], in_=src[:, 8:16, :])], b, writes=[b])
                S.op("pool", lambda e: e.tensor_copy(wb[:, 0:6, :], t[:, 0:6, :]), reads=[b], writes=[bw])
                S.op("act", lambda e: e.activation(out=wb[:, 6:11, :], in_=t[:, 6:11, :], func=AF.Copy), reads=[b], writes=[bw])
                S.op("dve", lambda e: e.tensor_copy(wb[:, 11:16, :], t[:, 11:16, :]), reads=[b], writes=[bw])
                return wb, bw

            q = [fetch(0)]
            for i in range(nsl):
                wb, bw = q.pop(0)
                if i + 1 < nsl:
                    q.append(fetch(i + 1))
                pt, pb = ps.next()
                for k in range(16):
                    S.op("pe", lambda e, k=k: e.matmul(pt[0:2, 0:256], lhsT=s2[:, k, :], rhs=wb[:, k, :],
                                                      start=(k == 0), stop=(k == 15)),
                         reads=[s2b, bw], writes=[pb])
                S.op("dve", lambda e, i=i: e.tensor_copy(modrow[:, i * 256:(i + 1) * 256], pt[0:2, 0:256]),
                     reads=[pb], writes=[mrb])
            S.op("dve", lambda e: e.tensor_tensor(modrow[:], modrow[:], badd[:], ALU.add), reads=[mrb, baddb], writes=[mrb])
            S.op("dve", lambda e: e.scalar_tensor_tensor(out=modrow[:, D:2 * D], in0=modrow[:, D:2 * D], scalar=1.0,
                                                         in1=g2[:], op0=ALU.add, op1=ALU.mult),
                 reads=[mrb, g2b], writes=[mrb])
            S.dma([lambda e: e.dma_start(out=self.modv[l], in_=modrow[:])], mrb, reads=[mrb])
            S.end_stage()

    def stage_norm(self, l):
        S, nc = self.S, self.nc
        with ExitStack() as st:
            xr = Ring(nc, st, "xt", [128, D], F32, 3)
            junk = self.sb(st, "junk", [128, D], BF16)
            junkb = Buf("junk")
            tf = Ring(nc, st, "tf", [128, D], F32, 2)
            hb = Ring(nc, st, "hb", [128, D], BF16, 2)
            ssr = Ring(nc, st, "ssn", [128, 2], F32, 4)
            abc = self.sb(st, "abc", [128, D], F32)
            bbc = self.sb(st, "bbc", [128, D], F32)
            abcb, bbcb = Buf("abc"), Buf("bbc")
            pst = Ring(nc, st, "psT", [128, 8, 128], BF16, 4, psum=True)
            xsrc = self.x if l == 0 else self.x1
            csrc = self.ctx if l == 0 else self.hc1
            for t in range(NT):
                if t == 0 or t == NXT:
                    w = 0 if t == 0 else 1
                    S.dma([lambda e, w=w: e.dma_start(out=abc[:], in_=self.modv[l, w, D:2 * D].partition_broadcast(128))],
                          abcb, writes=[abcb])
                    S.dma([lambda e, w=w: e.dma_start(out=bbc[:], in_=self.modv[l, w, 0:D].partition_broadcast(128))],
                          bbcb, writes=[bbcb])
                src = xsrc[t * 128:(t + 1) * 128, :] if t < NXT else csrc[(t - NXT) * 128:(t - NXT + 1) * 128, :]
                xt, xb = xr.next()
                S.dma([lambda e, xt=xt, src=src: e.dma_start(out=xt[:], in_=src)], xb, writes=[xb])
                ss, ssb = ssr.next()
                S.op("act", lambda e, xt=xt, ss=ss: e.activation(out=junk[:], in_=xt[:], func=AF.Square, accum_out=ss[:, 0:1]),
                     reads=[xb], writes=[junkb, ssb])
                S.op("act", lambda e, ss=ss: e.activation(out=ss[:, 1:2], in_=ss[:, 0:1], func=AF.Sqrt, bias=EPS, scale=1.0 / D),
                     reads=[ssb], writes=[ssb])
                S.op("dve", lambda e, ss=ss: e.reciprocal(ss[:, 1:2], ss[:, 1:2]), reads=[ssb], writes=[ssb])
                t1, t1b = tf.next()
                S.op("dve", lambda e, xt=xt, ss=ss, t1=t1: e.scalar_tensor_tensor(out=t1[:], in0=xt[:], scalar=ss[:, 1:2], in1=abc[:],
                                                                               op0=ALU.mult, op1=ALU.mult),
                     reads=[xb, ssb, abcb], writes=[t1b])
                h, hbb = hb.next()
                S.op("pool" if t % 2 else "dve", lambda e, t1=t1, h=h: e.tensor_tensor(h[:], t1[:], bbc[:], ALU.add),
                     reads=[t1b, bbcb], writes=[hbb])
                for g in range(2):
                    pt, pb = pst.next()
                    for i in range(8):
                        c = g * 8 + i
                        S.op("pe", lambda e, pt=pt, i=i, c=c, h=h: e.transpose(pt[:, i, :], h[:, c * 128:(c + 1) * 128], self.idb[:]),
                             reads=[hbb, self.cb], writes=[pb])
                    eng = "act" if g == 0 else "dve"
                    if eng == "act":
                        S.op("act", lambda e, pt=pt, g=g, t=t: e.activation(out=self.big[:, g * 8:(g + 1) * 8, t * 128:(t + 1) * 128],
                                                                          in_=pt[:], func=AF.Copy),
                             reads=[pb], writes=[self.bigb[t]])
                    else:
                        S.op("dve", lambda e, pt=pt, g=g, t=t: e.tensor_copy(self.big[:, g * 8:(g + 1) * 8, t * 128:(t + 1) * 128], pt[:]),
                             reads=[pb], writes=[self.bigb[t]])
            S.end_stage()

    def stage_proj(self, l, last):
        S, nc = self.S, self.nc
        G0 = l * NGAIN
        w2d = self.w_in[l]
        chunks = [(0, 512), (512, 512), (1024, 512), (1536, 512), (2048, 256)]
        with ExitStack() as st:
            ws = Prog.WS(self, st, 16, nwb=5, conv=("pool",))
            self.ps_p = Ring(nc, st, "psP", [128, 512], F32, 4, psum=True)
            self.ps_s = Ring(nc, st, "psS", [128, 512], F32, 2, psum=True)
            self.ps_r = Ring(nc, st, "psR", [128, 512], F32, 2, psum=True)
            self.sqr = Ring(nc, st, "sqr", [128, 512], BF16, 4)
            self.f32r = Ring(nc, st, "f32r", [128, 512], F32, 8)
            self.obr = Ring(nc, st, "obr", [128, 512], BF16, 6)
            self.qfr = Ring(nc, st, "qfr", [128, 512], F32, 5)
            vout = Ring(nc, st, "vout", [128, 128], BF16, 4)

            def feat_group(wb, bw, m0, m, tok0, n):
                pt, pb = self.ps_p.next()
                tiles = [self.bigb[i] for i in range(tok0 // 128, (tok0 + n) // 128)]
                for k in range(16):
                    S.op("pe", lambda e, k=k: e.matmul(pt[0:m, 0:n], lhsT=wb[:, k, m0:m0 + m], rhs=self.big[:, k, tok0:tok0 + n],
                                                      start=(k == 0), stop=(k == 15)),
                         reads=[bw] + tiles, writes=[pb])
                return pt, pb

            with ExitStack() as st2:
                tab, tabb = self.load(st2, "ropeh", [128, 2 * SEQ], F32, self.rope_h)
                qk = []
                for (nm, nh, gcol, rope) in (("a_q", 4, 0, True), ("a_k", 2, 1, True), ("b_q", 4, 2, False), ("b_k", 4, 3, False),
                                             ("c_q", 4, 4, True), ("c_k", 2, 5, True)):
                    for h in range(nh):
                        qk.append((COL[nm] + h * 128, SQR[nm] + h * 128, gcol, rope))
                nxt = ws.fetch(w2d, qk[0][0], 128)
                for i, (c0, r0, gcol, rope) in enumerate(qk):
                    wb, bw = nxt
                    if i + 1 < len(qk):
                        nxt = ws.fetch(w2d, qk[i + 1][0], 128)
                    for (tok0, n) in chunks:
                        pt, pb = feat_group(wb, bw, 0, 128, tok0, n)
                        pt, pb = self.evac(pt, pb, 128, n)
                        rstd, rb = self.rstd_of([(pt[0:128, 0:n], 128, pb)], 128.0, n)
                        self.finish_qk(pt[0:128, 0:n], 128, pb, self.gn[:, G0 + gcol:G0 + gcol + 1], rstd, rb,
                                       rope and tok0 < SEQ, tab, tabb, tok0, n, self.sq[r0:r0 + 128, tok0:tok0 + n], 128)
                S.barrier()
            self.proj_d(l, st, ws, w2d, chunks, feat_group)
            S.barrier()
            vg = []
            for nm, ncol in (("a_v", 256), ("b_v", 512), ("c_v", 256)):
                for j in range(ncol // 128):
                    vg.append((COL[nm] + j * 128, SVC[nm] + j * 128))
            nxt = ws.fetch(w2d, vg[0][0], 128)
            for i, (c0, v0) in enumerate(vg):
                wb, bw = nxt
                if i + 1 < len(vg):
                    nxt = ws.fetch(w2d, vg[i + 1][0], 128)
                for t in range(NT):
                    pt, pb = self.ps_p.next()
                    for k in range(16):
                        S.op("pe", lambda e, k=k, t=t: e.matmul(pt[:, 0:128], lhsT=self.big[:, k, t * 128:(t + 1) * 128], rhs=wb[:, k, :],
                                                              start=(k == 0), stop=(k == 15)),
                             reads=[bw, self.bigb[t]], writes=[pb])
                    vo, vob = vout.next()
                    eng = "act" if t % 2 == 0 else "dve"
                    if eng == "act":
                        S.op("act", lambda e: e.activation(out=vo[:], in_=pt[:, 0:128], func=AF.Copy), reads=[pb], writes=[vob])
                    else:
                        S.op("dve", lambda e: e.tensor_copy(vo[:], pt[:, 0:128]), reads=[pb], writes=[vob])
                    S.dma([lambda e, t=t: e.dma_start(out=self.sv[t * 128:(t + 1) * 128, v0:v0 + 128], in_=vo[:])], vob, reads=[vob])
            gg = []
            for bi, nm in enumerate(("a_g", "b_g", "c_g", "d_g")):
                for h in range(4):
                    gg.append((COL[nm] + h * 128, SQR["gate"] + (bi * 4 + h) * 128))
            gchunks = chunks[:4] if last else chunks
            nxt = ws.fetch(w2d, gg[0][0], 128)
            for i, (c0, r0) in enumerate(gg):
                wb, bw = nxt
                if i + 1 < len(gg):
                    nxt = ws.fetch(w2d, gg[i + 1][0], 128)
                for (tok0, n) in gchunks:
                    pt, pb = feat_group(wb, bw, 0, 128, tok0, n)
                    o, ob = self.obr.next()
                    S.op("act", lambda e: e.activation(out=o[:, 0:n], in_=pt[:, 0:n], func=AF.Silu), reads=[pb], writes=[ob])
                    S.dma([lambda e: e.dma_start(out=self.sq[r0:r0 + 128, tok0:tok0 + n], in_=o[:, 0:n])], ob, reads=[ob])
            S.end_stage()

    def evac(self, pt, pb, nr, n):
        qf, qfb = self.qfr.next()
        self.S.op("act", lambda e: e.activation(out=qf[0:nr, 0:n], in_=pt[0:nr, 0:n], func=AF.Copy), reads=[pb], writes=[qfb])
        return qf, qfb

    def rstd_of(self, parts, dim, n):
        S = self.S
        st_, sb_ = self.ps_s.next()
        for i, (src, nr, b) in enumerate(parts):
            sq, sqb = self.sqr.next()
            S.op("act", lambda e: e.activation(out=sq[0:nr, 0:n], in_=src, func=AF.Square), reads=[b], writes=[sqb])
            S.op("pe", lambda e: e.matmul(st_[:, 0:n], lhsT=self.ones[0:nr, :], rhs=sq[0:nr, 0:n],
                                          start=(i == 0), stop=(i == len(parts) - 1)),
                 reads=[sqb, self.cb], writes=[sb_])
        r, rb = self.f32r.next()
        S.op("act", lambda e: e.activation(out=r[:, 0:n], in_=st_[:, 0:n], func=AF.Sqrt, bias=EPS, scale=1.0 / dim),
             reads=[sb_], writes=[rb])
        S.op("dve", lambda e: e.reciprocal(r[:, 0:n], r[:, 0:n]), reads=[rb], writes=[rb])
        return r, rb

    def finish_qk(self, src, nr, sbuf_, gain, rstd, rb, rope, tab, tabb, tok0, n, dst, rdim):
        S = self.S
        qn, qb = self.obr.next()
        S.op("dve", lambda e: e.scalar_tensor_tensor(out=qn[0:nr, 0:n], in0=src, scalar=gain[0:nr, :], in1=rstd[0:nr, 0:n],
                                                     op0=ALU.mult, op1=ALU.mult),
             reads=[sbuf_, rb, self.gnb], writes=[qb])
        if rope:
            rt, rtb = self.ps_r.next()
            rm = self.rmb[0:128, 0:128] if rdim == 128 else self.rmb[0:64, 128:192]
            S.op("pe", lambda e: e.matmul(rt[0:nr, 0:n], lhsT=rm, rhs=qn[0:nr, 0:n], start=True, stop=True),
                 reads=[qb, self.cb], writes=[rtb])
            t1, t1b = self.f32r.next()
            t2, t2b = self.f32r.next()
            S.op("dve", lambda e: e.tensor_tensor(t1[0:nr, 0:n], qn[0:nr, 0:n], tab[0:nr, tok0:tok0 + n], ALU.mult),
                 reads=[qb, tabb], writes=[t1b])
            S.op("dve", lambda e: e.tensor_tensor(t2[0:nr, 0:n], rt[0:nr, 0:n], tab[0:nr, SEQ + tok0:SEQ + tok0 + n], ALU.mult),
                 reads=[rtb, tabb], writes=[t2b])
            o, ob = self.obr.next()
            S.op("dve", lambda e: e.tensor_tensor(o[0:nr, 0:n], t1[0:nr, 0:n], t2[0:nr, 0:n], ALU.add),
                 reads=[t1b, t2b], writes=[ob])
        else:
            o, ob = qn, qb
        S.dma([lambda e: e.dma_start(out=dst, in_=o[0:nr, 0:n])], ob, reads=[ob])

    def proj_d(self, l, st_outer, ws, w2d, chunks, feat_group):
        S, nc = self.S, self.nc
        G0 = l * NGAIN
        with ExitStack() as st:
            tab, tabb = self.load(st, "roper", [64, 2 * SEQ], F32, self.rope_r)
            nxt_n = ws.fetch(w2d, COL["d_q"], 128)
            nxt_r = ws.fetch(w2d, COL["d_q"] + 128, 64)
            for h in range(4):
                (wn, bwn), (wr, bwr) = nxt_n, nxt_r
                for (tok0, n) in chunks:
                    pn, pnb = feat_group(wn, bwn, 0, 128, tok0, n)
                    pn, pnb = self.evac(pn, pnb, 128, n)
                    pr, prb = feat_group(wr, bwr, 0, 64, tok0, n)
                    pr, prb = self.evac(pr, prb, 64, n)
                    rstd, rb = self.rstd_of([(pn[0:128, 0:n], 128, pnb), (pr[0:64, 0:n], 64, prb)], 192.0, n)
                    self.finish_qk(pn[0:128, 0:n], 128, pnb, self.gn[:, G0 + 6:G0 + 7], rstd, rb, False, None, None, tok0, n,
                                   self.sq[SQR["d_qn"] + h * 128:SQR["d_qn"] + (h + 1) * 128, tok0:tok0 + n], 128)
                    self.finish_qk(pr[0:64, 0:n], 64, prb, self.gn[:, G0 + 7:G0 + 8], rstd, rb, tok0 < SEQ, tab, tabb, tok0, n,
                                   self.sq[SQR["d_qr"] + h * 64:SQR["d_qr"] + (h + 1) * 64, tok0:tok0 + n], 64)
                if h + 1 < 4:
                    nxt_n = ws.fetch(w2d, COL["d_q"] + (h + 1) * 192, 128)
                    nxt_r = ws.fetch(w2d, COL["d_q"] + (h + 1) * 192 + 128, 64)
            wck = [ws.fetch(w2d, COL["d_ckv"] + j * 128, 128) for j in range(4)]
            ckvn = self.sb(st, "ckvn", [128, 4, T], BF16)
            ckb = Buf("ckvn")
            kr = self.sb(st, "krraw", [64, T], F32)
            krb = Buf("krraw")
            wkr, wkrb = ws.fetch(w2d, COL["d_kr"], 64)
            for (tok0, n) in chunks:
                parts = []
                for j in range(4):
                    pt, pb = self.ps_p.next()
                    tiles = [self.bigb[i] for i in range(tok0 // 128, (tok0 + n) // 128)]
                    for k in range(16):
                        S.op("pe", lambda e, k=k, j=j, pt=pt: e.matmul(pt[:, 0:n], lhsT=wck[j][0][:, k, :],
                                                                   rhs=self.big[:, k, tok0:tok0 + n], start=(k == 0), stop=(k == 15)),
                             reads=[wck[j][1]] + tiles, writes=[pb])
                    pt, pb = self.evac(pt, pb, 128, n)
                    parts.append((pt[0:128, 0:n], 128, pb))
                rstd, rb = self.rstd_of(parts, 512.0, n)
                for j in range(4):
                    src, _, pb = parts[j]
                    S.op("dve", lambda e, j=j, src=src: e.scalar_tensor_tensor(out=ckvn[:, j, tok0:tok0 + n], in0=src,
                                                                           scalar=self.gn[:, G0 + 10 + j:G0 + 11 + j], in1=rstd[:, 0:n],
                                                                           op0=ALU.mult, op1=ALU.mult),
                         reads=[pb, rb, self.gnb], writes=[ckb])
                pr, prb = feat_group(wkr, wkrb, 0, 64, tok0, n)
                S.op("act", lambda e: e.activation(out=kr[:, tok0:tok0 + n], in_=pr[0:64, 0:n], func=AF.Copy), reads=[prb], writes=[krb])
            wsu = Prog.WS(self, st, 4, nwb=3, conv=("pool",))
            vout = Ring(nc, st, "voutd", [128, 128], BF16, 4)
            for h in range(4):
                wk, wkb = wsu.fetch(self.w_uk[l], h * 128, 128)
                wv, wvb = wsu.fetch(self.w_uv[l], h * 128, 128)
                for (tok0, n) in chunks:
                    pt, pb = self.ps_p.next()
                    for k in range(4):
                        S.op("pe", lambda e, k=k: e.matmul(pt[:, 0:n], lhsT=wk[:, k, :], rhs=ckvn[:, k, tok0:tok0 + n],
                                                          start=(k == 0), stop=(k == 3)),
                             reads=[wkb, ckb], writes=[pb])
                    pt, pb = self.evac(pt, pb, 128, n)
                    rstd, rb = self.rstd_of([(pt[0:128, 0:n], 128, pb), (kr[0:64, tok0:tok0 + n], 64, krb)], 192.0, n)
                    self.finish_qk(pt[0:128, 0:n], 128, pb, self.gn[:, G0 + 8:G0 + 9], rstd, rb, False, None, None, tok0, n,
                                   self.sq[SQR["d_kn"] + h * 128:SQR["d_kn"] + (h + 1) * 128, tok0:tok0 + n], 128)
                    self.finish_qk(kr[0:64, tok0:tok0 + n], 64, krb, self.gn[:, G0 + 9:G0 + 10], rstd, rb, tok0 < SEQ, tab, tabb, tok0, n,
                                   self.sq[SQR["d_kr"] + h * 64:SQR["d_kr"] + (h + 1) * 64, tok0:tok0 + n], 64)
                for t in range(NT):
                    pt, pb = self.ps_p.next()
                    for k in range(4):
                        S.op("pe", lambda e, k=k, t=t: e.matmul(pt[:, 0:128], lhsT=ckvn[:, k, t * 128:(t + 1) * 128], rhs=wv[:, k, :],
                                                              start=(k == 0), stop=(k == 3)),
                             reads=[wvb, ckb], writes=[pb])
                    vo, vob = vout.next()
                    S.op("dve", lambda e: e.tensor_copy(vo[:], pt[:, 0:128]), reads=[pb], writes=[vob])
                    v0 = SVC["d_v"] + h * 128
                    S.dma([lambda e, t=t: e.dma_start(out=self.sv[t * 128:(t + 1) * 128, v0:v0 + 128], in_=vo[:])], vob, reads=[vob])
            S.barrier()

    def stage_attn(self, l, last):
        S, nc = self.S, self.nc
        with ExitStack() as st:
            qr_ = Ring(nc, st, "aq", [128, T], BF16, 2)
            kr_ = Ring(nc, st, "ak", [128, T], BF16, 2)
            q2_ = Ring(nc, st, "aq2", [64, T], BF16, 2)
            k2_ = Ring(nc, st, "ak2", [64, T], BF16, 2)
            vr_ = Ring(nc, st, "av", [128, NT, 128], BF16, 2)
            gr_ = Ring(nc, st, "ag", [128, T], BF16, 2)
            self.pS = Ring(nc, st, "pS", [128, 512], F32, 4, psum=True)
            self.pO = Ring(nc, st, "pO", [128, 512], F32, 2, psum=True)
            self.pN = Ring(nc, st, "pN", [128, 512], F32, 2, psum=True)
            self.pr = Ring(nc, st, "pr", [128, 512], BF16, 6)
            self.tr = Ring(nc, st, "tr", [128, 512], F32, 6)
            pt_, ptb = self.load(st, "ptab", [128, 4 * NPT * 64], F32, self.rpbx[l])
            with ExitStack() as st2:
                mb, mbb = self.load(st2, "mbt", [128, NPT * 64], F32, self.maskb)
                for h in range(4):
                    S.op("dve", lambda e, h=h: e.tensor_tensor(pt_[:, h * NPT * 64:(h + 1) * NPT * 64], pt_[:, h * NPT * 64:(h + 1) * NPT * 64],
                                                               mb[:], ALU.add), reads=[ptb, mbb], writes=[ptb])
                S.barrier()
            mc, mcb = self.load(st, "mct", [128, 256], F32, self.maskc)
            sk, skb = self.load(st, "sk", [128, DEPTH * 4], F32, self.sink[0].partition_broadcast(128))
            S.op("act", lambda e: e.activation(out=sk[:], in_=sk[:], func=AF.Exp), reads=[skb], writes=[skb])

            heads = []
            for h in range(4):
                heads.append(dict(kind="glob", q=[(SQR["a_q"] + h * 128, 128)], k=[(SQR["a_k"] + (h // 2) * 128, 128)],
                                  v=SVC["a_v"] + (h // 2) * 128, g=SQR["gate"] + h * 128, mc=h, scale=128 ** -0.5, sink=None))
            for h in range(4):
                heads.append(dict(kind="nbr", q=[(SQR["b_q"] + h * 128, 128)], k=[(SQR["b_k"] + h * 128, 128)],
                                  v=SVC["b_v"] + h * 128, g=SQR["gate"] + (4 + h) * 128, mc=4 + h, scale=128 ** -0.5, sink=None, h=h))
            for h in range(4):
                heads.append(dict(kind="win", q=[(SQR["c_q"] + h * 128, 128)], k=[(SQR["c_k"] + (h // 2) * 128, 128)],
                                  v=SVC["c_v"] + (h // 2) * 128, g=SQR["gate"] + (8 + h) * 128, mc=8 + h, scale=128 ** -0.5,
                                  sink=l * 4 + h))
            for h in range(4):
                heads.append(dict(kind="glob", q=[(SQR["d_qn"] + h * 128, 128), (SQR["d_qr"] + h * 64, 64)],
                                  k=[(SQR["d_kn"] + h * 128, 128), (SQR["d_kr"] + h * 64, 64)],
                                  v=SVC["d_v"] + h * 128, g=SQR["gate"] + (12 + h) * 128, mc=12 + h, scale=192 ** -0.5, sink=None))
            if self.debug and self.debug[0].startswith("mix") and len(self.debug) > 3:
                heads = [heads[i] for i in self.debug[3]]

            def loadhead(hd):
                r = {}
                q, qb = qr_.next()
                k, kb = kr_.next()
                v, vb = vr_.next()
                g, gb = gr_.next()
                r0, _ = hd["q"][0]
                S.dma([lambda e: e.dma_start(out=q[:], in_=self.sq[r0:r0 + 128, :])], qb, writes=[qb])
                k0, _ = hd["k"][0]
                S.dma([lambda e: e.dma_start(out=k[:], in_=self.sq[k0:k0 + 128, :])], kb, writes=[kb])
                v0 = hd["v"]
                S.dma([lambda e: e.dma_start(out=v[:], in_=self.sv[:, v0:v0 + 128].rearrange("(j p) c -> p j c", p=128))], vb, writes=[vb])
                g0 = hd["g"]
                S.dma([lambda e: e.dma_start(out=g[:], in_=self.sq[g0:g0 + 128, :])], gb, writes=[gb])
                r["qp"] = [(q, 128, qb)]
                r["kp"] = [(k, 128, kb)]
                if len(hd["q"]) > 1:
                    q2, q2b = q2_.next()
                    k2, k2b = k2_.next()
                    r1, _ = hd["q"][1]
                    k1, _ = hd["k"][1]
                    S.dma([lambda e: e.dma_start(out=q2[:], in_=self.sq[r1:r1 + 64, :])], q2b, writes=[q2b])
                    S.dma([lambda e: e.dma_start(out=k2[:], in_=self.sq[k1:k1 + 64, :])], k2b, writes=[k2b])
                    r["qp"].append((q2, 64, q2b))
                    r["kp"].append((k2, 64, k2b))
                r["v"], r["vb"], r["g"], r["gb"] = v, vb, g, gb
                return r

            nxt = loadhead(heads[0])
            for hi, hd in enumerate(heads):
                cur = nxt
                if hi + 1 < len(heads):
                    nxt = loadhead(heads[hi + 1])
                esink = sk[:, hd["sink"]:hd["sink"] + 1] if hd["sink"] is not None else None
                if hd["kind"] == "glob":
                    for qb_ in range(4):
                        self.attn_block(cur, hd, qb_ * 512, 512, [(j, None, None) for j in range(NT)], esink, skb)
                elif hd["kind"] == "win":
                    for i in range(NXT):
                        ch = [(16, None, None), (17, None, None)]
                        if i > 0:
                            ch.append((i - 1, mc[:, 0:128], mcb))
                        ch.append((i, None, None))
                        if i < NXT - 1:
                            ch.append((i + 1, mc[:, 128:256], mcb))
                        self.attn_block(cur, hd, i * 128, 128, ch, esink, skb)
                else:
                    h = hd["h"]
                    for qrow in range(32):
                        rs = min(max(qrow - 4, 0), 24)
                        ch = [(16, None, None), (17, None, None)]
                        if rs % 2 == 0:
                            a0 = rs - qrow + 7
                            p0 = a0 // 2 if a0 % 2 == 0 else 7 + (a0 - 1) // 2
                            js = [rs // 2 + c for c in range(4)]
                        else:
                            p0 = 14
                            js = [(rs - 1) // 2 + c for c in range(5)]
                        for c, j in enumerate(js):
                            b0 = (h * NPT + p0 + c) * 64
                            ch.append((j, pt_[:, b0:b0 + 64], ptb))
                        self.attn_block(cur, hd, qrow * 64, 64, ch, esink, skb)
                if not last:
                    self.attn_block(cur, hd, SEQ, CTXL, [(16, None, None), (17, None, None)], esink, skb)
            S.end_stage()

    def attn_block(self, cur, hd, q0, nq, chunks, esink, skb):
        S = self.S
        scale = hd["scale"]
        po, pob = self.pO.next()
        pn, pnb = self.pN.next()
        v, vb = cur["v"], cur["vb"]
        pend = None
        nch = len(chunks)

        def pv(idx, j, p, pb_):
            S.op("pe", lambda e: e.matmul(po[:, 0:nq], lhsT=v[:, j, :], rhs=p[:, 0:nq], start=(idx == 0), stop=(idx == nch - 1)),
                 reads=[vb, pb_], writes=[pob])
            S.op("pe", lambda e: e.matmul(pn[:, 0:nq], lhsT=self.ones[:, :], rhs=p[:, 0:nq], start=(idx == 0), stop=(idx == nch - 1)),
                 reads=[self.cb, pb_], writes=[pnb])

        for idx, (j, bias, biasb) in enumerate(chunks):
            ps, psb = self.pS.next()
            np_ = len(cur["qp"])
            for pi in range(np_):
                qt, nr, qb_ = cur["qp"][pi]
                kt, _, kb_ = cur["kp"][pi]
                S.op("pe", lambda e, pi=pi, qt=qt, kt=kt, nr=nr: e.matmul(ps[:, 0:nq], lhsT=kt[0:nr, j * 128:(j + 1) * 128],
                                                                         rhs=qt[0:nr, q0:q0 + nq], start=(pi == 0), stop=(pi == np_ - 1)),
                     reads=[qb_, kb_], writes=[psb])
            p, pb_ = self.pr.next()
            if bias is None:
                S.op("act", lambda e: e.activation(out=p[:, 0:nq], in_=ps[:, 0:nq], func=AF.Exp, scale=scale), reads=[psb], writes=[pb_])
            else:
                t, tb = self.tr.next()
                S.op("dve", lambda e: e.scalar_tensor_tensor(out=t[:, 0:nq], in0=ps[:, 0:nq], scalar=scale, in1=bias,
                                                             op0=ALU.mult, op1=ALU.add), reads=[psb, biasb], writes=[tb])
                S.op("act", lambda e: e.activation(out=p[:, 0:nq], in_=t[:, 0:nq], func=AF.Exp), reads=[tb], writes=[pb_])
            if pend is not None:
                pv(*pend)
            pend = (idx, j, p, pb_)
        pv(*pend)
        ri, rib = self.tr.next()
        if esink is not None:
            S.op("dve", lambda e: e.tensor_scalar(ri[:, 0:nq], pn[:, 0:nq], esink, None, op0=ALU.add), reads=[pnb, skb], writes=[rib])
            S.op("dve", lambda e: e.reciprocal(ri[:, 0:nq], ri[:, 0:nq]), reads=[rib], writes=[rib])
        else:
            S.op("dve", lambda e: e.reciprocal(ri[:, 0:nq], pn[:, 0:nq]), reads=[pnb], writes=[rib])
        o, ob = self.tr.next()
        S.op("dve", lambda e: e.tensor_tensor(o[:, 0:nq], po[:, 0:nq], ri[:, 0:nq], ALU.mult), reads=[pob, rib], writes=[ob])
        tiles = [self.bigb[i] for i in range(q0 // 128, (q0 + nq + 127) // 128)]
        g, gb = cur["g"], cur["gb"]
        S.op("pool", lambda e: e.tensor_tensor(self.big[:, hd["mc"], q0:q0 + nq], o[:, 0:nq], g[:, q0:q0 + nq], ALU.mult),
             reads=[ob, gb], writes=tiles)

    def attn_b1(self, cur, hd, q0, nq, groups):
        S = self.S
        scale = hd["scale"]
        pend = []
        for (js, bias, biasb) in groups:
            ps, psb = self.pS.next()
            w = len(js) * nq
            np_ = len(cur["qp"])
            for c, j in enumerate(js):
                for pi in range(np_):
                    qt, nr, qb_ = cur["qp"][pi]
                    kt, _, kb_ = cur["kp"][pi]
                    S.op("pe", lambda e: e.matmul(ps[:, c * nq:(c + 1) * nq], lhsT=kt[0:nr, j * 128:(j + 1) * 128],
                                                  rhs=qt[0:nr, q0:q0 + nq], start=(pi == 0), stop=(pi == np_ - 1)),
                         reads=[qb_, kb_], writes=[psb])
            p, pb_ = self.pr.next()
            if bias is None:
                S.op("act", lambda e: e.activation(out=p[:, 0:w], in_=ps[:, 0:w], func=AF.Exp, scale=scale), reads=[psb], writes=[pb_])
            else:
                t, tb = self.tr.next()
                S.op("dve", lambda e: e.scalar_tensor_tensor(out=t[:, 0:w], in0=ps[:, 0:w], scalar=scale, in1=bias,
                                                             op0=ALU.mult, op1=ALU.add), reads=[psb, biasb], writes=[tb])
                S.op("act", lambda e: e.activation(out=p[:, 0:w], in_=t[:, 0:w], func=AF.Exp), reads=[tb], writes=[pb_])
            for c, j in enumerate(js):
                pend.append((j, p, pb_, c))
        return (q0, nq, pend)

    def attn_b2(self, cur, hd, state, esink, skb):
        S = self.S
        q0, nq, pend = state
        po, pob = self.pO.next()
        pn, pnb = self.pN.next()
        v, vb = cur["v"], cur["vb"]
        total = len(pend)
        for idx, (j, p, pb_, c) in enumerate(pend):
            S.op("pe", lambda e: e.matmul(po[:, 0:nq], lhsT=v[:, j, :], rhs=p[:, c * nq:(c + 1) * nq],
                                          start=(idx == 0), stop=(idx == total - 1)), reads=[vb, pb_], writes=[pob])
            S.op("pe", lambda e: e.matmul(pn[:, 0:nq], lhsT=self.ones[:, :], rhs=p[:, c * nq:(c + 1) * nq],
                                          start=(idx == 0), stop=(idx == total - 1)), reads=[self.cb, pb_], writes=[pnb])
        self.attn_fin(cur, hd, q0, nq, po, pob, pn, pnb, esink, skb)

    def attn_fin(self, cur, hd, q0, nq, po, pob, pn, pnb, esink, skb):
        S = self.S
        ri, rib = self.tr.next()
        if esink is not None:
            S.op("dve", lambda e: e.tensor_scalar(ri[:, 0:nq], pn[:, 0:nq], esink, None, op0=ALU.add), reads=[pnb, skb], writes=[rib])
            S.op("dve", lambda e: e.reciprocal(ri[:, 0:nq], ri[:, 0:nq]), reads=[rib], writes=[rib])
        else:
            S.op("dve", lambda e: e.reciprocal(ri[:, 0:nq], pn[:, 0:nq]), reads=[pnb], writes=[rib])
        o, ob = self.tr.next()
        S.op("dve", lambda e: e.tensor_tensor(o[:, 0:nq], po[:, 0:nq], ri[:, 0:nq], ALU.mult), reads=[pob, rib], writes=[ob])
        tiles = [self.bigb[i] for i in range(q0 // 128, (q0 + nq + 127) // 128)]
        g, gb = cur["g"], cur["gb"]
        S.op("pool", lambda e: e.tensor_tensor(self.big[:, hd["mc"], q0:q0 + nq], o[:, 0:nq], g[:, q0:q0 + nq], ALU.mult),
             reads=[ob, gb], writes=tiles)

    def stage_out(self, l, last):
        S, nc = self.S, self.nc
        w2d = self.w_out[l]
        ntile = NXT if last else NT
        with ExitStack() as st:
            ws = Prog.WS(self, st, 16, nwb=8, conv=("act", "pool"))
            psO = Ring(nc, st, "psO", [128, 512], F32, 3, psum=True)
            xr = Ring(nc, st, "xo", [128, 512], F32, 4)
            yr = Ring(nc, st, "yo", [128, 512], F32, 4)
            gt = self.sb(st, "gtx", [128, D], F32)
            gtb = Buf("gtx")
            gc = self.sb(st, "gtc", [128, D], F32)
            gcb = Buf("gtc")
            S.dma([lambda e: e.dma_start(out=gt[:], in_=self.modv[l, 0, 2 * D:3 * D].partition_broadcast(128))], gtb, writes=[gtb])
            S.dma([lambda e: e.dma_start(out=gc[:], in_=self.modv[l, 1, 2 * D:3 * D].partition_broadcast(128))], gcb, writes=[gcb])
            xsrc = self.x if l == 0 else self.x1
            csrc = self.ctx if l == 0 else self.hc1
            xdst = self.out if last else self.x1
            slabs = [ws.fetch(w2d, j * 128, 128) for j in range(4)]
            for cb_ in range(4):
                cur = slabs
                if cb_ + 1 < 4:
                    slabs = [ws.fetch(w2d, (cb_ + 1) * 512 + j * 128, 128) for j in range(4)]
                c0 = cb_ * 512
                def tile_io(t):
                    if t < NXT:
                        return (xsrc[t * 128:(t + 1) * 128, c0:c0 + 512], xdst[t * 128:(t + 1) * 128, c0:c0 + 512], gt, gtb)
                    return (csrc[(t - NXT) * 128:(t - NXT + 1) * 128, c0:c0 + 512],
                            self.hc1[(t - NXT) * 128:(t - NXT + 1) * 128, c0:c0 + 512], gc, gcb)

                def xload(t):
                    xt, xb = xr.next()
                    src = tile_io(t)[0]
                    S.dma([lambda e: e.dma_start(out=xt[:], in_=src)], xb, writes=[xb])
                    return xt, xb

                xq = [xload(0), xload(1)]
                for t in range(ntile):
                    _, dst, gg, ggb = tile_io(t)
                    xt, xb = xq.pop(0)
                    if t + 2 < ntile:
                        xq.append(xload(t + 2))
                    pt, pb = psO.next()
                    for j in range(4):
                        wb, bw = cur[j]
                        for k in range(16):
                            S.op("pe", lambda e: e.matmul(pt[:, j * 128:(j + 1) * 128], lhsT=self.big[:, k, t * 128:(t + 1) * 128],
                                                          rhs=wb[:, k, :], start=(k == 0), stop=(k == 15)),
                                 reads=[bw, self.bigb[t]], writes=[pb])
                    y, yb = yr.next()
                    S.op("dve", lambda e: e.tensor_tensor(y[:], pt[:], gg[:, c0:c0 + 512], ALU.mult), reads=[pb, ggb], writes=[yb])
                    S.op("dve", lambda e: e.tensor_tensor(y[:], y[:], xt[:], ALU.add), reads=[yb, xb], writes=[yb])
                    S.dma([lambda e: e.dma_start(out=dst, in_=y[:])], yb, reads=[yb])
            S.end_stage()


def _rope_tables(rot_dim):
    t = np.arange(SEQ)
    row = (t // 64).astype(np.float32)
    col = (t % 64).astype(np.float32)
    nf = rot_dim // 4
    inv = (10000.0 ** (-np.arange(nf, dtype=np.float32) / nf)).astype(np.float32)
    ang = np.concatenate([row[:, None] * inv, col[:, None] * inv], axis=-1).astype(np.float32)
    cos = np.cos(ang).astype(np.float32).T
    sin = np.sin(ang).astype(np.float32).T
    return np.ascontiguousarray(np.concatenate([np.concatenate([cos, cos], 0), np.concatenate([sin, sin], 0)], axis=1))


def _rmat():
    r = np.zeros((128, 192), np.float32)
    for m in range(64):
        r[m + 64, m] = -1.0
        r[m, m + 64] = 1.0
    for m in range(32):
        r[m + 32, 128 + m] = -1.0
        r[m, 128 + m + 32] = 1.0
    return r


def _nbr_entries():
    ent = [(a, a + 1) for a in range(0, 14, 2)] + [(a, a + 1) for a in range(1, 14, 2)]
    ent += [(None, 3), (4, 5), (6, 7), (8, 9), (10, None)]
    return ent


def _nbr_tables(rpb):
    ent = _nbr_entries()
    kcol = np.arange(64)[:, None]
    qcol = np.arange(64)[None, :]
    cs = np.clip(qcol - 8, 0, 48)
    colvalid = (kcol >= cs) & (kcol < cs + 16)
    dc = np.clip(kcol - qcol + 15, 0, 30)
    mask = np.full((NPT, 128, 64), NEG, np.float32)
    idx_a = np.zeros((NPT, 128), np.int64)
    blk_ok = np.zeros((NPT, 128), bool)
    for e, (at, ab) in enumerate(ent):
        for half, a in ((0, at), (1, ab)):
            sl = slice(half * 64, half * 64 + 64)
            if a is None:
                continue
            idx_a[e, sl] = a
            blk_ok[e, sl] = True
            mask[e, sl, :] = np.where(colvalid, 0.0, NEG)
    dcf = np.concatenate([dc, dc], 0)
    g = rpb[:, :, idx_a[:, :, None], dcf[None, :, :]]
    g = np.where(blk_ok[None, None, :, :, None], g, np.float32(0.0)).astype(np.float32)
    rpbx = np.ascontiguousarray(g.transpose(0, 3, 1, 2, 4).reshape(DEPTH, 128, 4 * NPT * 64))
    maskb = np.ascontiguousarray(mask.transpose(1, 0, 2).reshape(128, NPT * 64))
    return rpbx, maskb


def _maskc():
    p = np.arange(128)[:, None]
    f = np.arange(128)[None, :]
    prev = np.where(f <= p, 0.0, NEG).astype(np.float32)
    nxt = np.where(p <= f, 0.0, NEG).astype(np.float32)
    return np.ascontiguousarray(np.concatenate([prev, nxt], axis=1))


def _col(v, n=128):
    o = np.zeros((128,), np.float32)
    o[:len(v)] = v
    return o


def make_in_maps(inp):
    f = lambda a: np.ascontiguousarray(np.asarray(a, dtype=np.float32))
    gains = np.zeros((128, DEPTH * NGAIN), np.float32)
    for l in range(DEPTH):
        cols = [inp["a_q_g"][l], inp["a_k_g"][l], inp["b_q_g"][l], inp["b_k_g"][l], inp["c_q_g"][l], inp["c_k_g"][l],
                inp["d_q_g"][l][:128], inp["d_q_g"][l][128:], inp["d_k_g"][l][:128], inp["d_k_g"][l][128:]]
        cols += [inp["d_kv_g"][l][j * 128:(j + 1) * 128] for j in range(4)]
        for j, c in enumerate(cols):
            gains[:, l * NGAIN + j] = _col(np.asarray(c, np.float32))
    rpbx, maskb = _nbr_tables(f(inp["b_rpb"]))
    shared = dict(norm_g=f(inp["norm_g"]), b_ada=f(inp["b_ada"]), w_ada=f(inp["w_ada"]), w_in=f(inp["w_in"]),
                  w_out=f(inp["w_out"]), w_uk=f(inp["d_w_uk"]), w_uv=f(inp["d_w_uv"]), gains=gains,
                  sink=f(inp["c_sink"]).reshape(1, DEPTH * 4), rpbx=rpbx, maskb=maskb, maskc=_maskc(),
                  ident=np.eye(128, dtype=np.float32), rmat=_rmat(), rope_h=_rope_tables(128), rope_r=_rope_tables(64))
    cctx = f(inp["c_ctx"]).reshape(16, 128).T
    maps = []
    for b in range(NCORES):
        cf = np.ascontiguousarray(np.concatenate([f(inp["c"][b]).reshape(16, 128).T, cctx], axis=1))
        m = dict(shared)
        m.update(x=f(inp["x"][b]), ctx=f(inp["ctx"][b]), cfm=cf)
        maps.append(m)
    return maps


_PROG = {}


def kernel(**inputs):
    if "p" not in _PROG:
        _PROG["p"] = Prog()
    prog = _PROG["p"]
    maps = make_in_maps(inputs)
    res = run_bass_kernel_spmd(prog.nc, maps, core_ids=list(range(NCORES)))
    return np.stack([np.asarray(r["out"], dtype=np.float32) for r in res.results], axis=0)
```

```python
import numpy as np
from contextlib import ExitStack
import concourse.bass as bass
import concourse.mybir as mybir
from concourse.bass_utils import run_bass_kernel_spmd

F32 = mybir.dt.float32
BF16 = mybir.dt.bfloat16
AF = mybir.ActivationFunctionType
ALU = mybir.AluOpType

D = 2048
SEQ = 2048
CTXL = 256
T = SEQ + CTXL
NT = T // 128
NXT = SEQ // 128
DEPTH = 2
INC = 6976
EPS = 1e-6
NEG = -30000.0
NCORES = 8

COL = dict(a_q=0, a_k=512, a_v=768, a_g=1024, b_q=1536, b_k=2048, b_v=2560, b_g=3072,
           c_q=3584, c_k=4096, c_v=4352, c_g=4608, d_q=5120, d_ckv=5888, d_kr=6400, d_g=6464)
SQR = {}
_r = 0
for _n, _rows in (("a_q", 512), ("a_k", 256), ("b_q", 512), ("b_k", 512), ("c_q", 512), ("c_k", 256),
                  ("d_qn", 512), ("d_qr", 256), ("d_kn", 512), ("d_kr", 256), ("gate", 2048)):
    SQR[_n] = _r
    _r += _rows
SQ_ROWS = _r
SVC = dict(a_v=0, b_v=256, c_v=768, d_v=1024)
SV_COLS = 1536
NGAIN = 14
NPT = 19


class Buf:
    __slots__ = ("name", "w", "r", "sem", "cnt")

    def __init__(self, name):
        self.name = name
        self.w = None
        self.r = []
        self.sem = None
        self.cnt = 0


class Sched:
    ENG = ("pe", "act", "dve", "pool", "sp")

    def __init__(self, nc, stack):
        self.nc = nc
        self.stack = stack
        self.eng = {"pe": nc.tensor, "act": nc.scalar, "dve": nc.vector, "pool": nc.gpsimd, "sp": nc.sync}
        self.sems = {}
        self.count = {}
        for e in ("pe", "act", "dve", "pool"):
            self.sems[e] = stack.enter_context(nc.semaphore("prog_" + e))
            self.count[e] = 0
        self.seen = {e: {} for e in self.ENG}
        self.semcnt = {}
        self.freek = []
        self.stagek = []
        self.persist = True
        self.ninst = 0
        self.nwait = 0

    def _bufsem(self, b):
        if b.sem is None:
            if self.freek:
                key = self.freek.pop()
            else:
                key = "dsem%d" % len(self.semcnt)
                self.sems[key] = self.stack.enter_context(self.nc.semaphore(key))
                self.semcnt[key] = 0
            b.sem = key
            if not self.persist:
                self.stagek.append(key)
        return b.sem

    def end_stage(self):
        self.barrier()
        self.freek.extend(self.stagek)
        self.stagek = []

    def _wait(self, engine, deps):
        need = {}
        for t in deps:
            if t is None:
                continue
            k, v = t
            if need.get(k, 0) < v:
                need[k] = v
        seen = self.seen[engine]
        for k, v in need.items():
            if seen.get(k, 0) >= v:
                continue
            self.eng[engine].wait_ge(self.sems[k], v)
            self.nwait += 1
            seen[k] = v

    def op(self, engine, fn, reads=(), writes=(), inc=True):
        deps = []
        own = set()
        for b in reads:
            deps.append(b.w)
            if b.w is not None and b.w[0] == engine:
                own.add(b.w)
        for b in writes:
            deps.append(b.w)
            deps.extend(b.r)
        deps = [t for t in deps if t is not None and (t[0] != engine or t in own)]
        self._wait(engine, deps)
        inst = fn(self.eng[engine])
        if inc:
            self.count[engine] += 1
            inst.then_inc(self.sems[engine], 1)
            tok = (engine, self.count[engine])
        else:
            assert engine == "pe"
            tok = (engine, self.count[engine] + 1)
        for b in writes:
            b.w = tok
            b.r = []
        for b in reads:
            if b not in writes:
                b.r.append(tok)
        self.ninst += 1
        return inst

    def dma(self, fns, owner, reads=(), writes=(), queue="sp"):
        deps = []
        for b in reads:
            deps.append(b.w)
        for b in writes:
            deps.append(b.w)
            deps.extend(b.r)
        self._wait(queue, deps)
        key = self._bufsem(owner)
        for fn in fns:
            inst = fn(self.eng[queue])
            self.semcnt[key] += 16
            inst.then_inc(self.sems[key], 16)
            self.ninst += 1
        tok = (key, self.semcnt[key])
        for b in writes:
            b.w = tok
            b.r = []
        for b in reads:
            if b not in writes:
                b.r.append(tok)
        return tok

    def barrier(self):
        toks = [(e, self.count[e]) for e in ("pe", "act", "dve", "pool")]
        toks += [(k, v) for k, v in self.semcnt.items()]
        for e in self.ENG:
            self._wait(e, toks)


_UID = [0]


def _uniq(name):
    _UID[0] += 1
    return "%s_u%d" % (name, _UID[0])


class Ring:
    def __init__(self, nc, st, name, shape, dtype, n, psum=False):
        self.t = []
        for i in range(n):
            nm = _uniq("%s%d" % (name, i))
            if psum:
                t = st.enter_context(nc.psum_tensor(nm, shape, dtype))
            else:
                t = st.enter_context(nc.sbuf_tensor(nm, shape, dtype))
            self.t.append((t, Buf(nm)))
        self.i = 0

    def next(self):
        r = self.t[self.i % len(self.t)]
        self.i += 1
        return r


class Prog:
    def __init__(self, depth=DEPTH, debug=None):
        self.depth = depth
        self.debug = debug
        nc = bass.Bass("TRN2", target_bir_lowering=False)
        self.nc = nc

        def din(name, shape, dt=F32):
            return nc.dram_tensor(name, list(shape), dt, kind="ExternalInput").ap()

        self.x = din("x", [SEQ, D])
        self.ctx = din("ctx", [CTXL, D])
        self.cfm = din("cfm", [128, 32])
        self.norm_g = din("norm_g", [DEPTH, D])
        self.b_ada = din("b_ada", [DEPTH, 3 * D])
        self.w_ada = din("w_ada", [DEPTH, D, 3 * D])
        self.w_in = din("w_in", [DEPTH, D, INC])
        self.w_out = din("w_out", [DEPTH, D, D])
        self.w_uk = din("w_uk", [DEPTH, 512, 512])
        self.w_uv = din("w_uv", [DEPTH, 512, 512])
        self.gains = din("gains", [128, DEPTH * NGAIN])
        self.sink = din("sink", [1, DEPTH * 4])
        self.rpbx = din("rpbx", [DEPTH, 128, 4 * NPT * 64])
        self.maskb = din("maskb", [128, NPT * 64])
        self.maskc = din("maskc", [128, 256])
        self.ident = din("ident", [128, 128])
        self.rmat = din("rmat", [128, 192])
        self.rope_h = din("rope_h", [128, 2 * SEQ])
        self.rope_r = din("rope_r", [64, 2 * SEQ])
        self.out = nc.dram_tensor("out", [SEQ, D], F32, kind="ExternalOutput").ap()
        self.modv = nc.dram_tensor("modv", [DEPTH, 2, 3 * D], F32).ap()
        self.sq = nc.dram_tensor("sq", [SQ_ROWS, T], BF16).ap()
        self.sv = nc.dram_tensor("sv", [T, SV_COLS], BF16).ap()
        self.x1 = nc.dram_tensor("x1", [SEQ, D], F32).ap()
        self.hc1 = nc.dram_tensor("hc1", [CTXL, D], F32).ap()
        if debug:
            self.dbg = nc.dram_tensor("dbg", list(debug[1]), debug[2], kind="ExternalOutput").ap()

        with ExitStack() as st:
            self.st = st
            self.S = Sched(nc, st)
            self.build()

    def sb(self, st, name, shape, dt):
        return st.enter_context(self.nc.sbuf_tensor(_uniq(name), list(shape), dt))

    def load(self, st, name, shape, dt, src):
        t = self.sb(st, name, shape, dt)
        b = Buf(name)
        self.S.dma([lambda e: e.dma_start(out=t[:], in_=src)], b, writes=[b])
        return t, b

    def build(self):
        S, nc, st = self.S, self.nc, self.st
        self.big = self.sb(st, "big", [128, 16, T], BF16)
        self.bigb = [Buf("big%d" % i) for i in range(NT)]
        idf, idfb = self.load(st, "idf", [128, 128], F32, self.ident)
        rmf, rmfb = self.load(st, "rmf", [128, 192], F32, self.rmat)
        self.idb = self.sb(st, "idb", [128, 128], BF16)
        self.rmb = self.sb(st, "rmb", [128, 192], BF16)
        self.ones = self.sb(st, "ones", [128, 128], BF16)
        self.cb = Buf("consts")
        S.op("dve", lambda e: e.tensor_copy(self.idb[:], idf[:]), reads=[idfb], writes=[self.cb])
        S.op("dve", lambda e: e.tensor_copy(self.rmb[:], rmf[:]), reads=[rmfb], writes=[self.cb])
        S.op("dve", lambda e: e.memset(self.ones[:], 1.0), writes=[self.cb])
        self.gn, self.gnb = self.load(st, "gn", [128, DEPTH * NGAIN], F32, self.gains)
        S.barrier()
        S.persist = False
        for l in range(self.depth):
            with nc.named_scope("ada%d" % l):
                self.stage_ada(l)
        for l in range(self.depth):
            last = (l == DEPTH - 1)
            with nc.named_scope("norm%d" % l):
                self.stage_norm(l)
            if self.debug and self.debug[0] == "hT%d" % l:
                self.dump_big()
                return
            with nc.named_scope("proj%d" % l):
                self.stage_proj(l, last)
            if self.debug and self.debug[0] == "sq%d" % l:
                return self.dump_dram(self.sq)
            if self.debug and self.debug[0] == "sv%d" % l:
                return self.dump_dram(self.sv)
            with nc.named_scope("attn%d" % l):
                self.stage_attn(l, last)
            if self.debug and self.debug[0] == "mix%d" % l:
                self.dump_big()
                return
            with nc.named_scope("out%d" % l):
                self.stage_out(l, last)
            if self.debug and self.debug[0] == "xo%d" % l:
                return self.dump_dram(self.x1 if not last else self.out)

    def dump_big(self):
        S = self.S
        b = Buf("dump")
        S.dma([lambda e: e.dma_start(out=self.dbg, in_=self.big[:])], b)
        S.barrier()

    def dump_dram(self, src):
        S = self.S
        b = Buf("dump")
        S.dma([lambda e: e.dma_start(out=self.dbg, in_=src)], b)
        S.barrier()

    class WS:
        def __init__(self, P, st, kch, nwb=3, conv=("pool",)):
            self.P = P
            self.kch = kch
            self.stage = Ring(P.nc, st, "wst", [128, kch, 128], F32, 2)
            self.wb = Ring(P.nc, st, "wbb", [128, kch, 128], BF16, nwb)
            self.conv = conv
            self.n = 0

        def fetch(self, w2d, c0, ncols, kch=None):
            P, S = self.P, self.P.S
            kch = kch or self.kch
            stg, bs = self.stage.next()
            wb, bw = self.wb.next()
            src = w2d[:, c0:c0 + ncols].rearrange("(kc p) n -> p kc n", p=128)
            S.dma([lambda e: e.dma_start(out=stg[:, :kch, :ncols], in_=src)], bs, writes=[bs])
            eng = self.conv[self.n % len(self.conv)]
            self.n += 1
            if eng == "act":
                S.op("act", lambda e: e.activation(out=wb[:, :kch, :ncols], in_=stg[:, :kch, :ncols], func=AF.Copy),
                     reads=[bs], writes=[bw])
            else:
                S.op(eng, lambda e: e.tensor_copy(wb[:, :kch, :ncols], stg[:, :kch, :ncols]), reads=[bs], writes=[bw])
            return wb, bw

    def stage_ada(self, l):
        S, nc = self.S, self.nc
        with ExitStack() as st:
            cf, cfb = self.load(st, "cf", [128, 32], F32, self.cfm)
            s2 = self.sb(st, "s2", [128, 16, 2], BF16)
            s2b = Buf("s2")
            sil = self.sb(st, "sil", [128, 32], F32)
            silb = Buf("sil")
            S.op("act", lambda e: e.activation(out=sil[:], in_=cf[:], func=AF.Silu), reads=[cfb], writes=[silb])
            S.op("dve", lambda e: e.tensor_copy(s2[:, :, 0], sil[:, 0:16]), reads=[silb], writes=[s2b])
            S.op("dve", lambda e: e.tensor_copy(s2[:, :, 1], sil[:, 16:32]), reads=[silb], writes=[s2b])
            modrow = self.sb(st, "modrow", [2, 3 * D], F32)
            mrb = Buf("modrow")
            badd, baddb = self.load(st, "badd", [2, 3 * D], F32, self.b_ada[l].partition_broadcast(2))
            g2, g2b = self.load(st, "g2", [2, D], F32, self.norm_g[l].partition_broadcast(2))
            ps = Ring(nc, st, "psA", [128, 512], F32, 2, psum=True)
            stg = Ring(nc, st, "adst", [128, 16, 256], F32, 2)
            wbr = Ring(nc, st, "adwb", [128, 16, 256], BF16, 2)
            w2d = self.w_ada[l]
            nsl = 3 * D // 256

            def fetch(i):
                t, b = stg.next()
                wb, bw = wbr.next()
                src = w2d[:, i * 256:(i + 1) * 256].rearrange("(kc p) n -> p kc n", p=128)
                S.dma([lambda e: e.dma_start(out=t[:, 0:8, :], in_=src[:, 0:8, :]),
                       lambda e: e.dma_start(out=t[:, 8:16, :], in_=src[:, 8:16, :])], b, writes=[b])
                S.op("pool", lambda e: e.tensor_copy(wb[:, 0:6, :], t[:, 0:6, :]), reads=[b], writes=[bw])
                S.op("act", lambda e: e.activation(out=wb[:, 6:11, :], in_=t[:, 6:11, :], func=AF.Copy), reads=[b], writes=[bw])
                S.op("dve", lambda e: e.tensor_copy(wb[:, 11:16, :], t[:, 11:16, :]), reads=[b], writes=[bw])
                return wb, bw

            q = [fetch(0)]
            for i in range(nsl):
                wb, bw = q.pop(0)
                if i + 1 < nsl:
                    q.append(fetch(i + 1))
                pt, pb = ps.next()
                for k in range(16):
                    S.op("pe", lambda e, k=k: e.matmul(pt[0:2, 0:256], lhsT=s2[:, k, :], rhs=wb[:, k, :],
                                                      start=(k == 0), stop=(k == 15)),
                         reads=[s2b, bw], writes=[pb], inc=(k == 15))
                S.op("dve", lambda e, i=i: e.tensor_copy(modrow[:, i * 256:(i + 1) * 256], pt[0:2, 0:256]),
                     reads=[pb], writes=[mrb])
            S.op("dve", lambda e: e.tensor_tensor(modrow[:], modrow[:], badd[:], ALU.add), reads=[mrb, baddb], writes=[mrb])
            S.op("dve", lambda e: e.scalar_tensor_tensor(out=modrow[:, D:2 * D], in0=modrow[:, D:2 * D], scalar=1.0,
                                                         in1=g2[:], op0=ALU.add, op1=ALU.mult),
                 reads=[mrb, g2b], writes=[mrb])
            S.dma([lambda e: e.dma_start(out=self.modv[l], in_=modrow[:])], mrb, reads=[mrb])
            S.end_stage()

    def stage_norm(self, l):
        S, nc = self.S, self.nc
        with ExitStack() as st:
            xr = Ring(nc, st, "xt", [128, D], F32, 3)
            junk = self.sb(st, "junk", [128, D], BF16)
            junkb = Buf("junk")
            tf = Ring(nc, st, "tf", [128, D], F32, 2)
            hb = Ring(nc, st, "hb", [128, D], BF16, 2)
            ssr = Ring(nc, st, "ssn", [128, 2], F32, 4)
            abc = self.sb(st, "abc", [128, D], F32)
            bbc = self.sb(st, "bbc", [128, D], F32)
            abcb, bbcb = Buf("abc"), Buf("bbc")
            pst = Ring(nc, st, "psT", [128, 8, 128], BF16, 4, psum=True)
            xsrc = self.x if l == 0 else self.x1
            csrc = self.ctx if l == 0 else self.hc1
            for t in range(NT):
                if t == 0 or t == NXT:
                    w = 0 if t == 0 else 1
                    S.dma([lambda e, w=w: e.dma_start(out=abc[:], in_=self.modv[l, w, D:2 * D].partition_broadcast(128))],
                          abcb, writes=[abcb])
                    S.dma([lambda e, w=w: e.dma_start(out=bbc[:], in_=self.modv[l, w, 0:D].partition_broadcast(128))],
                          bbcb, writes=[bbcb])
                src = xsrc[t * 128:(t + 1) * 128, :] if t < NXT else csrc[(t - NXT) * 128:(t - NXT + 1) * 128, :]
                xt, xb = xr.next()
                S.dma([lambda e, xt=xt, src=src: e.dma_start(out=xt[:], in_=src)], xb, writes=[xb])
                ss, ssb = ssr.next()
                S.op("act", lambda e, xt=xt, ss=ss: e.activation(out=junk[:], in_=xt[:], func=AF.Square, accum_out=ss[:, 0:1]),
                     reads=[xb], writes=[junkb, ssb])
                S.op("act", lambda e, ss=ss: e.activation(out=ss[:, 1:2], in_=ss[:, 0:1], func=AF.Sqrt, bias=EPS, scale=1.0 / D),
                     reads=[ssb], writes=[ssb])
                S.op("dve", lambda e, ss=ss: e.reciprocal(ss[:, 1:2], ss[:, 1:2]), reads=[ssb], writes=[ssb])
                t1, t1b = tf.next()
                S.op("dve", lambda e, xt=xt, ss=ss, t1=t1: e.scalar_tensor_tensor(out=t1[:], in0=xt[:], scalar=ss[:, 1:2], in1=abc[:],
                                                                               op0=ALU.mult, op1=ALU.mult),
                     reads=[xb, ssb, abcb], writes=[t1b])
                h, hbb = hb.next()
                S.op("pool" if t % 2 else "dve", lambda e, t1=t1, h=h: e.tensor_tensor(h[:], t1[:], bbc[:], ALU.add),
                     reads=[t1b, bbcb], writes=[hbb])
                for g in range(2):
                    pt, pb = pst.next()
                    for i in range(8):
                        c = g * 8 + i
                        S.op("pe", lambda e, pt=pt, i=i, c=c, h=h: e.transpose(pt[:, i, :], h[:, c * 128:(c + 1) * 128], self.idb[:]),
                             reads=[hbb, self.cb], writes=[pb], inc=(i == 7))
                    eng = "act" if g == 0 else "dve"
                    if eng == "act":
                        S.op("act", lambda e, pt=pt, g=g, t=t: e.activation(out=self.big[:, g * 8:(g + 1) * 8, t * 128:(t + 1) * 128],
                                                                          in_=pt[:], func=AF.Copy),
                             reads=[pb], writes=[self.bigb[t]])
                    else:
                        S.op("dve", lambda e, pt=pt, g=g, t=t: e.tensor_copy(self.big[:, g * 8:(g + 1) * 8, t * 128:(t + 1) * 128], pt[:]),
                             reads=[pb], writes=[self.bigb[t]])
            S.end_stage()

    def stage_proj(self, l, last):
        S, nc = self.S, self.nc
        G0 = l * NGAIN
        w2d = self.w_in[l]
        chunks = [(0, 512), (512, 512), (1024, 512), (1536, 512), (2048, 256)]
        with ExitStack() as st:
            ws = Prog.WS(self, st, 16, nwb=5, conv=("pool",))
            self.ps_p = Ring(nc, st, "psP", [128, 512], F32, 4, psum=True)
            self.ps_s = Ring(nc, st, "psS", [128, 512], F32, 2, psum=True)
            self.ps_r = Ring(nc, st, "psR", [128, 512], F32, 2, psum=True)
            self.sqr = Ring(nc, st, "sqr", [128, 512], BF16, 4)
            self.f32r = Ring(nc, st, "f32r", [128, 512], F32, 8)
            self.obr = Ring(nc, st, "obr", [128, 512], BF16, 6)
            self.qfr = Ring(nc, st, "qfr", [128, 512], F32, 5)
            vout = Ring(nc, st, "vout", [128, 128], BF16, 4)

            def feat_group(wb, bw, m0, m, tok0, n):
                pt, pb = self.ps_p.next()
                tiles = [self.bigb[i] for i in range(tok0 // 128, (tok0 + n) // 128)]
                for k in range(16):
                    S.op("pe", lambda e, k=k: e.matmul(pt[0:m, 0:n], lhsT=wb[:, k, m0:m0 + m], rhs=self.big[:, k, tok0:tok0 + n],
                                                      start=(k == 0), stop=(k == 15)),
                         reads=[bw] + tiles, writes=[pb], inc=(k == 15))
                return pt, pb

            with ExitStack() as st2:
                tab, tabb = self.load(st2, "ropeh", [128, 2 * SEQ], F32, self.rope_h)
                qk = []
                for (nm, nh, gcol, rope) in (("a_q", 4, 0, True), ("a_k", 2, 1, True), ("b_q", 4, 2, False), ("b_k", 4, 3, False),
                                             ("c_q", 4, 4, True), ("c_k", 2, 5, True)):
                    for h in range(nh):
                        qk.append((COL[nm] + h * 128, SQR[nm] + h * 128, gcol, rope))
                nxt = ws.fetch(w2d, qk[0][0], 128)
                for i, (c0, r0, gcol, rope) in enumerate(qk):
                    wb, bw = nxt
                    if i + 1 < len(qk):
                        nxt = ws.fetch(w2d, qk[i + 1][0], 128)
                    for (tok0, n) in chunks:
                        pt, pb = feat_group(wb, bw, 0, 128, tok0, n)
                        pt, pb = self.evac(pt, pb, 128, n)
                        rstd, rb = self.rstd_of([(pt[0:128, 0:n], 128, pb)], 128.0, n)
                        self.finish_qk(pt[0:128, 0:n], 128, pb, self.gn[:, G0 + gcol:G0 + gcol + 1], rstd, rb,
                                       rope and tok0 < SEQ, tab, tabb, tok0, n, self.sq[r0:r0 + 128, tok0:tok0 + n], 128)
                S.barrier()
            self.proj_d(l, st, ws, w2d, chunks, feat_group)
            S.barrier()
            vg = []
            for nm, ncol in (("a_v", 256), ("b_v", 512), ("c_v", 256)):
                for j in range(ncol // 128):
                    vg.append((COL[nm] + j * 128, SVC[nm] + j * 128))
            nxt = ws.fetch(w2d, vg[0][0], 128)
            for i, (c0, v0) in enumerate(vg):
                wb, bw = nxt
                if i + 1 < len(vg):
                    nxt = ws.fetch(w2d, vg[i + 1][0], 128)
                for t in range(NT):
                    pt, pb = self.ps_p.next()
                    for k in range(16):
                        S.op("pe", lambda e, k=k, t=t: e.matmul(pt[:, 0:128], lhsT=self.big[:, k, t * 128:(t + 1) * 128], rhs=wb[:, k, :],
                                                              start=(k == 0), stop=(k == 15)),
                             reads=[bw, self.bigb[t]], writes=[pb], inc=(k == 15))
                    vo, vob = vout.next()
                    eng = "act" if t % 2 == 0 else "dve"
                    if eng == "act":
                        S.op("act", lambda e: e.activation(out=vo[:], in_=pt[:, 0:128], func=AF.Copy), reads=[pb], writes=[vob])
                    else:
                        S.op("dve", lambda e: e.tensor_copy(vo[:], pt[:, 0:128]), reads=[pb], writes=[vob])
                    S.dma([lambda e, t=t: e.dma_start(out=self.sv[t * 128:(t + 1) * 128, v0:v0 + 128], in_=vo[:])], vob, reads=[vob])
            gg = []
            for bi, nm in enumerate(("a_g", "b_g", "c_g", "d_g")):
                for h in range(4):
                    gg.append((COL[nm] + h * 128, SQR["gate"] + (bi * 4 + h) * 128))
            gchunks = chunks[:4] if last else chunks
            nxt = ws.fetch(w2d, gg[0][0], 128)
            for i, (c0, r0) in enumerate(gg):
                wb, bw = nxt
                if i + 1 < len(gg):
                    nxt = ws.fetch(w2d, gg[i + 1][0], 128)
                for (tok0, n) in gchunks:
                    pt, pb = feat_group(wb, bw, 0, 128, tok0, n)
                    o, ob = self.obr.next()
                    S.op("act", lambda e: e.activation(out=o[:, 0:n], in_=pt[:, 0:n], func=AF.Silu), reads=[pb], writes=[ob])
                    S.dma([lambda e: e.dma_start(out=self.sq[r0:r0 + 128, tok0:tok0 + n], in_=o[:, 0:n])], ob, reads=[ob])
            S.end_stage()

    def evac(self, pt, pb, nr, n):
        qf, qfb = self.qfr.next()
        self.S.op("act", lambda e: e.activation(out=qf[0:nr, 0:n], in_=pt[0:nr, 0:n], func=AF.Copy), reads=[pb], writes=[qfb])
        return qf, qfb

    def rstd_of(self, parts, dim, n):
        S = self.S
        st_, sb_ = self.ps_s.next()
        for i, (src, nr, b) in enumerate(parts):
            sq, sqb = self.sqr.next()
            S.op("act", lambda e: e.activation(out=sq[0:nr, 0:n], in_=src, func=AF.Square), reads=[b], writes=[sqb])
            S.op("pe", lambda e: e.matmul(st_[:, 0:n], lhsT=self.ones[0:nr, :], rhs=sq[0:nr, 0:n],
                                          start=(i == 0), stop=(i == len(parts) - 1)),
                 reads=[sqb, self.cb], writes=[sb_])
        r, rb = self.f32r.next()
        S.op("act", lambda e: e.activation(out=r[:, 0:n], in_=st_[:, 0:n], func=AF.Sqrt, bias=EPS, scale=1.0 / dim),
             reads=[sb_], writes=[rb])
        S.op("dve", lambda e: e.reciprocal(r[:, 0:n], r[:, 0:n]), reads=[rb], writes=[rb])
        return r, rb

    def finish_qk(self, src, nr, sbuf_, gain, rstd, rb, rope, tab, tabb, tok0, n, dst, rdim):
        S = self.S
        qn, qb = self.obr.next()
        S.op("dve", lambda e: e.scalar_tensor_tensor(out=qn[0:nr, 0:n], in0=src, scalar=gain[0:nr, :], in1=rstd[0:nr, 0:n],
                                                     op0=ALU.mult, op1=ALU.mult),
             reads=[sbuf_, rb, self.gnb], writes=[qb])
        if rope:
            rt, rtb = self.ps_r.next()
            rm = self.rmb[0:128, 0:128] if rdim == 128 else self.rmb[0:64, 128:192]
            S.op("pe", lambda e: e.matmul(rt[0:nr, 0:n], lhsT=rm, rhs=qn[0:nr, 0:n], start=True, stop=True),
                 reads=[qb, self.cb], writes=[rtb])
            t1, t1b = self.f32r.next()
            t2, t2b = self.f32r.next()
            S.op("dve", lambda e: e.tensor_tensor(t1[0:nr, 0:n], qn[0:nr, 0:n], tab[0:nr, tok0:tok0 + n], ALU.mult),
                 reads=[qb, tabb], writes=[t1b])
            S.op("dve", lambda e: e.tensor_tensor(t2[0:nr, 0:n], rt[0:nr, 0:n], tab[0:nr, SEQ + tok0:SEQ + tok0 + n], ALU.mult),
                 reads=[rtb, tabb], writes=[t2b])
            o, ob = self.obr.next()
            S.op("dve", lambda e: e.tensor_tensor(o[0:nr, 0:n], t1[0:nr, 0:n], t2[0:nr, 0:n], ALU.add),
                 reads=[t1b, t2b], writes=[ob])
        else:
            o, ob = qn, qb
        S.dma([lambda e: e.dma_start(out=dst, in_=o[0:nr, 0:n])], ob, reads=[ob])

    def proj_d(self, l, st_outer, ws, w2d, chunks, feat_group):
        S, nc = self.S, self.nc
        G0 = l * NGAIN
        with ExitStack() as st:
            tab, tabb = self.load(st, "roper", [64, 2 * SEQ], F32, self.rope_r)
            nxt_n = ws.fetch(w2d, COL["d_q"], 128)
            nxt_r = ws.fetch(w2d, COL["d_q"] + 128, 64)
            for h in range(4):
                (wn, bwn), (wr, bwr) = nxt_n, nxt_r
                for (tok0, n) in chunks:
                    pn, pnb = feat_group(wn, bwn, 0, 128, tok0, n)
                    pn, pnb = self.evac(pn, pnb, 128, n)
                    pr, prb = feat_group(wr, bwr, 0, 64, tok0, n)
                    pr, prb = self.evac(pr, prb, 64, n)
                    rstd, rb = self.rstd_of([(pn[0:128, 0:n], 128, pnb), (pr[0:64, 0:n], 64, prb)], 192.0, n)
                    self.finish_qk(pn[0:128, 0:n], 128, pnb, self.gn[:, G0 + 6:G0 + 7], rstd, rb, False, None, None, tok0, n,
                                   self.sq[SQR["d_qn"] + h * 128:SQR["d_qn"] + (h + 1) * 128, tok0:tok0 + n], 128)
                    self.finish_qk(pr[0:64, 0:n], 64, prb, self.gn[:, G0 + 7:G0 + 8], rstd, rb, tok0 < SEQ, tab, tabb, tok0, n,
                                   self.sq[SQR["d_qr"] + h * 64:SQR["d_qr"] + (h + 1) * 64, tok0:tok0 + n], 64)
                if h + 1 < 4:
                    nxt_n = ws.fetch(w2d, COL["d_q"] + (h + 1) * 192, 128)
                    nxt_r = ws.fetch(w2d, COL["d_q"] + (h + 1) * 192 + 128, 64)
            wck = [ws.fetch(w2d, COL["d_ckv"] + j * 128, 128) for j in range(4)]
            ckvn = self.sb(st, "ckvn", [128, 4, T], BF16)
            ckb = Buf("ckvn")
            kr = self.sb(st, "krraw", [64, T], F32)
            krb = Buf("krraw")
            wkr, wkrb = ws.fetch(w2d, COL["d_kr"], 64)
            for (tok0, n) in chunks:
                parts = []
                for j in range(4):
                    pt, pb = self.ps_p.next()
                    tiles = [self.bigb[i] for i in range(tok0 // 128, (tok0 + n) // 128)]
                    for k in range(16):
                        S.op("pe", lambda e, k=k, j=j, pt=pt: e.matmul(pt[:, 0:n], lhsT=wck[j][0][:, k, :],
                                                                   rhs=self.big[:, k, tok0:tok0 + n], start=(k == 0), stop=(k == 15)),
                             reads=[wck[j][1]] + tiles, writes=[pb], inc=(k == 15))
                    pt, pb = self.evac(pt, pb, 128, n)
                    parts.append((pt[0:128, 0:n], 128, pb))
                rstd, rb = self.rstd_of(parts, 512.0, n)
                for j in range(4):
                    src, _, pb = parts[j]
                    S.op("dve", lambda e, j=j, src=src: e.scalar_tensor_tensor(out=ckvn[:, j, tok0:tok0 + n], in0=src,
                                                                           scalar=self.gn[:, G0 + 10 + j:G0 + 11 + j], in1=rstd[:, 0:n],
                                                                           op0=ALU.mult, op1=ALU.mult),
                         reads=[pb, rb, self.gnb], writes=[ckb])
                pr, prb = feat_group(wkr, wkrb, 0, 64, tok0, n)
                S.op("act", lambda e: e.activation(out=kr[:, tok0:tok0 + n], in_=pr[0:64, 0:n], func=AF.Copy), reads=[prb], writes=[krb])
            wsu = Prog.WS(self, st, 4, nwb=3, conv=("pool",))
            vout = Ring(nc, st, "voutd", [128, 128], BF16, 4)
            for h in range(4):
                wk, wkb = wsu.fetch(self.w_uk[l], h * 128, 128)
                wv, wvb = wsu.fetch(self.w_uv[l], h * 128, 128)
                for (tok0, n) in chunks:
                    pt, pb = self.ps_p.next()
                    for k in range(4):
                        S.op("pe", lambda e, k=k: e.matmul(pt[:, 0:n], lhsT=wk[:, k, :], rhs=ckvn[:, k, tok0:tok0 + n],
                                                          start=(k == 0), stop=(k == 3)),
                             reads=[wkb, ckb], writes=[pb], inc=(k == 3))
                    pt, pb = self.evac(pt, pb, 128, n)
                    rstd, rb = self.rstd_of([(pt[0:128, 0:n], 128, pb), (kr[0:64, tok0:tok0 + n], 64, krb)], 192.0, n)
                    self.finish_qk(pt[0:128, 0:n], 128, pb, self.gn[:, G0 + 8:G0 + 9], rstd, rb, False, None, None, tok0, n,
                                   self.sq[SQR["d_kn"] + h * 128:SQR["d_kn"] + (h + 1) * 128, tok0:tok0 + n], 128)
                    self.finish_qk(kr[0:64, tok0:tok0 + n], 64, krb, self.gn[:, G0 + 9:G0 + 10], rstd, rb, tok0 < SEQ, tab, tabb, tok0, n,
                                   self.sq[SQR["d_kr"] + h * 64:SQR["d_kr"] + (h + 1) * 64, tok0:tok0 + n], 64)
                for t in range(NT):
                    pt, pb = self.ps_p.next()
                    for k in range(4):
                        S.op("pe", lambda e, k=k, t=t: e.matmul(pt[:, 0:128], lhsT=ckvn[:, k, t * 128:(t + 1) * 128], rhs=wv[:, k, :],
                                                              start=(k == 0), stop=(k == 3)),
                             reads=[wvb, ckb], writes=[pb], inc=(k == 3))
                    vo, vob = vout.next()
                    S.op("dve", lambda e: e.tensor_copy(vo[:], pt[:, 0:128]), reads=[pb], writes=[vob])
                    v0 = SVC["d_v"] + h * 128
                    S.dma([lambda e, t=t: e.dma_start(out=self.sv[t * 128:(t + 1) * 128, v0:v0 + 128], in_=vo[:])], vob, reads=[vob])
            S.barrier()

    def stage_attn(self, l, last):
        S, nc = self.S, self.nc
        with ExitStack() as st:
            qr_ = Ring(nc, st, "aq", [128, T], BF16, 2)
            kr_ = Ring(nc, st, "ak", [128, T], BF16, 2)
            q2_ = Ring(nc, st, "aq2", [64, T], BF16, 2)
            k2_ = Ring(nc, st, "ak2", [64, T], BF16, 2)
            vr_ = Ring(nc, st, "av", [128, NT, 128], BF16, 2)
            gr_ = Ring(nc, st, "ag", [128, T], BF16, 2)
            self.pS = Ring(nc, st, "pS", [128, 512], F32, 4, psum=True)
            self.pO = Ring(nc, st, "pO", [128, 512], F32, 2, psum=True)
            self.pN = Ring(nc, st, "pN", [128, 512], F32, 2, psum=True)
            self.pr = Ring(nc, st, "pr", [128, 512], BF16, 6)
            self.tr = Ring(nc, st, "tr", [128, 512], F32, 6)
            pt_, ptb = self.load(st, "ptab", [128, 4 * NPT * 64], F32, self.rpbx[l])
            with ExitStack() as st2:
                mb, mbb = self.load(st2, "mbt", [128, NPT * 64], F32, self.maskb)
                for h in range(4):
                    S.op("dve", lambda e, h=h: e.tensor_tensor(pt_[:, h * NPT * 64:(h + 1) * NPT * 64], pt_[:, h * NPT * 64:(h + 1) * NPT * 64],
                                                               mb[:], ALU.add), reads=[ptb, mbb], writes=[ptb])
                S.barrier()
            mc, mcb = self.load(st, "mct", [128, 256], F32, self.maskc)
            sk, skb = self.load(st, "sk", [128, DEPTH * 4], F32, self.sink[0].partition_broadcast(128))
            S.op("act", lambda e: e.activation(out=sk[:], in_=sk[:], func=AF.Exp), reads=[skb], writes=[skb])

            heads = []
            for h in range(4):
                heads.append(dict(kind="glob", q=[(SQR["a_q"] + h * 128, 128)], k=[(SQR["a_k"] + (h // 2) * 128, 128)],
                                  v=SVC["a_v"] + (h // 2) * 128, g=SQR["gate"] + h * 128, mc=h, scale=128 ** -0.5, sink=None))
            for h in range(4):
                heads.append(dict(kind="nbr", q=[(SQR["b_q"] + h * 128, 128)], k=[(SQR["b_k"] + h * 128, 128)],
                                  v=SVC["b_v"] + h * 128, g=SQR["gate"] + (4 + h) * 128, mc=4 + h, scale=128 ** -0.5, sink=None, h=h))
            for h in range(4):
                heads.append(dict(kind="win", q=[(SQR["c_q"] + h * 128, 128)], k=[(SQR["c_k"] + (h // 2) * 128, 128)],
                                  v=SVC["c_v"] + (h // 2) * 128, g=SQR["gate"] + (8 + h) * 128, mc=8 + h, scale=128 ** -0.5,
                                  sink=l * 4 + h))
            for h in range(4):
                heads.append(dict(kind="glob", q=[(SQR["d_qn"] + h * 128, 128), (SQR["d_qr"] + h * 64, 64)],
                                  k=[(SQR["d_kn"] + h * 128, 128), (SQR["d_kr"] + h * 64, 64)],
                                  v=SVC["d_v"] + h * 128, g=SQR["gate"] + (12 + h) * 128, mc=12 + h, scale=192 ** -0.5, sink=None))
            if self.debug and self.debug[0].startswith("mix") and len(self.debug) > 3:
                heads = [heads[i] for i in self.debug[3]]

            def loadhead(hd):
                r = {}
                q, qb = qr_.next()
                k, kb = kr_.next()
                v, vb = vr_.next()
                g, gb = gr_.next()
                r0, _ = hd["q"][0]
                S.dma([lambda e: e.dma_start(out=q[:], in_=self.sq[r0:r0 + 128, :])], qb, writes=[qb])
                k0, _ = hd["k"][0]
                S.dma([lambda e: e.dma_start(out=k[:], in_=self.sq[k0:k0 + 128, :])], kb, writes=[kb])
                v0 = hd["v"]
                S.dma([lambda e: e.dma_start(out=v[:], in_=self.sv[:, v0:v0 + 128].rearrange("(j p) c -> p j c", p=128))], vb, writes=[vb])
                g0 = hd["g"]
                S.dma([lambda e: e.dma_start(out=g[:], in_=self.sq[g0:g0 + 128, :])], gb, writes=[gb])
                r["qp"] = [(q, 128, qb)]
                r["kp"] = [(k, 128, kb)]
                if len(hd["q"]) > 1:
                    q2, q2b = q2_.next()
                    k2, k2b = k2_.next()
                    r1, _ = hd["q"][1]
                    k1, _ = hd["k"][1]
                    S.dma([lambda e: e.dma_start(out=q2[:], in_=self.sq[r1:r1 + 64, :])], q2b, writes=[q2b])
                    S.dma([lambda e: e.dma_start(out=k2[:], in_=self.sq[k1:k1 + 64, :])], k2b, writes=[k2b])
                    r["qp"].append((q2, 64, q2b))
                    r["kp"].append((k2, 64, k2b))
                r["v"], r["vb"], r["g"], r["gb"] = v, vb, g, gb
                return r

            nxt = loadhead(heads[0])
            for hi, hd in enumerate(heads):
                cur = nxt
                if hi + 1 < len(heads):
                    nxt = loadhead(heads[hi + 1])
                esink = sk[:, hd["sink"]:hd["sink"] + 1] if hd["sink"] is not None else None
                if hd["kind"] == "glob":
                    for qb_ in range(4):
                        self.attn_block(cur, hd, qb_ * 512, 512, [(j, None, None) for j in range(NT)], esink, skb)
                elif hd["kind"] == "win":
                    prev = None
                    for i in range(NXT):
                        groups = [([16, 17, i], None, None)]
                        if i == 0:
                            groups.append(([i + 1], mc[:, 128:256], mcb))
                        elif i == NXT - 1:
                            groups.append(([i - 1], mc[:, 0:128], mcb))
                        else:
                            groups.append(([i - 1, i + 1], mc[:, 0:256], mcb))
                        stt_ = self.attn_b1(cur, hd, i * 128, 128, groups)
                        if prev is not None:
                            self.attn_b2(cur, hd, prev, esink, skb)
                        prev = stt_
                    self.attn_b2(cur, hd, prev, esink, skb)
                else:
                    h = hd["h"]
                    prev = None
                    for qrow in range(32):
                        rs = min(max(qrow - 4, 0), 24)
                        if rs % 2 == 0:
                            a0 = rs - qrow + 7
                            p0 = a0 // 2 if a0 % 2 == 0 else 7 + (a0 - 1) // 2
                            js = [rs // 2 + c for c in range(4)]
                        else:
                            p0 = 14
                            js = [(rs - 1) // 2 + c for c in range(5)]
                        b0 = (h * NPT + p0) * 64
                        groups = [(js, pt_[:, b0:b0 + len(js) * 64], ptb), ([16, 17], None, None)]
                        stt_ = self.attn_b1(cur, hd, qrow * 64, 64, groups)
                        if prev is not None:
                            self.attn_b2(cur, hd, prev, esink, skb)
                        prev = stt_
                    self.attn_b2(cur, hd, prev, esink, skb)
                if not last:
                    self.attn_block(cur, hd, SEQ, CTXL, [(16, None, None), (17, None, None)], esink, skb)
            S.end_stage()

    def attn_block(self, cur, hd, q0, nq, chunks, esink, skb):
        S = self.S
        scale = hd["scale"]
        po, pob = self.pO.next()
        pn, pnb = self.pN.next()
        v, vb = cur["v"], cur["vb"]
        pend = None
        nch = len(chunks)

        def pv(idx, j, p, pb_):
            S.op("pe", lambda e: e.matmul(po[:, 0:nq], lhsT=v[:, j, :], rhs=p[:, 0:nq], start=(idx == 0), stop=(idx == nch - 1)),
                 reads=[vb, pb_], writes=[pob], inc=False)
            S.op("pe", lambda e: e.matmul(pn[:, 0:nq], lhsT=self.ones[:, :], rhs=p[:, 0:nq], start=(idx == 0), stop=(idx == nch - 1)),
                 reads=[self.cb, pb_], writes=[pnb])

        for idx, (j, bias, biasb) in enumerate(chunks):
            ps, psb = self.pS.next()
            np_ = len(cur["qp"])
            for pi in range(np_):
                qt, nr, qb_ = cur["qp"][pi]
                kt, _, kb_ = cur["kp"][pi]
                S.op("pe", lambda e, pi=pi, qt=qt, kt=kt, nr=nr: e.matmul(ps[:, 0:nq], lhsT=kt[0:nr, j * 128:(j + 1) * 128],
                                                                         rhs=qt[0:nr, q0:q0 + nq], start=(pi == 0), stop=(pi == np_ - 1)),
                     reads=[qb_, kb_], writes=[psb], inc=(pi == np_ - 1))
            p, pb_ = self.pr.next()
            if bias is None:
                S.op("act", lambda e: e.activation(out=p[:, 0:nq], in_=ps[:, 0:nq], func=AF.Exp, scale=scale), reads=[psb], writes=[pb_])
            else:
                t, tb = self.tr.next()
                S.op("dve", lambda e: e.scalar_tensor_tensor(out=t[:, 0:nq], in0=ps[:, 0:nq], scalar=scale, in1=bias,
                                                             op0=ALU.mult, op1=ALU.add), reads=[psb, biasb], writes=[tb])
                S.op("act", lambda e: e.activation(out=p[:, 0:nq], in_=t[:, 0:nq], func=AF.Exp), reads=[tb], writes=[pb_])
            if pend is not None:
                pv(*pend)
            pend = (idx, j, p, pb_)
        pv(*pend)
        ri, rib = self.tr.next()
        if esink is not None:
            S.op("dve", lambda e: e.tensor_scalar(ri[:, 0:nq], pn[:, 0:nq], esink, None, op0=ALU.add), reads=[pnb, skb], writes=[rib])
            S.op("dve", lambda e: e.reciprocal(ri[:, 0:nq], ri[:, 0:nq]), reads=[rib], writes=[rib])
        else:
            S.op("dve", lambda e: e.reciprocal(ri[:, 0:nq], pn[:, 0:nq]), reads=[pnb], writes=[rib])
        o, ob = self.tr.next()
        S.op("dve", lambda e: e.tensor_tensor(o[:, 0:nq], po[:, 0:nq], ri[:, 0:nq], ALU.mult), reads=[pob, rib], writes=[ob])
        tiles = [self.bigb[i] for i in range(q0 // 128, (q0 + nq + 127) // 128)]
        g, gb = cur["g"], cur["gb"]
        S.op("pool", lambda e: e.tensor_tensor(self.big[:, hd["mc"], q0:q0 + nq], o[:, 0:nq], g[:, q0:q0 + nq], ALU.mult),
             reads=[ob, gb], writes=tiles)

    def attn_b1(self, cur, hd, q0, nq, groups):
        S = self.S
        scale = hd["scale"]
        pend = []
        for (js, bias, biasb) in groups:
            ps, psb = self.pS.next()
            w = len(js) * nq
            np_ = len(cur["qp"])
            for c, j in enumerate(js):
                for pi in range(np_):
                    qt, nr, qb_ = cur["qp"][pi]
                    kt, _, kb_ = cur["kp"][pi]
                    S.op("pe", lambda e: e.matmul(ps[:, c * nq:(c + 1) * nq], lhsT=kt[0:nr, j * 128:(j + 1) * 128],
                                                  rhs=qt[0:nr, q0:q0 + nq], start=(pi == 0), stop=(pi == np_ - 1)),
                         reads=[qb_, kb_], writes=[psb], inc=(pi == np_ - 1 and c == len(js) - 1))
            p, pb_ = self.pr.next()
            if bias is None:
                S.op("act", lambda e: e.activation(out=p[:, 0:w], in_=ps[:, 0:w], func=AF.Exp, scale=scale), reads=[psb], writes=[pb_])
            else:
                t, tb = self.tr.next()
                S.op("dve", lambda e: e.scalar_tensor_tensor(out=t[:, 0:w], in0=ps[:, 0:w], scalar=scale, in1=bias,
                                                             op0=ALU.mult, op1=ALU.add), reads=[psb, biasb], writes=[tb])
                S.op("act", lambda e: e.activation(out=p[:, 0:w], in_=t[:, 0:w], func=AF.Exp), reads=[tb], writes=[pb_])
            for c, j in enumerate(js):
                pend.append((j, p, pb_, c))
        return (q0, nq, pend)

    def attn_b2(self, cur, hd, state, esink, skb):
        S = self.S
        q0, nq, pend = state
        po, pob = self.pO.next()
        pn, pnb = self.pN.next()
        v, vb = cur["v"], cur["vb"]
        total = len(pend)
        for idx, (j, p, pb_, c) in enumerate(pend):
            S.op("pe", lambda e: e.matmul(po[:, 0:nq], lhsT=v[:, j, :], rhs=p[:, c * nq:(c + 1) * nq],
                                          start=(idx == 0), stop=(idx == total - 1)), reads=[vb, pb_], writes=[pob], inc=False)
            S.op("pe", lambda e: e.matmul(pn[:, 0:nq], lhsT=self.ones[:, :], rhs=p[:, c * nq:(c + 1) * nq],
                                          start=(idx == 0), stop=(idx == total - 1)), reads=[self.cb, pb_], writes=[pnb])
        self.attn_fin(cur, hd, q0, nq, po, pob, pn, pnb, esink, skb)

    def attn_fin(self, cur, hd, q0, nq, po, pob, pn, pnb, esink, skb):
        S = self.S
        ri, rib = self.tr.next()
        if esink is not None:
            S.op("dve", lambda e: e.tensor_scalar(ri[:, 0:nq], pn[:, 0:nq], esink, None, op0=ALU.add), reads=[pnb, skb], writes=[rib])
            S.op("dve", lambda e: e.reciprocal(ri[:, 0:nq], ri[:, 0:nq]), reads=[rib], writes=[rib])
        else:
            S.op("dve", lambda e: e.reciprocal(ri[:, 0:nq], pn[:, 0:nq]), reads=[pnb], writes=[rib])
        o, ob = self.tr.next()
        S.op("dve", lambda e: e.tensor_tensor(o[:, 0:nq], po[:, 0:nq], ri[:, 0:nq], ALU.mult), reads=[pob, rib], writes=[ob])
        tiles = [self.bigb[i] for i in range(q0 // 128, (q0 + nq + 127) // 128)]
        g, gb = cur["g"], cur["gb"]
        S.op("pool", lambda e: e.tensor_tensor(self.big[:, hd["mc"], q0:q0 + nq], o[:, 0:nq], g[:, q0:q0 + nq], ALU.mult),
             reads=[ob, gb], writes=tiles)

    def stage_out(self, l, last):
        S, nc = self.S, self.nc
        w2d = self.w_out[l]
        ntile = NXT if last else NT
        with ExitStack() as st:
            ws = Prog.WS(self, st, 16, nwb=8, conv=("act", "pool"))
            psO = Ring(nc, st, "psO", [128, 512], F32, 3, psum=True)
            xr = Ring(nc, st, "xo", [128, 512], F32, 4)
            yr = Ring(nc, st, "yo", [128, 512], F32, 4)
            gt = self.sb(st, "gtx", [128, D], F32)
            gtb = Buf("gtx")
            gc = self.sb(st, "gtc", [128, D], F32)
            gcb = Buf("gtc")
            S.dma([lambda e: e.dma_start(out=gt[:], in_=self.modv[l, 0, 2 * D:3 * D].partition_broadcast(128))], gtb, writes=[gtb])
            S.dma([lambda e: e.dma_start(out=gc[:], in_=self.modv[l, 1, 2 * D:3 * D].partition_broadcast(128))], gcb, writes=[gcb])
            xsrc = self.x if l == 0 else self.x1
            csrc = self.ctx if l == 0 else self.hc1
            xdst = self.out if last else self.x1
            slabs = [ws.fetch(w2d, j * 128, 128) for j in range(4)]
            for cb_ in range(4):
                cur = slabs
                if cb_ + 1 < 4:
                    slabs = [ws.fetch(w2d, (cb_ + 1) * 512 + j * 128, 128) for j in range(4)]
                c0 = cb_ * 512
                def tile_io(t):
                    if t < NXT:
                        return (xsrc[t * 128:(t + 1) * 128, c0:c0 + 512], xdst[t * 128:(t + 1) * 128, c0:c0 + 512], gt, gtb)
                    return (csrc[(t - NXT) * 128:(t - NXT + 1) * 128, c0:c0 + 512],
                            self.hc1[(t - NXT) * 128:(t - NXT + 1) * 128, c0:c0 + 512], gc, gcb)

                def xload(t):
                    xt, xb = xr.next()
                    src = tile_io(t)[0]
                    S.dma([lambda e: e.dma_start(out=xt[:], in_=src)], xb, writes=[xb])
                    return xt, xb

                xq = [xload(0), xload(1)]
                for t in range(ntile):
                    _, dst, gg, ggb = tile_io(t)
                    xt, xb = xq.pop(0)
                    if t + 2 < ntile:
                        xq.append(xload(t + 2))
                    pt, pb = psO.next()
                    for j in range(4):
                        wb, bw = cur[j]
                        for k in range(16):
                            S.op("pe", lambda e: e.matmul(pt[:, j * 128:(j + 1) * 128], lhsT=self.big[:, k, t * 128:(t + 1) * 128],
                                                          rhs=wb[:, k, :], start=(k == 0), stop=(k == 15)),
                                 reads=[bw, self.bigb[t]], writes=[pb], inc=(k == 15))
                    y, yb = yr.next()
                    S.op("dve", lambda e: e.tensor_tensor(y[:], pt[:], gg[:, c0:c0 + 512], ALU.mult), reads=[pb, ggb], writes=[yb])
                    S.op("dve", lambda e: e.tensor_tensor(y[:], y[:], xt[:], ALU.add), reads=[yb, xb], writes=[yb])
                    S.dma([lambda e: e.dma_start(out=dst, in_=y[:])], yb, reads=[yb])
            S.end_stage()


def _rope_tables(rot_dim):
    t = np.arange(SEQ)
    row = (t // 64).astype(np.float32)
    col = (t % 64).astype(np.float32)
    nf = rot_dim // 4
    inv = (10000.0 ** (-np.arange(nf, dtype=np.float32) / nf)).astype(np.float32)
    ang = np.concatenate([row[:, None] * inv, col[:, None] * inv], axis=-1).astype(np.float32)
    cos = np.cos(ang).astype(np.float32).T
    sin = np.sin(ang).astype(np.float32).T
    return np.ascontiguousarray(np.concatenate([np.concatenate([cos, cos], 0), np.concatenate([sin, sin], 0)], axis=1))


def _rmat():
    r = np.zeros((128, 192), np.float32)
    for m in range(64):
        r[m + 64, m] = -1.0
        r[m, m + 64] = 1.0
    for m in range(32):
        r[m + 32, 128 + m] = -1.0
        r[m, 128 + m + 32] = 1.0
    return r


def _nbr_entries():
    ent = [(a, a + 1) for a in range(0, 14, 2)] + [(a, a + 1) for a in range(1, 14, 2)]
    ent += [(None, 3), (4, 5), (6, 7), (8, 9), (10, None)]
    return ent


def _nbr_tables(rpb):
    ent = _nbr_entries()
    kcol = np.arange(64)[:, None]
    qcol = np.arange(64)[None, :]
    cs = np.clip(qcol - 8, 0, 48)
    colvalid = (kcol >= cs) & (kcol < cs + 16)
    dc = np.clip(kcol - qcol + 15, 0, 30)
    mask = np.full((NPT, 128, 64), NEG, np.float32)
    idx_a = np.zeros((NPT, 128), np.int64)
    blk_ok = np.zeros((NPT, 128), bool)
    for e, (at, ab) in enumerate(ent):
        for half, a in ((0, at), (1, ab)):
            sl = slice(half * 64, half * 64 + 64)
            if a is None:
                continue
            idx_a[e, sl] = a
            blk_ok[e, sl] = True
            mask[e, sl, :] = np.where(colvalid, 0.0, NEG)
    dcf = np.concatenate([dc, dc], 0)
    g = rpb[:, :, idx_a[:, :, None], dcf[None, :, :]]
    g = np.where(blk_ok[None, None, :, :, None], g, np.float32(0.0)).astype(np.float32)
    rpbx = np.ascontiguousarray(g.transpose(0, 3, 1, 2, 4).reshape(DEPTH, 128, 4 * NPT * 64))
    maskb = np.ascontiguousarray(mask.transpose(1, 0, 2).reshape(128, NPT * 64))
    return rpbx, maskb


def _maskc():
    p = np.arange(128)[:, None]
    f = np.arange(128)[None, :]
    prev = np.where(f <= p, 0.0, NEG).astype(np.float32)
    nxt = np.where(p <= f, 0.0, NEG).astype(np.float32)
    return np.ascontiguousarray(np.concatenate([prev, nxt], axis=1))


def _col(v, n=128):
    o = np.zeros((128,), np.float32)
    o[:len(v)] = v
    return o


def make_in_maps(inp):
    f = lambda a: np.ascontiguousarray(np.asarray(a, dtype=np.float32))
    gains = np.zeros((128, DEPTH * NGAIN), np.float32)
    for l in range(DEPTH):
        cols = [inp["a_q_g"][l], inp["a_k_g"][l], inp["b_q_g"][l], inp["b_k_g"][l], inp["c_q_g"][l], inp["c_k_g"][l],
                inp["d_q_g"][l][:128], inp["d_q_g"][l][128:], inp["d_k_g"][l][:128], inp["d_k_g"][l][128:]]
        cols += [inp["d_kv_g"][l][j * 128:(j + 1) * 128] for j in range(4)]
        for j, c in enumerate(cols):
            gains[:, l * NGAIN + j] = _col(np.asarray(c, np.float32))
    rpbx, maskb = _nbr_tables(f(inp["b_rpb"]))
    shared = dict(norm_g=f(inp["norm_g"]), b_ada=f(inp["b_ada"]), w_ada=f(inp["w_ada"]), w_in=f(inp["w_in"]),
                  w_out=f(inp["w_out"]), w_uk=f(inp["d_w_uk"]), w_uv=f(inp["d_w_uv"]), gains=gains,
                  sink=f(inp["c_sink"]).reshape(1, DEPTH * 4), rpbx=rpbx, maskb=maskb, maskc=_maskc(),
                  ident=np.eye(128, dtype=np.float32), rmat=_rmat(), rope_h=_rope_tables(128), rope_r=_rope_tables(64))
    cctx = f(inp["c_ctx"]).reshape(16, 128).T
    maps = []
    for b in range(NCORES):
        cf = np.ascontiguousarray(np.concatenate([f(inp["c"][b]).reshape(16, 128).T, cctx], axis=1))
        m = dict(shared)
        m.update(x=f(inp["x"][b]), ctx=f(inp["ctx"][b]), cfm=cf)
        maps.append(m)
    return maps


_PROG = {}


def kernel(**inputs):
    if "p" not in _PROG:
        _PROG["p"] = Prog()
    prog = _PROG["p"]
    maps = make_in_maps(inputs)
    res = run_bass_kernel_spmd(prog.nc, maps, core_ids=list(range(NCORES)))
    return np.stack([np.asarray(r["out"], dtype=np.float32) for r in res.results], axis=0)
```

```python
import numpy as np
from contextlib import ExitStack
import concourse.bass as bass
import concourse.mybir as mybir
from concourse.bass_utils import run_bass_kernel_spmd

F32 = mybir.dt.float32
BF16 = mybir.dt.bfloat16
AF = mybir.ActivationFunctionType
ALU = mybir.AluOpType

D = 2048
SEQ = 2048
CTXL = 256
T = SEQ + CTXL
NT = T // 128
NXT = SEQ // 128
DEPTH = 2
INC = 6976
EPS = 1e-6
NEG = -30000.0
NCORES = 8

COL = dict(a_q=0, a_k=512, a_v=768, a_g=1024, b_q=1536, b_k=2048, b_v=2560, b_g=3072,
           c_q=3584, c_k=4096, c_v=4352, c_g=4608, d_q=5120, d_ckv=5888, d_kr=6400, d_g=6464)
SQR = {}
_r = 0
for _n, _rows in (("a_q", 512), ("a_k", 256), ("b_q", 512), ("b_k", 512), ("c_q", 512), ("c_k", 256),
                  ("d_qn", 512), ("d_qr", 256), ("d_kn", 512), ("d_kr", 256), ("gate", 2048)):
    SQR[_n] = _r
    _r += _rows
SQ_ROWS = _r
SVC = dict(a_v=0, b_v=256, c_v=768, d_v=1024)
SV_COLS = 1536
NGAIN = 14
NPT = 19


class Buf:
    __slots__ = ("name", "w", "r", "sem", "cnt")

    def __init__(self, name):
        self.name = name
        self.w = None
        self.r = []
        self.sem = None
        self.cnt = 0


class Sched:
    ENG = ("pe", "act", "dve", "pool", "sp")

    def __init__(self, nc, stack):
        self.nc = nc
        self.stack = stack
        self.eng = {"pe": nc.tensor, "act": nc.scalar, "dve": nc.vector, "pool": nc.gpsimd, "sp": nc.sync}
        self.sems = {}
        self.count = {}
        for e in ("pe", "act", "dve", "pool"):
            self.sems[e] = stack.enter_context(nc.semaphore("prog_" + e))
            self.count[e] = 0
        self.seen = {e: {} for e in self.ENG}
        self.semcnt = {}
        self.freek = []
        self.stagek = []
        self.persist = True
        self.ninst = 0
        self.nwait = 0

    def _bufsem(self, b):
        if b.sem is None:
            if self.freek:
                key = self.freek.pop()
            else:
                key = "dsem%d" % len(self.semcnt)
                self.sems[key] = self.stack.enter_context(self.nc.semaphore(key))
                self.semcnt[key] = 0
            b.sem = key
            if not self.persist:
                self.stagek.append(key)
        return b.sem

    def end_stage(self):
        self.barrier()
        self.freek.extend(self.stagek)
        self.stagek = []

    def _wait(self, engine, deps):
        need = {}
        for t in deps:
            if t is None:
                continue
            k, v = t
            if need.get(k, 0) < v:
                need[k] = v
        seen = self.seen[engine]
        for k, v in need.items():
            if seen.get(k, 0) >= v:
                continue
            self.eng[engine].wait_ge(self.sems[k], v)
            self.nwait += 1
            seen[k] = v

    def op(self, engine, fn, reads=(), writes=(), inc=True):
        deps = []
        own = set()
        for b in reads:
            deps.append(b.w)
            if b.w is not None and b.w[0] == engine:
                own.add(b.w)
        for b in writes:
            deps.append(b.w)
            deps.extend(b.r)
        deps = [t for t in deps if t is not None and (t[0] != engine or t in own)]
        self._wait(engine, deps)
        inst = fn(self.eng[engine])
        if inc:
            self.count[engine] += 1
            inst.then_inc(self.sems[engine], 1)
            tok = (engine, self.count[engine])
        else:
            assert engine == "pe"
            tok = (engine, self.count[engine] + 1)
        for b in writes:
            b.w = tok
            b.r = []
        for b in reads:
            if b not in writes:
                b.r.append(tok)
        self.ninst += 1
        return inst

    def dma(self, fns, owner, reads=(), writes=(), queue="sp"):
        deps = []
        for b in reads:
            deps.append(b.w)
        for b in writes:
            deps.append(b.w)
            deps.extend(b.r)
        self._wait(queue, deps)
        key = self._bufsem(owner)
        for fn in fns:
            inst = fn(self.eng[queue])
            self.semcnt[key] += 16
            inst.then_inc(self.sems[key], 16)
            self.ninst += 1
        tok = (key, self.semcnt[key])
        for b in writes:
            b.w = tok
            b.r = []
        for b in reads:
            if b not in writes:
                b.r.append(tok)
        return tok

    def barrier(self):
        toks = [(e, self.count[e]) for e in ("pe", "act", "dve", "pool")]
        toks += [(k, v) for k, v in self.semcnt.items()]
        for e in self.ENG:
            self._wait(e, toks)


_UID = [0]


def _uniq(name):
    _UID[0] += 1
    return "%s_u%d" % (name, _UID[0])


class Ring:
    def __init__(self, nc, st, name, shape, dtype, n, psum=False):
        self.t = []
        for i in range(n):
            nm = _uniq("%s%d" % (name, i))
            if psum:
                t = st.enter_context(nc.psum_tensor(nm, shape, dtype))
            else:
                t = st.enter_context(nc.sbuf_tensor(nm, shape, dtype))
            self.t.append((t, Buf(nm)))
        self.i = 0

    def next(self):
        r = self.t[self.i % len(self.t)]
        self.i += 1
        return r


class Prog:
    def __init__(self, depth=DEPTH, debug=None):
        self.depth = depth
        self.debug = debug
        nc = bass.Bass("TRN2", target_bir_lowering=False)
        self.nc = nc

        def din(name, shape, dt=F32):
            return nc.dram_tensor(name, list(shape), dt, kind="ExternalInput").ap()

        self.x = din("x", [SEQ, D])
        self.ctx = din("ctx", [CTXL, D])
        self.cfm = din("cfm", [128, 32])
        self.norm_g = din("norm_g", [DEPTH, D])
        self.b_ada = din("b_ada", [DEPTH, 3 * D])
        self.w_ada = din("w_ada", [DEPTH, D, 3 * D])
        self.w_in = din("w_in", [DEPTH, D, INC])
        self.w_out = din("w_out", [DEPTH, D, D])
        self.w_uk = din("w_uk", [DEPTH, 512, 512])
        self.w_uv = din("w_uv", [DEPTH, 512, 512])
        self.gains = din("gains", [128, DEPTH * NGAIN])
        self.sink = din("sink", [1, DEPTH * 4])
        self.rpbx = din("rpbx", [DEPTH, 128, 4 * NPT * 64])
        self.maskb = din("maskb", [128, NPT * 64])
        self.maskc = din("maskc", [128, 256])
        self.ident = din("ident", [128, 128])
        self.rmat = din("rmat", [128, 192])
        self.rope_h = din("rope_h", [128, 2 * SEQ])
        self.rope_r = din("rope_r", [64, 2 * SEQ])
        self.out = nc.dram_tensor("out", [SEQ, D], F32, kind="ExternalOutput").ap()
        self.modv = nc.dram_tensor("modv", [DEPTH, 2, 3 * D], F32).ap()
        self.sq = nc.dram_tensor("sq", [SQ_ROWS, T], BF16).ap()
        self.sv = nc.dram_tensor("sv", [T, SV_COLS], BF16).ap()
        self.x1 = nc.dram_tensor("x1", [SEQ, D], F32).ap()
        self.hc1 = nc.dram_tensor("hc1", [CTXL, D], F32).ap()
        if debug:
            self.dbg = nc.dram_tensor("dbg", list(debug[1]), debug[2], kind="ExternalOutput").ap()

        with ExitStack() as st:
            self.st = st
            self.S = Sched(nc, st)
            self.build()

    def sb(self, st, name, shape, dt):
        return st.enter_context(self.nc.sbuf_tensor(_uniq(name), list(shape), dt))

    def load(self, st, name, shape, dt, src):
        t = self.sb(st, name, shape, dt)
        b = Buf(name)
        self.S.dma([lambda e: e.dma_start(out=t[:], in_=src)], b, writes=[b])
        return t, b

    def build(self):
        S, nc, st = self.S, self.nc, self.st
        self.big = self.sb(st, "big", [128, 16, T], BF16)
        self.bigb = [Buf("big%d" % i) for i in range(NT)]
        idf, idfb = self.load(st, "idf", [128, 128], F32, self.ident)
        rmf, rmfb = self.load(st, "rmf", [128, 192], F32, self.rmat)
        self.idb = self.sb(st, "idb", [128, 128], BF16)
        self.rmb = self.sb(st, "rmb", [128, 192], BF16)
        self.ones = self.sb(st, "ones", [128, 128], BF16)
        self.cb = Buf("consts")
        S.op("dve", lambda e: e.tensor_copy(self.idb[:], idf[:]), reads=[idfb], writes=[self.cb])
        S.op("dve", lambda e: e.tensor_copy(self.rmb[:], rmf[:]), reads=[rmfb], writes=[self.cb])
        S.op("dve", lambda e: e.memset(self.ones[:], 1.0), writes=[self.cb])
        self.gn, self.gnb = self.load(st, "gn", [128, DEPTH * NGAIN], F32, self.gains)
        S.barrier()
        S.persist = False
        for l in range(self.depth):
            with nc.named_scope("ada%d" % l):
                self.stage_ada(l)
        for l in range(self.depth):
            last = (l == DEPTH - 1)
            with nc.named_scope("norm%d" % l):
                self.stage_norm(l)
            if self.debug and self.debug[0] == "hT%d" % l:
                self.dump_big()
                return
            with nc.named_scope("proj%d" % l):
                self.stage_proj(l, last)
            if self.debug and self.debug[0] == "sq%d" % l:
                return self.dump_dram(self.sq)
            if self.debug and self.debug[0] == "sv%d" % l:
                return self.dump_dram(self.sv)
            with nc.named_scope("attn%d" % l):
                self.stage_attn(l, last)
            if self.debug and self.debug[0] == "mix%d" % l:
                self.dump_big()
                return
            with nc.named_scope("out%d" % l):
                self.stage_out(l, last)
            if self.debug and self.debug[0] == "xo%d" % l:
                return self.dump_dram(self.x1 if not last else self.out)

    def dump_big(self):
        S = self.S
        b = Buf("dump")
        S.dma([lambda e: e.dma_start(out=self.dbg, in_=self.big[:])], b)
        S.barrier()

    def dump_dram(self, src):
        S = self.S
        b = Buf("dump")
        S.dma([lambda e: e.dma_start(out=self.dbg, in_=src)], b)
        S.barrier()

    class WS:
        def __init__(self, P, st, kch, nwb=3, conv=("pool",)):
            self.P = P
            self.kch = kch
            self.stage = Ring(P.nc, st, "wst", [128, kch, 128], F32, 2)
            self.wb = Ring(P.nc, st, "wbb", [128, kch, 128], BF16, nwb)
            self.conv = conv
            self.n = 0

        def fetch(self, w2d, c0, ncols, kch=None):
            P, S = self.P, self.P.S
            kch = kch or self.kch
            stg, bs = self.stage.next()
            wb, bw = self.wb.next()
            src = w2d[:, c0:c0 + ncols].rearrange("(kc p) n -> p kc n", p=128)
            S.dma([lambda e: e.dma_start(out=stg[:, :kch, :ncols], in_=src)], bs, writes=[bs])
            eng = self.conv[self.n % len(self.conv)]
            self.n += 1
            if eng == "act":
                S.op("act", lambda e: e.activation(out=wb[:, :kch, :ncols], in_=stg[:, :kch, :ncols], func=AF.Copy),
                     reads=[bs], writes=[bw])
            else:
                S.op(eng, lambda e: e.tensor_copy(wb[:, :kch, :ncols], stg[:, :kch, :ncols]), reads=[bs], writes=[bw])
            return wb, bw

    def stage_ada(self, l):
        S, nc = self.S, self.nc
        with ExitStack() as st:
            cf, cfb = self.load(st, "cf", [128, 32], F32, self.cfm)
            s2 = self.sb(st, "s2", [128, 16, 2], BF16)
            s2b = Buf("s2")
            sil = self.sb(st, "sil", [128, 32], F32)
            silb = Buf("sil")
            S.op("act", lambda e: e.activation(out=sil[:], in_=cf[:], func=AF.Silu), reads=[cfb], writes=[silb])
            S.op("dve", lambda e: e.tensor_copy(s2[:, :, 0], sil[:, 0:16]), reads=[silb], writes=[s2b])
            S.op("dve", lambda e: e.tensor_copy(s2[:, :, 1], sil[:, 16:32]), reads=[silb], writes=[s2b])
            modrow = self.sb(st, "modrow", [2, 3 * D], F32)
            mrb = Buf("modrow")
            badd, baddb = self.load(st, "badd", [2, 3 * D], F32, self.b_ada[l].partition_broadcast(2))
            g2, g2b = self.load(st, "g2", [2, D], F32, self.norm_g[l].partition_broadcast(2))
            ps = Ring(nc, st, "psA", [128, 512], F32, 2, psum=True)
            stg = Ring(nc, st, "adst", [128, 16, 256], F32, 2)
            wbr = Ring(nc, st, "adwb", [128, 16, 256], BF16, 2)
            w2d = self.w_ada[l]
            nsl = 3 * D // 256

            def fetch(i):
                t, b = stg.next()
                wb, bw = wbr.next()
                src = w2d[:, i * 256:(i + 1) * 256].rearrange("(kc p) n -> p kc n", p=128)
                S.dma([lambda e: e.dma_start(out=t[:, 0:8, :], in_=src[:, 0:8, :]),
                       lambda e: e.dma_start(out=t[:, 8:16, :], in_=src[:, 8:16, :])], b, writes=[b])
                S.op("pool", lambda e: e.tensor_copy(wb[:, 0:6, :], t[:, 0:6, :]), reads=[b], writes=[bw])
                S.op("act", lambda e: e.activation(out=wb[:, 6:11, :], in_=t[:, 6:11, :], func=AF.Copy), reads=[b], writes=[bw])
                S.op("dve", lambda e: e.tensor_copy(wb[:, 11:16, :], t[:, 11:16, :]), reads=[b], writes=[bw])
                return wb, bw

            q = [fetch(0)]
            for i in range(nsl):
                wb, bw = q.pop(0)
                if i + 1 < nsl:
                    q.append(fetch(i + 1))
                pt, pb = ps.next()
                for k in range(16):
                    S.op("pe", lambda e, k=k: e.matmul(pt[0:2, 0:256], lhsT=s2[:, k, :], rhs=wb[:, k, :],
                                                      start=(k == 0), stop=(k == 15)),
                         reads=[s2b, bw], writes=[pb], inc=(k == 15))
                S.op("dve", lambda e, i=i: e.tensor_copy(modrow[:, i * 256:(i + 1) * 256], pt[0:2, 0:256]),
                     reads=[pb], writes=[mrb])
            S.op("dve", lambda e: e.tensor_tensor(modrow[:], modrow[:], badd[:], ALU.add), reads=[mrb, baddb], writes=[mrb])
            S.op("dve", lambda e: e.scalar_tensor_tensor(out=modrow[:, D:2 * D], in0=modrow[:, D:2 * D], scalar=1.0,
                                                         in1=g2[:], op0=ALU.add, op1=ALU.mult),
                 reads=[mrb, g2b], writes=[mrb])
            S.dma([lambda e: e.dma_start(out=self.modv[l], in_=modrow[:])], mrb, reads=[mrb])
            S.end_stage()

    def stage_norm(self, l):
        S, nc = self.S, self.nc
        with ExitStack() as st:
            xr = Ring(nc, st, "xt", [128, D], F32, 3)
            junk = self.sb(st, "junk", [128, D], BF16)
            junkb = Buf("junk")
            tf = Ring(nc, st, "tf", [128, D], F32, 2)
            hb = Ring(nc, st, "hb", [128, D], BF16, 2)
            ssr = Ring(nc, st, "ssn", [128, 2], F32, 4)
            abc = self.sb(st, "abc", [128, D], F32)
            bbc = self.sb(st, "bbc", [128, D], F32)
            abcb, bbcb = Buf("abc"), Buf("bbc")
            pst = Ring(nc, st, "psT", [128, 8, 128], BF16, 4, psum=True)
            xsrc = self.x if l == 0 else self.x1
            csrc = self.ctx if l == 0 else self.hc1
            for t in range(NT):
                if t == 0 or t == NXT:
                    w = 0 if t == 0 else 1
                    S.dma([lambda e, w=w: e.dma_start(out=abc[:], in_=self.modv[l, w, D:2 * D].partition_broadcast(128))],
                          abcb, writes=[abcb])
                    S.dma([lambda e, w=w: e.dma_start(out=bbc[:], in_=self.modv[l, w, 0:D].partition_broadcast(128))],
                          bbcb, writes=[bbcb])
                src = xsrc[t * 128:(t + 1) * 128, :] if t < NXT else csrc[(t - NXT) * 128:(t - NXT + 1) * 128, :]
                xt, xb = xr.next()
                S.dma([lambda e, xt=xt, src=src: e.dma_start(out=xt[:], in_=src)], xb, writes=[xb])
                ss, ssb = ssr.next()
                S.op("act", lambda e, xt=xt, ss=ss: e.activation(out=junk[:], in_=xt[:], func=AF.Square, accum_out=ss[:, 0:1]),
                     reads=[xb], writes=[junkb, ssb])
                S.op("act", lambda e, ss=ss: e.activation(out=ss[:, 1:2], in_=ss[:, 0:1], func=AF.Sqrt, bias=EPS, scale=1.0 / D),
                     reads=[ssb], writes=[ssb])
                S.op("dve", lambda e, ss=ss: e.reciprocal(ss[:, 1:2], ss[:, 1:2]), reads=[ssb], writes=[ssb])
                t1, t1b = tf.next()
                S.op("dve", lambda e, xt=xt, ss=ss, t1=t1: e.scalar_tensor_tensor(out=t1[:], in0=xt[:], scalar=ss[:, 1:2], in1=abc[:],
                                                                               op0=ALU.mult, op1=ALU.mult),
                     reads=[xb, ssb, abcb], writes=[t1b])
                h, hbb = hb.next()
                S.op("pool" if t % 2 else "dve", lambda e, t1=t1, h=h: e.tensor_tensor(h[:], t1[:], bbc[:], ALU.add),
                     reads=[t1b, bbcb], writes=[hbb])
                for g in range(2):
                    pt, pb = pst.next()
                    for i in range(8):
                        c = g * 8 + i
                        S.op("pe", lambda e, pt=pt, i=i, c=c, h=h: e.transpose(pt[:, i, :], h[:, c * 128:(c + 1) * 128], self.idb[:]),
                             reads=[hbb, self.cb], writes=[pb], inc=(i == 7))
                    eng = "act" if g == 0 else "dve"
                    if eng == "act":
                        S.op("act", lambda e, pt=pt, g=g, t=t: e.activation(out=self.big[:, g * 8:(g + 1) * 8, t * 128:(t + 1) * 128],
                                                                          in_=pt[:], func=AF.Copy),
                             reads=[pb], writes=[self.bigb[t]])
                    else:
                        S.op("dve", lambda e, pt=pt, g=g, t=t: e.tensor_copy(self.big[:, g * 8:(g + 1) * 8, t * 128:(t + 1) * 128], pt[:]),
                             reads=[pb], writes=[self.bigb[t]])
            S.end_stage()

    def stage_proj(self, l, last):
        S, nc = self.S, self.nc
        G0 = l * NGAIN
        w2d = self.w_in[l]
        chunks = [(0, 512), (512, 512), (1024, 512), (1536, 512), (2048, 256)]
        with ExitStack() as st:
            ws = Prog.WS(self, st, 16, nwb=5, conv=("pool",))
            self.ps_p = Ring(nc, st, "psP", [128, 512], F32, 4, psum=True)
            self.ps_s = Ring(nc, st, "psS", [128, 512], F32, 2, psum=True)
            self.ps_r = Ring(nc, st, "psR", [128, 512], F32, 2, psum=True)
            self.sqr = Ring(nc, st, "sqr", [128, 512], BF16, 6)
            self.f32r = Ring(nc, st, "f32r", [128, 512], F32, 5)
            self.obr = Ring(nc, st, "obr", [128, 512], BF16, 8)
            self.qfr = Ring(nc, st, "qfr", [128, 512], F32, 6)
            vout = Ring(nc, st, "vout", [128, 128], BF16, 4)

            def feat_group(wb, bw, m0, m, tok0, n):
                pt, pb = self.ps_p.next()
                tiles = [self.bigb[i] for i in range(tok0 // 128, (tok0 + n) // 128)]
                for k in range(16):
                    S.op("pe", lambda e, k=k: e.matmul(pt[0:m, 0:n], lhsT=wb[:, k, m0:m0 + m], rhs=self.big[:, k, tok0:tok0 + n],
                                                      start=(k == 0), stop=(k == 15)),
                         reads=[bw] + tiles, writes=[pb], inc=(k == 15))
                return pt, pb

            with ExitStack() as st2:
                tab, tabb = self.load(st2, "ropeh", [128, 2 * SEQ], F32, self.rope_h)
                qk = []
                for (nm, nh, gcol, rope) in (("a_q", 4, 0, True), ("a_k", 2, 1, True), ("b_q", 4, 2, False), ("b_k", 4, 3, False),
                                             ("c_q", 4, 4, True), ("c_k", 2, 5, True)):
                    for h in range(nh):
                        qk.append((COL[nm] + h * 128, SQR[nm] + h * 128, gcol, rope))
                state = {"nxt": ws.fetch(w2d, qk[0][0], 128)}
                jobs = []
                for i, (c0, r0, gcol, rope) in enumerate(qk):
                    for ci, (tok0, n) in enumerate(chunks):
                        d = {}

                        def A(d=d, i=i, ci=ci, tok0=tok0, n=n):
                            if ci == 0:
                                state["cur"] = state["nxt"]
                                if i + 1 < len(qk):
                                    state["nxt"] = ws.fetch(w2d, qk[i + 1][0], 128)
                            wb, bw = state["cur"]
                            pt, pb = feat_group(wb, bw, 0, 128, tok0, n)
                            d["qf"], d["qfb"] = self.evac(pt, pb, 128, n)
                            d["sq"] = self.sq_part(d["qf"][0:128, 0:n], 128, d["qfb"], n)

                        def B(d=d, gcol=gcol, n=n):
                            rstd, rb = self.rstd_from([d["sq"]], 128.0, n)
                            d["qn"], d["qb"] = self.norm_apply(d["qf"][0:128, 0:n], 128, d["qfb"],
                                                               self.gn[:, G0 + gcol:G0 + gcol + 1], rstd, rb, n)

                        def C(d=d, rope=rope, tok0=tok0, n=n, r0=r0):
                            self.rope_store(d["qn"], d["qb"], 128, rope and tok0 < SEQ, tab, tabb, tok0, n,
                                            self.sq[r0:r0 + 128, tok0:tok0 + n], 128)
                        jobs.append((A, B, C))
                self.run_pipe(jobs)
                S.barrier()
            self.proj_d(l, st, ws, w2d, chunks, feat_group)
            S.barrier()
            vg = []
            for nm, ncol in (("a_v", 256), ("b_v", 512), ("c_v", 256)):
                for j in range(ncol // 128):
                    vg.append((COL[nm] + j * 128, SVC[nm] + j * 128))
            nxt = ws.fetch(w2d, vg[0][0], 128)
            for i, (c0, v0) in enumerate(vg):
                wb, bw = nxt
                if i + 1 < len(vg):
                    nxt = ws.fetch(w2d, vg[i + 1][0], 128)
                for t in range(NT):
                    pt, pb = self.ps_p.next()
                    for k in range(16):
                        S.op("pe", lambda e, k=k, t=t: e.matmul(pt[:, 0:128], lhsT=self.big[:, k, t * 128:(t + 1) * 128], rhs=wb[:, k, :],
                                                              start=(k == 0), stop=(k == 15)),
                             reads=[bw, self.bigb[t]], writes=[pb], inc=(k == 15))
                    vo, vob = vout.next()
                    eng = "act" if t % 2 == 0 else "dve"
                    if eng == "act":
                        S.op("act", lambda e: e.activation(out=vo[:], in_=pt[:, 0:128], func=AF.Copy), reads=[pb], writes=[vob])
                    else:
                        S.op("dve", lambda e: e.tensor_copy(vo[:], pt[:, 0:128]), reads=[pb], writes=[vob])
                    S.dma([lambda e, t=t: e.dma_start(out=self.sv[t * 128:(t + 1) * 128, v0:v0 + 128], in_=vo[:])], vob, reads=[vob])
            gg = []
            for bi, nm in enumerate(("a_g", "b_g", "c_g", "d_g")):
                for h in range(4):
                    gg.append((COL[nm] + h * 128, SQR["gate"] + (bi * 4 + h) * 128))
            gchunks = chunks[:4] if last else chunks
            nxt = ws.fetch(w2d, gg[0][0], 128)
            for i, (c0, r0) in enumerate(gg):
                wb, bw = nxt
                if i + 1 < len(gg):
                    nxt = ws.fetch(w2d, gg[i + 1][0], 128)
                for (tok0, n) in gchunks:
                    pt, pb = feat_group(wb, bw, 0, 128, tok0, n)
                    o, ob = self.obr.next()
                    S.op("act", lambda e: e.activation(out=o[:, 0:n], in_=pt[:, 0:n], func=AF.Silu), reads=[pb], writes=[ob])
                    S.dma([lambda e: e.dma_start(out=self.sq[r0:r0 + 128, tok0:tok0 + n], in_=o[:, 0:n])], ob, reads=[ob])
            S.end_stage()

    def evac(self, pt, pb, nr, n):
        qf, qfb = self.qfr.next()
        self.S.op("act", lambda e: e.activation(out=qf[0:nr, 0:n], in_=pt[0:nr, 0:n], func=AF.Copy), reads=[pb], writes=[qfb])
        return qf, qfb

    def rstd_of(self, parts, dim, n):
        S = self.S
        st_, sb_ = self.ps_s.next()
        for i, (src, nr, b) in enumerate(parts):
            sq, sqb = self.sqr.next()
            S.op("act", lambda e: e.activation(out=sq[0:nr, 0:n], in_=src, func=AF.Square), reads=[b], writes=[sqb])
            S.op("pe", lambda e: e.matmul(st_[:, 0:n], lhsT=self.ones[0:nr, :], rhs=sq[0:nr, 0:n],
                                          start=(i == 0), stop=(i == len(parts) - 1)),
                 reads=[sqb, self.cb], writes=[sb_])
        r, rb = self.f32r.next()
        S.op("act", lambda e: e.activation(out=r[:, 0:n], in_=st_[:, 0:n], func=AF.Sqrt, bias=EPS, scale=1.0 / dim),
             reads=[sb_], writes=[rb])
        S.op("dve", lambda e: e.reciprocal(r[:, 0:n], r[:, 0:n]), reads=[rb], writes=[rb])
        return r, rb

    def finish_qk(self, src, nr, sbuf_, gain, rstd, rb, rope, tab, tabb, tok0, n, dst, rdim):
        S = self.S
        qn, qb = self.obr.next()
        S.op("dve", lambda e: e.scalar_tensor_tensor(out=qn[0:nr, 0:n], in0=src, scalar=gain[0:nr, :], in1=rstd[0:nr, 0:n],
                                                     op0=ALU.mult, op1=ALU.mult),
             reads=[sbuf_, rb, self.gnb], writes=[qb])
        if rope:
            rt, rtb = self.ps_r.next()
            rm = self.rmb[0:128, 0:128] if rdim == 128 else self.rmb[0:64, 128:192]
            S.op("pe", lambda e: e.matmul(rt[0:nr, 0:n], lhsT=rm, rhs=qn[0:nr, 0:n], start=True, stop=True),
                 reads=[qb, self.cb], writes=[rtb])
            t1, t1b = self.f32r.next()
            t2, t2b = self.f32r.next()
            S.op("dve", lambda e: e.tensor_tensor(t1[0:nr, 0:n], qn[0:nr, 0:n], tab[0:nr, tok0:tok0 + n], ALU.mult),
                 reads=[qb, tabb], writes=[t1b])
            S.op("dve", lambda e: e.tensor_tensor(t2[0:nr, 0:n], rt[0:nr, 0:n], tab[0:nr, SEQ + tok0:SEQ + tok0 + n], ALU.mult),
                 reads=[rtb, tabb], writes=[t2b])
            o, ob = self.obr.next()
            S.op("dve", lambda e: e.tensor_tensor(o[0:nr, 0:n], t1[0:nr, 0:n], t2[0:nr, 0:n], ALU.add),
                 reads=[t1b, t2b], writes=[ob])
        else:
            o, ob = qn, qb
        S.dma([lambda e: e.dma_start(out=dst, in_=o[0:nr, 0:n])], ob, reads=[ob])

    def sq_part(self, src, nr, b, n):
        sq, sqb = self.sqr.next()
        self.S.op("act", lambda e: e.activation(out=sq[0:nr, 0:n], in_=src, func=AF.Square), reads=[b], writes=[sqb])
        return (sq, sqb, nr)

    def rstd_from(self, sqs, dim, n):
        S = self.S
        st_, sb_ = self.ps_s.next()
        for i, (sq, sqb, nr) in enumerate(sqs):
            S.op("pe", lambda e: e.matmul(st_[:, 0:n], lhsT=self.ones[0:nr, :], rhs=sq[0:nr, 0:n],
                                          start=(i == 0), stop=(i == len(sqs) - 1)),
                 reads=[sqb, self.cb], writes=[sb_], inc=(i == len(sqs) - 1))
        r, rb = self.f32r.next()
        S.op("act", lambda e: e.activation(out=r[:, 0:n], in_=st_[:, 0:n], func=AF.Sqrt, bias=EPS, scale=1.0 / dim),
             reads=[sb_], writes=[rb])
        S.op("dve", lambda e: e.reciprocal(r[:, 0:n], r[:, 0:n]), reads=[rb], writes=[rb])
        return r, rb

    def norm_apply(self, src, nr, sbuf_, gain, rstd, rb, n):
        qn, qb = self.obr.next()
        self.S.op("dve", lambda e: e.scalar_tensor_tensor(out=qn[0:nr, 0:n], in0=src, scalar=gain[0:nr, :], in1=rstd[0:nr, 0:n],
                                                          op0=ALU.mult, op1=ALU.mult),
                  reads=[sbuf_, rb, self.gnb], writes=[qb])
        return qn, qb

    def rope_store(self, qn, qb, nr, rope, tab, tabb, tok0, n, dst, rdim):
        S = self.S
        if rope:
            rt, rtb = self.ps_r.next()
            rm = self.rmb[0:128, 0:128] if rdim == 128 else self.rmb[0:64, 128:192]
            S.op("pe", lambda e: e.matmul(rt[0:nr, 0:n], lhsT=rm, rhs=qn[0:nr, 0:n], start=True, stop=True),
                 reads=[qb, self.cb], writes=[rtb])
            t1, t1b = self.f32r.next()
            t2, t2b = self.f32r.next()
            S.op("dve", lambda e: e.tensor_tensor(t1[0:nr, 0:n], qn[0:nr, 0:n], tab[0:nr, tok0:tok0 + n], ALU.mult),
                 reads=[qb, tabb], writes=[t1b])
            S.op("dve", lambda e: e.tensor_tensor(t2[0:nr, 0:n], rt[0:nr, 0:n], tab[0:nr, SEQ + tok0:SEQ + tok0 + n], ALU.mult),
                 reads=[rtb, tabb], writes=[t2b])
            o, ob = self.obr.next()
            S.op("dve", lambda e: e.tensor_tensor(o[0:nr, 0:n], t1[0:nr, 0:n], t2[0:nr, 0:n], ALU.add),
                 reads=[t1b, t2b], writes=[ob])
        else:
            o, ob = qn, qb
        S.dma([lambda e: e.dma_start(out=dst, in_=o[0:nr, 0:n])], ob, reads=[ob])

    def run_pipe(self, jobs):
        n = len(jobs)
        if n == 0:
            return
        jobs[0][0]()
        for s_ in range(n):
            if s_ + 1 < n:
                jobs[s_ + 1][0]()
            if s_ >= 1:
                jobs[s_ - 1][2]()
            jobs[s_][1]()
        jobs[n - 1][2]()

    def proj_d(self, l, st_outer, ws, w2d, chunks, feat_group):
        S, nc = self.S, self.nc
        G0 = l * NGAIN
        with ExitStack() as st:
            tab, tabb = self.load(st, "roper", [64, 2 * SEQ], F32, self.rope_r)
            state = {"nxt": (ws.fetch(w2d, COL["d_q"], 128), ws.fetch(w2d, COL["d_q"] + 128, 64))}
            jobs = []
            for h in range(4):
                for ci, (tok0, n) in enumerate(chunks):
                    d = {}

                    def A(d=d, h=h, ci=ci, tok0=tok0, n=n):
                        if ci == 0:
                            state["cur"] = state["nxt"]
                            if h + 1 < 4:
                                state["nxt"] = (ws.fetch(w2d, COL["d_q"] + (h + 1) * 192, 128),
                                                ws.fetch(w2d, COL["d_q"] + (h + 1) * 192 + 128, 64))
                        (wn, bwn), (wr, bwr) = state["cur"]
                        pn, pnb = feat_group(wn, bwn, 0, 128, tok0, n)
                        d["qn_f"], d["qn_fb"] = self.evac(pn, pnb, 128, n)
                        d["sqn"] = self.sq_part(d["qn_f"][0:128, 0:n], 128, d["qn_fb"], n)
                        pr, prb = feat_group(wr, bwr, 0, 64, tok0, n)
                        d["qr_f"], d["qr_fb"] = self.evac(pr, prb, 64, n)
                        d["sqr_"] = self.sq_part(d["qr_f"][0:64, 0:n], 64, d["qr_fb"], n)

                    def B(d=d, n=n):
                        rstd, rb = self.rstd_from([d["sqn"], d["sqr_"]], 192.0, n)
                        d["a"] = self.norm_apply(d["qn_f"][0:128, 0:n], 128, d["qn_fb"], self.gn[:, G0 + 6:G0 + 7], rstd, rb, n)
                        d["b"] = self.norm_apply(d["qr_f"][0:64, 0:n], 64, d["qr_fb"], self.gn[:, G0 + 7:G0 + 8], rstd, rb, n)

                    def C(d=d, h=h, tok0=tok0, n=n):
                        self.rope_store(d["a"][0], d["a"][1], 128, False, None, None, tok0, n,
                                        self.sq[SQR["d_qn"] + h * 128:SQR["d_qn"] + (h + 1) * 128, tok0:tok0 + n], 128)
                        self.rope_store(d["b"][0], d["b"][1], 64, tok0 < SEQ, tab, tabb, tok0, n,
                                        self.sq[SQR["d_qr"] + h * 64:SQR["d_qr"] + (h + 1) * 64, tok0:tok0 + n], 64)
                    jobs.append((A, B, C))
            self.run_pipe(jobs)
            wck = [ws.fetch(w2d, COL["d_ckv"] + j * 128, 128) for j in range(4)]
            ckvn = self.sb(st, "ckvn", [128, 4, T], BF16)
            ckb = Buf("ckvn")
            kr = self.sb(st, "krraw", [64, T], F32)
            krb = Buf("krraw")
            wkr, wkrb = ws.fetch(w2d, COL["d_kr"], 64)
            for (tok0, n) in chunks:
                parts = []
                for j in range(4):
                    pt, pb = self.ps_p.next()
                    tiles = [self.bigb[i] for i in range(tok0 // 128, (tok0 + n) // 128)]
                    for k in range(16):
                        S.op("pe", lambda e, k=k, j=j, pt=pt: e.matmul(pt[:, 0:n], lhsT=wck[j][0][:, k, :],
                                                                   rhs=self.big[:, k, tok0:tok0 + n], start=(k == 0), stop=(k == 15)),
                             reads=[wck[j][1]] + tiles, writes=[pb], inc=(k == 15))
                    pt, pb = self.evac(pt, pb, 128, n)
                    parts.append((pt[0:128, 0:n], 128, pb))
                rstd, rb = self.rstd_of(parts, 512.0, n)
                for j in range(4):
                    src, _, pb = parts[j]
                    S.op("dve", lambda e, j=j, src=src: e.scalar_tensor_tensor(out=ckvn[:, j, tok0:tok0 + n], in0=src,
                                                                           scalar=self.gn[:, G0 + 10 + j:G0 + 11 + j], in1=rstd[:, 0:n],
                                                                           op0=ALU.mult, op1=ALU.mult),
                         reads=[pb, rb, self.gnb], writes=[ckb])
                pr, prb = feat_group(wkr, wkrb, 0, 64, tok0, n)
                S.op("act", lambda e: e.activation(out=kr[:, tok0:tok0 + n], in_=pr[0:64, 0:n], func=AF.Copy), reads=[prb], writes=[krb])
            wsu = Prog.WS(self, st, 4, nwb=8, conv=("pool",))
            vout = Ring(nc, st, "voutd", [128, 128], BF16, 4)
            wks = [wsu.fetch(self.w_uk[l], h * 128, 128) for h in range(4)]
            wvs = [wsu.fetch(self.w_uv[l], h * 128, 128) for h in range(4)]
            jobs = []
            for h in range(4):
                for ci, (tok0, n) in enumerate(chunks):
                    d = {}

                    def A(d=d, h=h, tok0=tok0, n=n):
                        wk, wkb = wks[h]
                        pt, pb = self.ps_p.next()
                        for k in range(4):
                            S.op("pe", lambda e, k=k: e.matmul(pt[:, 0:n], lhsT=wk[:, k, :], rhs=ckvn[:, k, tok0:tok0 + n],
                                                              start=(k == 0), stop=(k == 3)),
                                 reads=[wkb, ckb], writes=[pb], inc=(k == 3))
                        d["kf"], d["kfb"] = self.evac(pt, pb, 128, n)
                        d["sqn"] = self.sq_part(d["kf"][0:128, 0:n], 128, d["kfb"], n)
                        d["sqr_"] = self.sq_part(kr[0:64, tok0:tok0 + n], 64, krb, n)

                    def B(d=d, tok0=tok0, n=n):
                        rstd, rb = self.rstd_from([d["sqn"], d["sqr_"]], 192.0, n)
                        d["a"] = self.norm_apply(d["kf"][0:128, 0:n], 128, d["kfb"], self.gn[:, G0 + 8:G0 + 9], rstd, rb, n)
                        d["b"] = self.norm_apply(kr[0:64, tok0:tok0 + n], 64, krb, self.gn[:, G0 + 9:G0 + 10], rstd, rb, n)

                    def C(d=d, h=h, tok0=tok0, n=n):
                        self.rope_store(d["a"][0], d["a"][1], 128, False, None, None, tok0, n,
                                        self.sq[SQR["d_kn"] + h * 128:SQR["d_kn"] + (h + 1) * 128, tok0:tok0 + n], 128)
                        self.rope_store(d["b"][0], d["b"][1], 64, tok0 < SEQ, tab, tabb, tok0, n,
                                        self.sq[SQR["d_kr"] + h * 64:SQR["d_kr"] + (h + 1) * 64, tok0:tok0 + n], 64)
                    jobs.append((A, B, C))
            self.run_pipe(jobs)
            for h in range(4):
                wv, wvb = wvs[h]
                v0 = SVC["d_v"] + h * 128
                for t in range(NT):
                    pt, pb = self.ps_p.next()
                    for k in range(4):
                        S.op("pe", lambda e, k=k, t=t: e.matmul(pt[:, 0:128], lhsT=ckvn[:, k, t * 128:(t + 1) * 128], rhs=wv[:, k, :],
                                                              start=(k == 0), stop=(k == 3)),
                             reads=[wvb, ckb], writes=[pb], inc=(k == 3))
                    vo, vob = vout.next()
                    S.op("dve", lambda e: e.tensor_copy(vo[:], pt[:, 0:128]), reads=[pb], writes=[vob])
                    S.dma([lambda e, t=t: e.dma_start(out=self.sv[t * 128:(t + 1) * 128, v0:v0 + 128], in_=vo[:])], vob, reads=[vob])
            S.barrier()

    def stage_attn(self, l, last):
        S, nc = self.S, self.nc
        with ExitStack() as st:
            qr_ = Ring(nc, st, "aq", [128, T], BF16, 2)
            kr_ = Ring(nc, st, "ak", [128, T], BF16, 2)
            q2_ = Ring(nc, st, "aq2", [64, T], BF16, 2)
            k2_ = Ring(nc, st, "ak2", [64, T], BF16, 2)
            vr_ = Ring(nc, st, "av", [128, NT, 128], BF16, 2)
            gr_ = Ring(nc, st, "ag", [128, T], BF16, 2)
            self.pS = Ring(nc, st, "pS", [128, 512], F32, 4, psum=True)
            self.pO = Ring(nc, st, "pO", [128, 512], F32, 2, psum=True)
            self.pN = Ring(nc, st, "pN", [128, 512], F32, 2, psum=True)
            self.pr = Ring(nc, st, "pr", [128, 512], BF16, 6)
            self.tr = Ring(nc, st, "tr", [128, 512], F32, 6)
            pt_, ptb = self.load(st, "ptab", [128, 4 * NPT * 64], F32, self.rpbx[l])
            with ExitStack() as st2:
                mb, mbb = self.load(st2, "mbt", [128, NPT * 64], F32, self.maskb)
                for h in range(4):
                    S.op("dve", lambda e, h=h: e.tensor_tensor(pt_[:, h * NPT * 64:(h + 1) * NPT * 64], pt_[:, h * NPT * 64:(h + 1) * NPT * 64],
                                                               mb[:], ALU.add), reads=[ptb, mbb], writes=[ptb])
                S.barrier()
            mc, mcb = self.load(st, "mct", [128, 256], F32, self.maskc)
            sk, skb = self.load(st, "sk", [128, DEPTH * 4], F32, self.sink[0].partition_broadcast(128))
            S.op("act", lambda e: e.activation(out=sk[:], in_=sk[:], func=AF.Exp), reads=[skb], writes=[skb])

            heads = []
            for h in range(4):
                heads.append(dict(kind="glob", q=[(SQR["a_q"] + h * 128, 128)], k=[(SQR["a_k"] + (h // 2) * 128, 128)],
                                  v=SVC["a_v"] + (h // 2) * 128, g=SQR["gate"] + h * 128, mc=h, scale=128 ** -0.5, sink=None))
            for h in range(4):
                heads.append(dict(kind="nbr", q=[(SQR["b_q"] + h * 128, 128)], k=[(SQR["b_k"] + h * 128, 128)],
                                  v=SVC["b_v"] + h * 128, g=SQR["gate"] + (4 + h) * 128, mc=4 + h, scale=128 ** -0.5, sink=None, h=h))
            for h in range(4):
                heads.append(dict(kind="win", q=[(SQR["c_q"] + h * 128, 128)], k=[(SQR["c_k"] + (h // 2) * 128, 128)],
                                  v=SVC["c_v"] + (h // 2) * 128, g=SQR["gate"] + (8 + h) * 128, mc=8 + h, scale=128 ** -0.5,
                                  sink=l * 4 + h))
            for h in range(4):
                heads.append(dict(kind="glob", q=[(SQR["d_qn"] + h * 128, 128), (SQR["d_qr"] + h * 64, 64)],
                                  k=[(SQR["d_kn"] + h * 128, 128), (SQR["d_kr"] + h * 64, 64)],
                                  v=SVC["d_v"] + h * 128, g=SQR["gate"] + (12 + h) * 128, mc=12 + h, scale=192 ** -0.5, sink=None))
            if self.debug and self.debug[0].startswith("mix") and len(self.debug) > 3:
                heads = [heads[i] for i in self.debug[3]]

            def loadhead(hd):
                r = {}
                q, qb = qr_.next()
                k, kb = kr_.next()
                v, vb = vr_.next()
                g, gb = gr_.next()
                r0, _ = hd["q"][0]
                S.dma([lambda e: e.dma_start(out=q[:], in_=self.sq[r0:r0 + 128, :])], qb, writes=[qb])
                k0, _ = hd["k"][0]
                S.dma([lambda e: e.dma_start(out=k[:], in_=self.sq[k0:k0 + 128, :])], kb, writes=[kb])
                v0 = hd["v"]
                S.dma([lambda e: e.dma_start(out=v[:], in_=self.sv[:, v0:v0 + 128].rearrange("(j p) c -> p j c", p=128))], vb, writes=[vb])
                g0 = hd["g"]
                S.dma([lambda e: e.dma_start(out=g[:], in_=self.sq[g0:g0 + 128, :])], gb, writes=[gb])
                r["qp"] = [(q, 128, qb)]
                r["kp"] = [(k, 128, kb)]
                if len(hd["q"]) > 1:
                    q2, q2b = q2_.next()
                    k2, k2b = k2_.next()
                    r1, _ = hd["q"][1]
                    k1, _ = hd["k"][1]
                    S.dma([lambda e: e.dma_start(out=q2[:], in_=self.sq[r1:r1 + 64, :])], q2b, writes=[q2b])
                    S.dma([lambda e: e.dma_start(out=k2[:], in_=self.sq[k1:k1 + 64, :])], k2b, writes=[k2b])
                    r["qp"].append((q2, 64, q2b))
                    r["kp"].append((k2, 64, k2b))
                r["v"], r["vb"], r["g"], r["gb"] = v, vb, g, gb
                return r

            nxt = loadhead(heads[0])
            for hi, hd in enumerate(heads):
                cur = nxt
                if hi + 1 < len(heads):
                    nxt = loadhead(heads[hi + 1])
                esink = sk[:, hd["sink"]:hd["sink"] + 1] if hd["sink"] is not None else None
                if hd["kind"] == "glob":
                    for qb_ in range(4):
                        self.attn_block(cur, hd, qb_ * 512, 512, [(j, None, None) for j in range(NT)], esink, skb)
                elif hd["kind"] == "win":
                    prev = None
                    for i in range(NXT):
                        groups = [([16, 17, i], None, None)]
                        if i == 0:
                            groups.append(([i + 1], mc[:, 128:256], mcb))
                        elif i == NXT - 1:
                            groups.append(([i - 1], mc[:, 0:128], mcb))
                        else:
                            groups.append(([i - 1, i + 1], mc[:, 0:256], mcb))
                        stt_ = self.attn_b1(cur, hd, i * 128, 128, groups)
                        if prev is not None:
                            self.attn_b2(cur, hd, prev, esink, skb)
                        prev = stt_
                    self.attn_b2(cur, hd, prev, esink, skb)
                else:
                    h = hd["h"]
                    prev = None
                    for qrow in range(32):
                        rs = min(max(qrow - 4, 0), 24)
                        if rs % 2 == 0:
                            a0 = rs - qrow + 7
                            p0 = a0 // 2 if a0 % 2 == 0 else 7 + (a0 - 1) // 2
                            js = [rs // 2 + c for c in range(4)]
                        else:
                            p0 = 14
                            js = [(rs - 1) // 2 + c for c in range(5)]
                        b0 = (h * NPT + p0) * 64
                        groups = [(js, pt_[:, b0:b0 + len(js) * 64], ptb), ([16, 17], None, None)]
                        stt_ = self.attn_b1(cur, hd, qrow * 64, 64, groups)
                        if prev is not None:
                            self.attn_b2(cur, hd, prev, esink, skb)
                        prev = stt_
                    self.attn_b2(cur, hd, prev, esink, skb)
                if not last:
                    self.attn_block(cur, hd, SEQ, CTXL, [(16, None, None), (17, None, None)], esink, skb)
            S.end_stage()

    def attn_block(self, cur, hd, q0, nq, chunks, esink, skb):
        S = self.S
        scale = hd["scale"]
        po, pob = self.pO.next()
        pn, pnb = self.pN.next()
        v, vb = cur["v"], cur["vb"]
        pend = None
        nch = len(chunks)

        def pv(idx, j, p, pb_):
            S.op("pe", lambda e: e.matmul(po[:, 0:nq], lhsT=v[:, j, :], rhs=p[:, 0:nq], start=(idx == 0), stop=(idx == nch - 1)),
                 reads=[vb, pb_], writes=[pob], inc=False)
            S.op("pe", lambda e: e.matmul(pn[:, 0:nq], lhsT=self.ones[:, :], rhs=p[:, 0:nq], start=(idx == 0), stop=(idx == nch - 1)),
                 reads=[self.cb, pb_], writes=[pnb])

        for idx, (j, bias, biasb) in enumerate(chunks):
            ps, psb = self.pS.next()
            np_ = len(cur["qp"])
            for pi in range(np_):
                qt, nr, qb_ = cur["qp"][pi]
                kt, _, kb_ = cur["kp"][pi]
                S.op("pe", lambda e, pi=pi, qt=qt, kt=kt, nr=nr: e.matmul(ps[:, 0:nq], lhsT=kt[0:nr, j * 128:(j + 1) * 128],
                                                                         rhs=qt[0:nr, q0:q0 + nq], start=(pi == 0), stop=(pi == np_ - 1)),
                     reads=[qb_, kb_], writes=[psb], inc=(pi == np_ - 1))
            p, pb_ = self.pr.next()
            if bias is None:
                S.op("act", lambda e: e.activation(out=p[:, 0:nq], in_=ps[:, 0:nq], func=AF.Exp, scale=scale), reads=[psb], writes=[pb_])
            else:
                t, tb = self.tr.next()
                S.op("dve", lambda e: e.scalar_tensor_tensor(out=t[:, 0:nq], in0=ps[:, 0:nq], scalar=scale, in1=bias,
                                                             op0=ALU.mult, op1=ALU.add), reads=[psb, biasb], writes=[tb])
                S.op("act", lambda e: e.activation(out=p[:, 0:nq], in_=t[:, 0:nq], func=AF.Exp), reads=[tb], writes=[pb_])
            if pend is not None:
                pv(*pend)
            pend = (idx, j, p, pb_)
        pv(*pend)
        ri, rib = self.tr.next()
        if esink is not None:
            S.op("dve", lambda e: e.tensor_scalar(ri[:, 0:nq], pn[:, 0:nq], esink, None, op0=ALU.add), reads=[pnb, skb], writes=[rib])
            S.op("dve", lambda e: e.reciprocal(ri[:, 0:nq], ri[:, 0:nq]), reads=[rib], writes=[rib])
        else:
            S.op("dve", lambda e: e.reciprocal(ri[:, 0:nq], pn[:, 0:nq]), reads=[pnb], writes=[rib])
        o, ob = self.tr.next()
        S.op("dve", lambda e: e.tensor_tensor(o[:, 0:nq], po[:, 0:nq], ri[:, 0:nq], ALU.mult), reads=[pob, rib], writes=[ob])
        tiles = [self.bigb[i] for i in range(q0 // 128, (q0 + nq + 127) // 128)]
        g, gb = cur["g"], cur["gb"]
        S.op("pool", lambda e: e.tensor_tensor(self.big[:, hd["mc"], q0:q0 + nq], o[:, 0:nq], g[:, q0:q0 + nq], ALU.mult),
             reads=[ob, gb], writes=tiles)

    def attn_b1(self, cur, hd, q0, nq, groups):
        S = self.S
        scale = hd["scale"]
        pend = []
        for (js, bias, biasb) in groups:
            ps, psb = self.pS.next()
            w = len(js) * nq
            np_ = len(cur["qp"])
            for c, j in enumerate(js):
                for pi in range(np_):
                    qt, nr, qb_ = cur["qp"][pi]
                    kt, _, kb_ = cur["kp"][pi]
                    S.op("pe", lambda e: e.matmul(ps[:, c * nq:(c + 1) * nq], lhsT=kt[0:nr, j * 128:(j + 1) * 128],
                                                  rhs=qt[0:nr, q0:q0 + nq], start=(pi == 0), stop=(pi == np_ - 1)),
                         reads=[qb_, kb_], writes=[psb], inc=(pi == np_ - 1 and c == len(js) - 1))
            p, pb_ = self.pr.next()
            if bias is None:
                S.op("act", lambda e: e.activation(out=p[:, 0:w], in_=ps[:, 0:w], func=AF.Exp, scale=scale), reads=[psb], writes=[pb_])
            else:
                t, tb = self.tr.next()
                S.op("dve", lambda e: e.scalar_tensor_tensor(out=t[:, 0:w], in0=ps[:, 0:w], scalar=scale, in1=bias,
                                                             op0=ALU.mult, op1=ALU.add), reads=[psb, biasb], writes=[tb])
                S.op("act", lambda e: e.activation(out=p[:, 0:w], in_=t[:, 0:w], func=AF.Exp), reads=[tb], writes=[pb_])
            for c, j in enumerate(js):
                pend.append((j, p, pb_, c))
        return (q0, nq, pend)

    def attn_b2(self, cur, hd, state, esink, skb):
        S = self.S
        q0, nq, pend = state
        po, pob = self.pO.next()
        pn, pnb = self.pN.next()
        v, vb = cur["v"], cur["vb"]
        total = len(pend)
        for idx, (j, p, pb_, c) in enumerate(pend):
            S.op("pe", lambda e: e.matmul(po[:, 0:nq], lhsT=v[:, j, :], rhs=p[:, c * nq:(c + 1) * nq],
                                          start=(idx == 0), stop=(idx == total - 1)), reads=[vb, pb_], writes=[pob], inc=False)
            S.op("pe", lambda e: e.matmul(pn[:, 0:nq], lhsT=self.ones[:, :], rhs=p[:, c * nq:(c + 1) * nq],
                                          start=(idx == 0), stop=(idx == total - 1)), reads=[self.cb, pb_], writes=[pnb])
        self.attn_fin(cur, hd, q0, nq, po, pob, pn, pnb, esink, skb)

    def attn_fin(self, cur, hd, q0, nq, po, pob, pn, pnb, esink, skb):
        S = self.S
        ri, rib = self.tr.next()
        if esink is not None:
            S.op("dve", lambda e: e.tensor_scalar(ri[:, 0:nq], pn[:, 0:nq], esink, None, op0=ALU.add), reads=[pnb, skb], writes=[rib])
            S.op("dve", lambda e: e.reciprocal(ri[:, 0:nq], ri[:, 0:nq]), reads=[rib], writes=[rib])
        else:
            S.op("dve", lambda e: e.reciprocal(ri[:, 0:nq], pn[:, 0:nq]), reads=[pnb], writes=[rib])
        o, ob = self.tr.next()
        S.op("dve", lambda e: e.tensor_tensor(o[:, 0:nq], po[:, 0:nq], ri[:, 0:nq], ALU.mult), reads=[pob, rib], writes=[ob])
        tiles = [self.bigb[i] for i in range(q0 // 128, (q0 + nq + 127) // 128)]
        g, gb = cur["g"], cur["gb"]
        S.op("pool", lambda e: e.tensor_tensor(self.big[:, hd["mc"], q0:q0 + nq], o[:, 0:nq], g[:, q0:q0 + nq], ALU.mult),
             reads=[ob, gb], writes=tiles)

    def stage_out(self, l, last):
        S, nc = self.S, self.nc
        w2d = self.w_out[l]
        ntile = NXT if last else NT
        with ExitStack() as st:
            ws = Prog.WS(self, st, 16, nwb=8, conv=("act", "pool"))
            psO = Ring(nc, st, "psO", [128, 512], F32, 3, psum=True)
            xr = Ring(nc, st, "xo", [128, 512], F32, 4)
            yr = Ring(nc, st, "yo", [128, 512], F32, 4)
            gt = self.sb(st, "gtx", [128, D], F32)
            gtb = Buf("gtx")
            gc = self.sb(st, "gtc", [128, D], F32)
            gcb = Buf("gtc")
            S.dma([lambda e: e.dma_start(out=gt[:], in_=self.modv[l, 0, 2 * D:3 * D].partition_broadcast(128))], gtb, writes=[gtb])
            S.dma([lambda e: e.dma_start(out=gc[:], in_=self.modv[l, 1, 2 * D:3 * D].partition_broadcast(128))], gcb, writes=[gcb])
            xsrc = self.x if l == 0 else self.x1
            csrc = self.ctx if l == 0 else self.hc1
            xdst = self.out if last else self.x1
            slabs = [ws.fetch(w2d, j * 128, 128) for j in range(4)]
            for cb_ in range(4):
                cur = slabs
                if cb_ + 1 < 4:
                    slabs = [ws.fetch(w2d, (cb_ + 1) * 512 + j * 128, 128) for j in range(4)]
                c0 = cb_ * 512
                def tile_io(t):
                    if t < NXT:
                        return (xsrc[t * 128:(t + 1) * 128, c0:c0 + 512], xdst[t * 128:(t + 1) * 128, c0:c0 + 512], gt, gtb)
                    return (csrc[(t - NXT) * 128:(t - NXT + 1) * 128, c0:c0 + 512],
                            self.hc1[(t - NXT) * 128:(t - NXT + 1) * 128, c0:c0 + 512], gc, gcb)

                def xload(t):
                    xt, xb = xr.next()
                    src = tile_io(t)[0]
                    S.dma([lambda e: e.dma_start(out=xt[:], in_=src)], xb, writes=[xb])
                    return xt, xb

                xq = [xload(0), xload(1)]
                for t in range(ntile):
                    _, dst, gg, ggb = tile_io(t)
                    xt, xb = xq.pop(0)
                    if t + 2 < ntile:
                        xq.append(xload(t + 2))
                    pt, pb = psO.next()
                    for j in range(4):
                        wb, bw = cur[j]
                        for k in range(16):
                            S.op("pe", lambda e: e.matmul(pt[:, j * 128:(j + 1) * 128], lhsT=self.big[:, k, t * 128:(t + 1) * 128],
                                                          rhs=wb[:, k, :], start=(k == 0), stop=(k == 15)),
                                 reads=[bw, self.bigb[t]], writes=[pb], inc=(k == 15))
                    y, yb = yr.next()
                    S.op("dve", lambda e: e.tensor_tensor(y[:], pt[:], gg[:, c0:c0 + 512], ALU.mult), reads=[pb, ggb], writes=[yb])
                    S.op("dve", lambda e: e.tensor_tensor(y[:], y[:], xt[:], ALU.add), reads=[yb, xb], writes=[yb])
                    S.dma([lambda e: e.dma_start(out=dst, in_=y[:])], yb, reads=[yb])
            S.end_stage()


def _rope_tables(rot_dim):
    t = np.arange(SEQ)
    row = (t // 64).astype(np.float32)
    col = (t % 64).astype(np.float32)
    nf = rot_dim // 4
    inv = (10000.0 ** (-np.arange(nf, dtype=np.float32) / nf)).astype(np.float32)
    ang = np.concatenate([row[:, None] * inv, col[:, None] * inv], axis=-1).astype(np.float32)
    cos = np.cos(ang).astype(np.float32).T
    sin = np.sin(ang).astype(np.float32).T
    return np.ascontiguousarray(np.concatenate([np.concatenate([cos, cos], 0), np.concatenate([sin, sin], 0)], axis=1))


def _rmat():
    r = np.zeros((128, 192), np.float32)
    for m in range(64):
        r[m + 64, m] = -1.0
        r[m, m + 64] = 1.0
    for m in range(32):
        r[m + 32, 128 + m] = -1.0
        r[m, 128 + m + 32] = 1.0
    return r


def _nbr_entries():
    ent = [(a, a + 1) for a in range(0, 14, 2)] + [(a, a + 1) for a in range(1, 14, 2)]
    ent += [(None, 3), (4, 5), (6, 7), (8, 9), (10, None)]
    return ent


def _nbr_tables(rpb):
    ent = _nbr_entries()
    kcol = np.arange(64)[:, None]
    qcol = np.arange(64)[None, :]
    cs = np.clip(qcol - 8, 0, 48)
    colvalid = (kcol >= cs) & (kcol < cs + 16)
    dc = np.clip(kcol - qcol + 15, 0, 30)
    mask = np.full((NPT, 128, 64), NEG, np.float32)
    idx_a = np.zeros((NPT, 128), np.int64)
    blk_ok = np.zeros((NPT, 128), bool)
    for e, (at, ab) in enumerate(ent):
        for half, a in ((0, at), (1, ab)):
            sl = slice(half * 64, half * 64 + 64)
            if a is None:
                continue
            idx_a[e, sl] = a
            blk_ok[e, sl] = True
            mask[e, sl, :] = np.where(colvalid, 0.0, NEG)
    dcf = np.concatenate([dc, dc], 0)
    g = rpb[:, :, idx_a[:, :, None], dcf[None, :, :]]
    g = np.where(blk_ok[None, None, :, :, None], g, np.float32(0.0)).astype(np.float32)
    rpbx = np.ascontiguousarray(g.transpose(0, 3, 1, 2, 4).reshape(DEPTH, 128, 4 * NPT * 64))
    maskb = np.ascontiguousarray(mask.transpose(1, 0, 2).reshape(128, NPT * 64))
    return rpbx, maskb


def _maskc():
    p = np.arange(128)[:, None]
    f = np.arange(128)[None, :]
    prev = np.where(f <= p, 0.0, NEG).astype(np.float32)
    nxt = np.where(p <= f, 0.0, NEG).astype(np.float32)
    return np.ascontiguousarray(np.concatenate([prev, nxt], axis=1))


def _col(v, n=128):
    o = np.zeros((128,), np.float32)
    o[:len(v)] = v
    return o


def make_in_maps(inp):
    f = lambda a: np.ascontiguousarray(np.asarray(a, dtype=np.float32))
    gains = np.zeros((128, DEPTH * NGAIN), np.float32)
    for l in range(DEPTH):
        cols = [inp["a_q_g"][l], inp["a_k_g"][l], inp["b_q_g"][l], inp["b_k_g"][l], inp["c_q_g"][l], inp["c_k_g"][l],
                inp["d_q_g"][l][:128], inp["d_q_g"][l][128:], inp["d_k_g"][l][:128], inp["d_k_g"][l][128:]]
        cols += [inp["d_kv_g"][l][j * 128:(j + 1) * 128] for j in range(4)]
        for j, c in enumerate(cols):
            gains[:, l * NGAIN + j] = _col(np.asarray(c, np.float32))
    rpbx, maskb = _nbr_tables(f(inp["b_rpb"]))
    shared = dict(norm_g=f(inp["norm_g"]), b_ada=f(inp["b_ada"]), w_ada=f(inp["w_ada"]), w_in=f(inp["w_in"]),
                  w_out=f(inp["w_out"]), w_uk=f(inp["d_w_uk"]), w_uv=f(inp["d_w_uv"]), gains=gains,
                  sink=f(inp["c_sink"]).reshape(1, DEPTH * 4), rpbx=rpbx, maskb=maskb, maskc=_maskc(),
                  ident=np.eye(128, dtype=np.float32), rmat=_rmat(), rope_h=_rope_tables(128), rope_r=_rope_tables(64))
    cctx = f(inp["c_ctx"]).reshape(16, 128).T
    maps = []
    for b in range(NCORES):
        cf = np.ascontiguousarray(np.concatenate([f(inp["c"][b]).reshape(16, 128).T, cctx], axis=1))
        m = dict(shared)
        m.update(x=f(inp["x"][b]), ctx=f(inp["ctx"][b]), cfm=cf)
        maps.append(m)
    return maps


_PROG = {}


def kernel(**inputs):
    if "p" not in _PROG:
        _PROG["p"] = Prog()
    prog = _PROG["p"]
    maps = make_in_maps(inputs)
    res = run_bass_kernel_spmd(prog.nc, maps, core_ids=list(range(NCORES)))
    return np.stack([np.asarray(r["out"], dtype=np.float32) for r in res.results], axis=0)
```

```python
import numpy as np
from contextlib import ExitStack
import concourse.bass as bass
import concourse.mybir as mybir
from concourse.bass_utils import run_bass_kernel_spmd

F32 = mybir.dt.float32
BF16 = mybir.dt.bfloat16
AF = mybir.ActivationFunctionType
ALU = mybir.AluOpType

D = 2048
SEQ = 2048
CTXL = 256
T = SEQ + CTXL
NT = T // 128
NXT = SEQ // 128
DEPTH = 2
INC = 6976
EPS = 1e-6
NEG = -30000.0
NCORES = 8

COL = dict(a_q=0, a_k=512, a_v=768, a_g=1024, b_q=1536, b_k=2048, b_v=2560, b_g=3072,
           c_q=3584, c_k=4096, c_v=4352, c_g=4608, d_q=5120, d_ckv=5888, d_kr=6400, d_g=6464)
SQR = {}
_r = 0
for _n, _rows in (("a_q", 512), ("a_k", 256), ("b_q", 512), ("b_k", 512), ("c_q", 512), ("c_k", 256),
                  ("d_qn", 512), ("d_qr", 256), ("d_kn", 512), ("d_kr", 256), ("gate", 2048)):
    SQR[_n] = _r
    _r += _rows
SQ_ROWS = _r
SVC = dict(a_v=0, b_v=256, c_v=768, d_v=1024)
SV_COLS = 1536
NGAIN = 14
NPT = 19


class Buf:
    __slots__ = ("name", "w", "r", "sem", "cnt")

    def __init__(self, name):
        self.name = name
        self.w = None
        self.r = []
        self.sem = None
        self.cnt = 0


class Sched:
    ENG = ("pe", "act", "dve", "pool", "sp")

    def __init__(self, nc, stack):
        self.nc = nc
        self.stack = stack
        self.eng = {"pe": nc.tensor, "act": nc.scalar, "dve": nc.vector, "pool": nc.gpsimd, "sp": nc.sync}
        self.sems = {}
        self.count = {}
        for e in ("pe", "act", "dve", "pool"):
            self.sems[e] = stack.enter_context(nc.semaphore("prog_" + e))
            self.count[e] = 0
        self.seen = {e: {} for e in self.ENG}
        self.semcnt = {}
        self.freek = []
        self.stagek = []
        self.persist = True
        self.ninst = 0
        self.nwait = 0

    def _bufsem(self, b):
        if b.sem is None:
            if self.freek:
                key = self.freek.pop()
            else:
                key = "dsem%d" % len(self.semcnt)
                self.sems[key] = self.stack.enter_context(self.nc.semaphore(key))
                self.semcnt[key] = 0
            b.sem = key
            if not self.persist:
                self.stagek.append(key)
        return b.sem

    def end_stage(self):
        self.barrier()
        self.freek.extend(self.stagek)
        self.stagek = []

    def _wait(self, engine, deps):
        need = {}
        for t in deps:
            if t is None:
                continue
            k, v = t
            if need.get(k, 0) < v:
                need[k] = v
        seen = self.seen[engine]
        for k, v in need.items():
            if seen.get(k, 0) >= v:
                continue
            self.eng[engine].wait_ge(self.sems[k], v)
            self.nwait += 1
            seen[k] = v

    def op(self, engine, fn, reads=(), writes=(), inc=True):
        deps = []
        own = set()
        for b in reads:
            deps.append(b.w)
            if b.w is not None and b.w[0] == engine:
                own.add(b.w)
        for b in writes:
            deps.append(b.w)
            deps.extend(b.r)
        deps = [t for t in deps if t is not None and (t[0] != engine or t in own)]
        self._wait(engine, deps)
        inst = fn(self.eng[engine])
        if inc:
            self.count[engine] += 1
            inst.then_inc(self.sems[engine], 1)
            tok = (engine, self.count[engine])
        else:
            assert engine == "pe"
            tok = (engine, self.count[engine] + 1)
        for b in writes:
            b.w = tok
            b.r = []
        for b in reads:
            if b not in writes:
                b.r.append(tok)
        self.ninst += 1
        return inst

    def dma(self, fns, owner, reads=(), writes=(), queue="sp"):
        deps = []
        for b in reads:
            deps.append(b.w)
        for b in writes:
            deps.append(b.w)
            deps.extend(b.r)
        self._wait(queue, deps)
        key = self._bufsem(owner)
        for fn in fns:
            inst = fn(self.eng[queue])
            self.semcnt[key] += 16
            inst.then_inc(self.sems[key], 16)
            self.ninst += 1
        tok = (key, self.semcnt[key])
        for b in writes:
            b.w = tok
            b.r = []
        for b in reads:
            if b not in writes:
                b.r.append(tok)
        return tok

    def barrier(self):
        toks = [(e, self.count[e]) for e in ("pe", "act", "dve", "pool")]
        toks += [(k, v) for k, v in self.semcnt.items()]
        for e in self.ENG:
            self._wait(e, toks)


_UID = [0]


def _uniq(name):
    _UID[0] += 1
    return "%s_u%d" % (name, _UID[0])


class Ring:
    def __init__(self, nc, st, name, shape, dtype, n, psum=False):
        self.t = []
        for i in range(n):
            nm = _uniq("%s%d" % (name, i))
            if psum:
                t = st.enter_context(nc.psum_tensor(nm, shape, dtype))
            else:
                t = st.enter_context(nc.sbuf_tensor(nm, shape, dtype))
            self.t.append((t, Buf(nm)))
        self.i = 0

    def next(self):
        r = self.t[self.i % len(self.t)]
        self.i += 1
        return r


class Prog:
    def __init__(self, depth=DEPTH, debug=None):
        self.depth = depth
        self.debug = debug
        nc = bass.Bass("TRN2", target_bir_lowering=False)
        self.nc = nc

        def din(name, shape, dt=F32):
            return nc.dram_tensor(name, list(shape), dt, kind="ExternalInput").ap()

        self.x = din("x", [SEQ, D])
        self.ctx = din("ctx", [CTXL, D])
        self.cfm = din("cfm", [128, 32])
        self.norm_g = din("norm_g", [DEPTH, D])
        self.b_ada = din("b_ada", [DEPTH, 3 * D])
        self.w_ada = din("w_ada", [DEPTH, D, 3 * D])
        self.w_in = din("w_in", [DEPTH, D, INC])
        self.w_out = din("w_out", [DEPTH, D, D])
        self.w_uk = din("w_uk", [DEPTH, 512, 512])
        self.w_uv = din("w_uv", [DEPTH, 512, 512])
        self.gains = din("gains", [128, DEPTH * NGAIN])
        self.sink = din("sink", [1, DEPTH * 4])
        self.rpbx = din("rpbx", [DEPTH, 128, 4 * NPT * 64])
        self.maskb = din("maskb", [128, NPT * 64])
        self.maskc = din("maskc", [128, 256])
        self.ident = din("ident", [128, 128])
        self.rmat = din("rmat", [128, 192])
        self.rope_h = din("rope_h", [128, 2 * SEQ])
        self.rope_r = din("rope_r", [64, 2 * SEQ])
        self.out = nc.dram_tensor("out", [SEQ, D], F32, kind="ExternalOutput").ap()
        self.modv = nc.dram_tensor("modv", [DEPTH, 2, 3 * D], F32).ap()
        self.sq = nc.dram_tensor("sq", [SQ_ROWS, T], BF16).ap()
        self.sv = nc.dram_tensor("sv", [T, SV_COLS], BF16).ap()
        self.x1 = nc.dram_tensor("x1", [SEQ, D], F32).ap()
        self.hc1 = nc.dram_tensor("hc1", [CTXL, D], F32).ap()
        if debug:
            self.dbg = nc.dram_tensor("dbg", list(debug[1]), debug[2], kind="ExternalOutput").ap()

        with ExitStack() as st:
            self.st = st
            self.S = Sched(nc, st)
            self.build()

    def sb(self, st, name, shape, dt):
        return st.enter_context(self.nc.sbuf_tensor(_uniq(name), list(shape), dt))

    def load(self, st, name, shape, dt, src):
        t = self.sb(st, name, shape, dt)
        b = Buf(name)
        self.S.dma([lambda e: e.dma_start(out=t[:], in_=src)], b, writes=[b])
        return t, b

    def build(self):
        S, nc, st = self.S, self.nc, self.st
        self.big = self.sb(st, "big", [128, 16, T], BF16)
        self.bigb = [Buf("big%d" % i) for i in range(NT)]
        idf, idfb = self.load(st, "idf", [128, 128], F32, self.ident)
        rmf, rmfb = self.load(st, "rmf", [128, 192], F32, self.rmat)
        self.idb = self.sb(st, "idb", [128, 128], BF16)
        self.rmb = self.sb(st, "rmb", [128, 192], BF16)
        self.ones = self.sb(st, "ones", [128, 128], BF16)
        self.cb = Buf("consts")
        S.op("dve", lambda e: e.tensor_copy(self.idb[:], idf[:]), reads=[idfb], writes=[self.cb])
        S.op("dve", lambda e: e.tensor_copy(self.rmb[:], rmf[:]), reads=[rmfb], writes=[self.cb])
        S.op("dve", lambda e: e.memset(self.ones[:], 1.0), writes=[self.cb])
        self.gn, self.gnb = self.load(st, "gn", [128, DEPTH * NGAIN], F32, self.gains)
        S.barrier()
        S.persist = False
        for l in range(self.depth):
            with nc.named_scope("ada%d" % l):
                self.stage_ada(l)
        for l in range(self.depth):
            last = (l == DEPTH - 1)
            with nc.named_scope("norm%d" % l):
                self.stage_norm(l)
            if self.debug and self.debug[0] == "hT%d" % l:
                self.dump_big()
                return
            with nc.named_scope("proj%d" % l):
                self.stage_proj(l, last)
            if self.debug and self.debug[0] == "sq%d" % l:
                return self.dump_dram(self.sq)
            if self.debug and self.debug[0] == "sv%d" % l:
                return self.dump_dram(self.sv)
            with nc.named_scope("attn%d" % l):
                self.stage_attn(l, last)
            if self.debug and self.debug[0] == "mix%d" % l:
                self.dump_big()
                return
            with nc.named_scope("out%d" % l):
                self.stage_out(l, last)
            if self.debug and self.debug[0] == "xo%d" % l:
                return self.dump_dram(self.x1 if not last else self.out)

    def dump_big(self):
        S = self.S
        b = Buf("dump")
        S.dma([lambda e: e.dma_start(out=self.dbg, in_=self.big[:])], b)
        S.barrier()

    def dump_dram(self, src):
        S = self.S
        b = Buf("dump")
        S.dma([lambda e: e.dma_start(out=self.dbg, in_=src)], b)
        S.barrier()

    class WS:
        def __init__(self, P, st, kch, nwb=3, conv=("pool",)):
            self.P = P
            self.kch = kch
            self.stage = Ring(P.nc, st, "wst", [128, kch, 128], F32, 2)
            self.wb = Ring(P.nc, st, "wbb", [128, kch, 128], BF16, nwb)
            self.conv = conv
            self.n = 0

        def fetch(self, w2d, c0, ncols, kch=None):
            P, S = self.P, self.P.S
            kch = kch or self.kch
            stg, bs = self.stage.next()
            wb, bw = self.wb.next()
            src = w2d[:, c0:c0 + ncols].rearrange("(kc p) n -> p kc n", p=128)
            S.dma([lambda e: e.dma_start(out=stg[:, :kch, :ncols], in_=src)], bs, writes=[bs])
            eng = self.conv[self.n % len(self.conv)]
            self.n += 1
            if eng == "act":
                S.op("act", lambda e: e.activation(out=wb[:, :kch, :ncols], in_=stg[:, :kch, :ncols], func=AF.Copy),
                     reads=[bs], writes=[bw])
            else:
                S.op(eng, lambda e: e.tensor_copy(wb[:, :kch, :ncols], stg[:, :kch, :ncols]), reads=[bs], writes=[bw])
            return wb, bw

    def stage_ada(self, l):
        S, nc = self.S, self.nc
        with ExitStack() as st:
            cf, cfb = self.load(st, "cf", [128, 32], F32, self.cfm)
            s2 = self.sb(st, "s2", [128, 16, 2], BF16)
            s2b = Buf("s2")
            sil = self.sb(st, "sil", [128, 32], F32)
            silb = Buf("sil")
            S.op("act", lambda e: e.activation(out=sil[:], in_=cf[:], func=AF.Silu), reads=[cfb], writes=[silb])
            S.op("dve", lambda e: e.tensor_copy(s2[:, :, 0], sil[:, 0:16]), reads=[silb], writes=[s2b])
            S.op("dve", lambda e: e.tensor_copy(s2[:, :, 1], sil[:, 16:32]), reads=[silb], writes=[s2b])
            modrow = self.sb(st, "modrow", [2, 3 * D], F32)
            mrb = Buf("modrow")
            badd, baddb = self.load(st, "badd", [2, 3 * D], F32, self.b_ada[l].partition_broadcast(2))
            g2, g2b = self.load(st, "g2", [2, D], F32, self.norm_g[l].partition_broadcast(2))
            ps = Ring(nc, st, "psA", [128, 512], F32, 2, psum=True)
            stg = Ring(nc, st, "adst", [128, 16, 256], F32, 2)
            wbr = Ring(nc, st, "adwb", [128, 16, 256], BF16, 2)
            w2d = self.w_ada[l]
            nsl = 3 * D // 256

            def fetch(i):
                t, b = stg.next()
                wb, bw = wbr.next()
                src = w2d[:, i * 256:(i + 1) * 256].rearrange("(kc p) n -> p kc n", p=128)
                S.dma([lambda e: e.dma_start(out=t[:, 0:8, :], in_=src[:, 0:8, :]),
                       lambda e: e.dma_start(out=t[:, 8:16, :], in_=src[:, 8:16, :])], b, writes=[b])
                S.op("pool", lambda e: e.tensor_copy(wb[:, 0:6, :], t[:, 0:6, :]), reads=[b], writes=[bw])
                S.op("act", lambda e: e.activation(out=wb[:, 6:11, :], in_=t[:, 6:11, :], func=AF.Copy), reads=[b], writes=[bw])
                S.op("dve", lambda e: e.tensor_copy(wb[:, 11:16, :], t[:, 11:16, :]), reads=[b], writes=[bw])
                return wb, bw

            q = [fetch(0)]
            for i in range(nsl):
                wb, bw = q.pop(0)
                if i + 1 < nsl:
                    q.append(fetch(i + 1))
                pt, pb = ps.next()
                for k in range(16):
                    S.op("pe", lambda e, k=k: e.matmul(pt[0:2, 0:256], lhsT=s2[:, k, :], rhs=wb[:, k, :],
                                                      start=(k == 0), stop=(k == 15)),
                         reads=[s2b, bw], writes=[pb], inc=(k == 15))
                S.op("dve", lambda e, i=i: e.tensor_copy(modrow[:, i * 256:(i + 1) * 256], pt[0:2, 0:256]),
                     reads=[pb], writes=[mrb])
            S.op("dve", lambda e: e.tensor_tensor(modrow[:], modrow[:], badd[:], ALU.add), reads=[mrb, baddb], writes=[mrb])
            S.op("dve", lambda e: e.scalar_tensor_tensor(out=modrow[:, D:2 * D], in0=modrow[:, D:2 * D], scalar=1.0,
                                                         in1=g2[:], op0=ALU.add, op1=ALU.mult),
                 reads=[mrb, g2b], writes=[mrb])
            S.dma([lambda e: e.dma_start(out=self.modv[l], in_=modrow[:])], mrb, reads=[mrb])
            S.end_stage()

    def stage_norm(self, l):
        S, nc = self.S, self.nc
        with ExitStack() as st:
            xr = Ring(nc, st, "xt", [128, D], F32, 3)
            junk = self.sb(st, "junk", [128, D], BF16)
            junkb = Buf("junk")
            tf = Ring(nc, st, "tf", [128, D], F32, 2)
            hb = Ring(nc, st, "hb", [128, D], BF16, 3)
            ssr = Ring(nc, st, "ssn", [128, 2], F32, 4)
            abc = self.sb(st, "abc", [128, D], F32)
            bbc = self.sb(st, "bbc", [128, D], F32)
            abcb, bbcb = Buf("abc"), Buf("bbc")
            pst = Ring(nc, st, "psT", [128, 8, 128], BF16, 4, psum=True)
            xsrc = self.x if l == 0 else self.x1
            csrc = self.ctx if l == 0 else self.hc1
            def P1(t):
                if t == 0 or t == NXT:
                    w = 0 if t == 0 else 1
                    S.dma([lambda e: e.dma_start(out=abc[:], in_=self.modv[l, w, D:2 * D].partition_broadcast(128))],
                          abcb, writes=[abcb])
                    S.dma([lambda e: e.dma_start(out=bbc[:], in_=self.modv[l, w, 0:D].partition_broadcast(128))],
                          bbcb, writes=[bbcb])
                src = xsrc[t * 128:(t + 1) * 128, :] if t < NXT else csrc[(t - NXT) * 128:(t - NXT + 1) * 128, :]
                xt, xb = xr.next()
                S.dma([lambda e: e.dma_start(out=xt[:], in_=src)], xb, writes=[xb])
                ss, ssb = ssr.next()
                S.op("act", lambda e: e.activation(out=junk[:], in_=xt[:], func=AF.Square, accum_out=ss[:, 0:1]),
                     reads=[xb], writes=[junkb, ssb])
                S.op("act", lambda e: e.activation(out=ss[:, 1:2], in_=ss[:, 0:1], func=AF.Sqrt, bias=EPS, scale=1.0 / D),
                     reads=[ssb], writes=[ssb])
                S.op("dve", lambda e: e.reciprocal(ss[:, 1:2], ss[:, 1:2]), reads=[ssb], writes=[ssb])
                t1, t1b = tf.next()
                S.op("dve", lambda e: e.scalar_tensor_tensor(out=t1[:], in0=xt[:], scalar=ss[:, 1:2], in1=abc[:],
                                                             op0=ALU.mult, op1=ALU.mult),
                     reads=[xb, ssb, abcb], writes=[t1b])
                h, hbb = hb.next()
                S.op("pool" if t % 2 else "dve", lambda e: e.tensor_tensor(h[:], t1[:], bbc[:], ALU.add),
                     reads=[t1b, bbcb], writes=[hbb])
                return h, hbb

            def P2(t, h, hbb):
                for g in range(2):
                    pt, pb = pst.next()
                    for i in range(8):
                        c = g * 8 + i
                        S.op("pe", lambda e: e.transpose(pt[:, i, :], h[:, c * 128:(c + 1) * 128], self.idb[:]),
                             reads=[hbb, self.cb], writes=[pb], inc=(i == 7))
                    if g == 0:
                        S.op("act", lambda e: e.activation(out=self.big[:, g * 8:(g + 1) * 8, t * 128:(t + 1) * 128],
                                                           in_=pt[:], func=AF.Copy),
                             reads=[pb], writes=[self.bigb[t]])
                    else:
                        S.op("dve", lambda e: e.tensor_copy(self.big[:, g * 8:(g + 1) * 8, t * 128:(t + 1) * 128], pt[:]),
                             reads=[pb], writes=[self.bigb[t]])

            pend = P1(0)
            for t in range(NT):
                nxt_ = P1(t + 1) if t + 1 < NT else None
                P2(t, *pend)
                pend = nxt_
            S.end_stage()

    def stage_proj(self, l, last):
        S, nc = self.S, self.nc
        G0 = l * NGAIN
        w2d = self.w_in[l]
        chunks = [(0, 512), (512, 512), (1024, 512), (1536, 512), (2048, 256)]
        with ExitStack() as st:
            ws = Prog.WS(self, st, 16, nwb=5, conv=("pool",))
            self.ps_p = Ring(nc, st, "psP", [128, 512], F32, 4, psum=True)
            self.ps_s = Ring(nc, st, "psS", [128, 512], F32, 2, psum=True)
            self.ps_r = Ring(nc, st, "psR", [128, 512], F32, 2, psum=True)
            self.sqr = Ring(nc, st, "sqr", [128, 512], BF16, 6)
            self.f32r = Ring(nc, st, "f32r", [128, 512], F32, 5)
            self.obr = Ring(nc, st, "obr", [128, 512], BF16, 8)
            self.qfr = Ring(nc, st, "qfr", [128, 512], F32, 6)
            vout = Ring(nc, st, "vout", [128, 128], BF16, 4)

            def feat_group(wb, bw, m0, m, tok0, n):
                pt, pb = self.ps_p.next()
                tiles = [self.bigb[i] for i in range(tok0 // 128, (tok0 + n) // 128)]
                for k in range(16):
                    S.op("pe", lambda e, k=k: e.matmul(pt[0:m, 0:n], lhsT=wb[:, k, m0:m0 + m], rhs=self.big[:, k, tok0:tok0 + n],
                                                      start=(k == 0), stop=(k == 15)),
                         reads=[bw] + tiles, writes=[pb], inc=(k == 15))
                return pt, pb

            with ExitStack() as st2:
                tab, tabb = self.load(st2, "ropeh", [128, 2 * SEQ], F32, self.rope_h)
                qk = []
                for (nm, nh, gcol, rope) in (("a_q", 4, 0, True), ("a_k", 2, 1, True), ("b_q", 4, 2, False), ("b_k", 4, 3, False),
                                             ("c_q", 4, 4, True), ("c_k", 2, 5, True)):
                    for h in range(nh):
                        qk.append((COL[nm] + h * 128, SQR[nm] + h * 128, gcol, rope))
                state = {"nxt": ws.fetch(w2d, qk[0][0], 128)}
                jobs = []
                for i, (c0, r0, gcol, rope) in enumerate(qk):
                    for ci, (tok0, n) in enumerate(chunks):
                        d = {}

                        def A(d=d, i=i, ci=ci, tok0=tok0, n=n):
                            if ci == 0:
                                state["cur"] = state["nxt"]
                                if i + 1 < len(qk):
                                    state["nxt"] = ws.fetch(w2d, qk[i + 1][0], 128)
                            wb, bw = state["cur"]
                            pt, pb = feat_group(wb, bw, 0, 128, tok0, n)
                            d["qf"], d["qfb"] = self.evac(pt, pb, 128, n)
                            d["sq"] = self.sq_part(d["qf"][0:128, 0:n], 128, d["qfb"], n)

                        def B(d=d, gcol=gcol, n=n):
                            rstd, rb = self.rstd_from([d["sq"]], 128.0, n)
                            d["qn"], d["qb"] = self.norm_apply(d["qf"][0:128, 0:n], 128, d["qfb"],
                                                               self.gn[:, G0 + gcol:G0 + gcol + 1], rstd, rb, n)

                        def C(d=d, rope=rope, tok0=tok0, n=n, r0=r0):
                            self.rope_store(d["qn"], d["qb"], 128, rope and tok0 < SEQ, tab, tabb, tok0, n,
                                            self.sq[r0:r0 + 128, tok0:tok0 + n], 128)
                        jobs.append((A, B, C))
                self.run_pipe(jobs)
                S.barrier()
            self.proj_d(l, st, ws, w2d, chunks, feat_group)
            S.barrier()
            vg = []
            for nm, ncol in (("a_v", 256), ("b_v", 512), ("c_v", 256)):
                for j in range(ncol // 128):
                    vg.append((COL[nm] + j * 128, SVC[nm] + j * 128))
            nxt = ws.fetch(w2d, vg[0][0], 128)
            for i, (c0, v0) in enumerate(vg):
                wb, bw = nxt
                if i + 1 < len(vg):
                    nxt = ws.fetch(w2d, vg[i + 1][0], 128)
                for t in range(NT):
                    pt, pb = self.ps_p.next()
                    for k in range(16):
                        S.op("pe", lambda e, k=k, t=t: e.matmul(pt[:, 0:128], lhsT=self.big[:, k, t * 128:(t + 1) * 128], rhs=wb[:, k, :],
                                                              start=(k == 0), stop=(k == 15)),
                             reads=[bw, self.bigb[t]], writes=[pb], inc=(k == 15))
                    vo, vob = vout.next()
                    eng = "act" if t % 2 == 0 else "dve"
                    if eng == "act":
                        S.op("act", lambda e: e.activation(out=vo[:], in_=pt[:, 0:128], func=AF.Copy), reads=[pb], writes=[vob])
                    else:
                        S.op("dve", lambda e: e.tensor_copy(vo[:], pt[:, 0:128]), reads=[pb], writes=[vob])
                    S.dma([lambda e, t=t: e.dma_start(out=self.sv[t * 128:(t + 1) * 128, v0:v0 + 128], in_=vo[:])], vob, reads=[vob])
            gg = []
            for bi, nm in enumerate(("a_g", "b_g", "c_g", "d_g")):
                for h in range(4):
                    gg.append((COL[nm] + h * 128, SQR["gate"] + (bi * 4 + h) * 128))
            gchunks = chunks[:4] if last else chunks
            nxt = ws.fetch(w2d, gg[0][0], 128)
            for i, (c0, r0) in enumerate(gg):
                wb, bw = nxt
                if i + 1 < len(gg):
                    nxt = ws.fetch(w2d, gg[i + 1][0], 128)
                for (tok0, n) in gchunks:
                    pt, pb = feat_group(wb, bw, 0, 128, tok0, n)
                    o, ob = self.obr.next()
                    S.op("act", lambda e: e.activation(out=o[:, 0:n], in_=pt[:, 0:n], func=AF.Silu), reads=[pb], writes=[ob])
                    S.dma([lambda e: e.dma_start(out=self.sq[r0:r0 + 128, tok0:tok0 + n], in_=o[:, 0:n])], ob, reads=[ob])
            S.end_stage()

    def evac(self, pt, pb, nr, n):
        qf, qfb = self.qfr.next()
        self.S.op("act", lambda e: e.activation(out=qf[0:nr, 0:n], in_=pt[0:nr, 0:n], func=AF.Copy), reads=[pb], writes=[qfb])
        return qf, qfb

    def rstd_of(self, parts, dim, n):
        S = self.S
        st_, sb_ = self.ps_s.next()
        for i, (src, nr, b) in enumerate(parts):
            sq, sqb = self.sqr.next()
            S.op("act", lambda e: e.activation(out=sq[0:nr, 0:n], in_=src, func=AF.Square), reads=[b], writes=[sqb])
            S.op("pe", lambda e: e.matmul(st_[:, 0:n], lhsT=self.ones[0:nr, :], rhs=sq[0:nr, 0:n],
                                          start=(i == 0), stop=(i == len(parts) - 1)),
                 reads=[sqb, self.cb], writes=[sb_])
        r, rb = self.f32r.next()
        S.op("act", lambda e: e.activation(out=r[:, 0:n], in_=st_[:, 0:n], func=AF.Sqrt, bias=EPS, scale=1.0 / dim),
             reads=[sb_], writes=[rb])
        S.op("dve", lambda e: e.reciprocal(r[:, 0:n], r[:, 0:n]), reads=[rb], writes=[rb])
        return r, rb

    def finish_qk(self, src, nr, sbuf_, gain, rstd, rb, rope, tab, tabb, tok0, n, dst, rdim):
        S = self.S
        qn, qb = self.obr.next()
        S.op("dve", lambda e: e.scalar_tensor_tensor(out=qn[0:nr, 0:n], in0=src, scalar=gain[0:nr, :], in1=rstd[0:nr, 0:n],
                                                     op0=ALU.mult, op1=ALU.mult),
             reads=[sbuf_, rb, self.gnb], writes=[qb])
        if rope:
            rt, rtb = self.ps_r.next()
            rm = self.rmb[0:128, 0:128] if rdim == 128 else self.rmb[0:64, 128:192]
            S.op("pe", lambda e: e.matmul(rt[0:nr, 0:n], lhsT=rm, rhs=qn[0:nr, 0:n], start=True, stop=True),
                 reads=[qb, self.cb], writes=[rtb])
            t1, t1b = self.f32r.next()
            t2, t2b = self.f32r.next()
            S.op("dve", lambda e: e.tensor_tensor(t1[0:nr, 0:n], qn[0:nr, 0:n], tab[0:nr, tok0:tok0 + n], ALU.mult),
                 reads=[qb, tabb], writes=[t1b])
            S.op("dve", lambda e: e.tensor_tensor(t2[0:nr, 0:n], rt[0:nr, 0:n], tab[0:nr, SEQ + tok0:SEQ + tok0 + n], ALU.mult),
                 reads=[rtb, tabb], writes=[t2b])
            o, ob = self.obr.next()
            S.op("dve", lambda e: e.tensor_tensor(o[0:nr, 0:n], t1[0:nr, 0:n], t2[0:nr, 0:n], ALU.add),
                 reads=[t1b, t2b], writes=[ob])
        else:
            o, ob = qn, qb
        S.dma([lambda e: e.dma_start(out=dst, in_=o[0:nr, 0:n])], ob, reads=[ob])

    def sq_part(self, src, nr, b, n):
        sq, sqb = self.sqr.next()
        self.S.op("act", lambda e: e.activation(out=sq[0:nr, 0:n], in_=src, func=AF.Square), reads=[b], writes=[sqb])
        return (sq, sqb, nr)

    def rstd_from(self, sqs, dim, n):
        S = self.S
        st_, sb_ = self.ps_s.next()
        for i, (sq, sqb, nr) in enumerate(sqs):
            S.op("pe", lambda e: e.matmul(st_[:, 0:n], lhsT=self.ones[0:nr, :], rhs=sq[0:nr, 0:n],
                                          start=(i == 0), stop=(i == len(sqs) - 1)),
                 reads=[sqb, self.cb], writes=[sb_], inc=(i == len(sqs) - 1))
        r, rb = self.f32r.next()
        S.op("act", lambda e: e.activation(out=r[:, 0:n], in_=st_[:, 0:n], func=AF.Sqrt, bias=EPS, scale=1.0 / dim),
             reads=[sb_], writes=[rb])
        S.op("dve", lambda e: e.reciprocal(r[:, 0:n], r[:, 0:n]), reads=[rb], writes=[rb])
        return r, rb

    def norm_apply(self, src, nr, sbuf_, gain, rstd, rb, n):
        qn, qb = self.obr.next()
        self.S.op("dve", lambda e: e.scalar_tensor_tensor(out=qn[0:nr, 0:n], in0=src, scalar=gain[0:nr, :], in1=rstd[0:nr, 0:n],
                                                          op0=ALU.mult, op1=ALU.mult),
                  reads=[sbuf_, rb, self.gnb], writes=[qb])
        return qn, qb

    def rope_store(self, qn, qb, nr, rope, tab, tabb, tok0, n, dst, rdim):
        S = self.S
        if rope:
            rt, rtb = self.ps_r.next()
            rm = self.rmb[0:128, 0:128] if rdim == 128 else self.rmb[0:64, 128:192]
            S.op("pe", lambda e: e.matmul(rt[0:nr, 0:n], lhsT=rm, rhs=qn[0:nr, 0:n], start=True, stop=True),
                 reads=[qb, self.cb], writes=[rtb])
            t1, t1b = self.f32r.next()
            t2, t2b = self.f32r.next()
            S.op("dve", lambda e: e.tensor_tensor(t1[0:nr, 0:n], qn[0:nr, 0:n], tab[0:nr, tok0:tok0 + n], ALU.mult),
                 reads=[qb, tabb], writes=[t1b])
            S.op("dve", lambda e: e.tensor_tensor(t2[0:nr, 0:n], rt[0:nr, 0:n], tab[0:nr, SEQ + tok0:SEQ + tok0 + n], ALU.mult),
                 reads=[rtb, tabb], writes=[t2b])
            o, ob = self.obr.next()
            S.op("dve", lambda e: e.tensor_tensor(o[0:nr, 0:n], t1[0:nr, 0:n], t2[0:nr, 0:n], ALU.add),
                 reads=[t1b, t2b], writes=[ob])
        else:
            o, ob = qn, qb
        S.dma([lambda e: e.dma_start(out=dst, in_=o[0:nr, 0:n])], ob, reads=[ob])

    def run_pipe(self, jobs):
        n = len(jobs)
        if n == 0:
            return
        jobs[0][0]()
        for s_ in range(n):
            if s_ + 1 < n:
                jobs[s_ + 1][0]()
            if s_ >= 1:
                jobs[s_ - 1][2]()
            jobs[s_][1]()
        jobs[n - 1][2]()

    def proj_d(self, l, st_outer, ws, w2d, chunks, feat_group):
        S, nc = self.S, self.nc
        G0 = l * NGAIN
        with ExitStack() as st:
            tab, tabb = self.load(st, "roper", [64, 2 * SEQ], F32, self.rope_r)
            state = {"nxt": (ws.fetch(w2d, COL["d_q"], 128), ws.fetch(w2d, COL["d_q"] + 128, 64))}
            jobs = []
            for h in range(4):
                for ci, (tok0, n) in enumerate(chunks):
                    d = {}

                    def A(d=d, h=h, ci=ci, tok0=tok0, n=n):
                        if ci == 0:
                            state["cur"] = state["nxt"]
                            if h + 1 < 4:
                                state["nxt"] = (ws.fetch(w2d, COL["d_q"] + (h + 1) * 192, 128),
                                                ws.fetch(w2d, COL["d_q"] + (h + 1) * 192 + 128, 64))
                        (wn, bwn), (wr, bwr) = state["cur"]
                        pn, pnb = feat_group(wn, bwn, 0, 128, tok0, n)
                        d["qn_f"], d["qn_fb"] = self.evac(pn, pnb, 128, n)
                        d["sqn"] = self.sq_part(d["qn_f"][0:128, 0:n], 128, d["qn_fb"], n)
                        pr, prb = feat_group(wr, bwr, 0, 64, tok0, n)
                        d["qr_f"], d["qr_fb"] = self.evac(pr, prb, 64, n)
                        d["sqr_"] = self.sq_part(d["qr_f"][0:64, 0:n], 64, d["qr_fb"], n)

                    def B(d=d, n=n):
                        rstd, rb = self.rstd_from([d["sqn"], d["sqr_"]], 192.0, n)
                        d["a"] = self.norm_apply(d["qn_f"][0:128, 0:n], 128, d["qn_fb"], self.gn[:, G0 + 6:G0 + 7], rstd, rb, n)
                        d["b"] = self.norm_apply(d["qr_f"][0:64, 0:n], 64, d["qr_fb"], self.gn[:, G0 + 7:G0 + 8], rstd, rb, n)

                    def C(d=d, h=h, tok0=tok0, n=n):
                        self.rope_store(d["a"][0], d["a"][1], 128, False, None, None, tok0, n,
                                        self.sq[SQR["d_qn"] + h * 128:SQR["d_qn"] + (h + 1) * 128, tok0:tok0 + n], 128)
                        self.rope_store(d["b"][0], d["b"][1], 64, tok0 < SEQ, tab, tabb, tok0, n,
                                        self.sq[SQR["d_qr"] + h * 64:SQR["d_qr"] + (h + 1) * 64, tok0:tok0 + n], 64)
                    jobs.append((A, B, C))
            self.run_pipe(jobs)
            wck = [ws.fetch(w2d, COL["d_ckv"] + j * 128, 128) for j in range(4)]
            ckvn = self.sb(st, "ckvn", [128, 4, T], BF16)
            ckb = Buf("ckvn")
            kr = self.sb(st, "krraw", [64, T], F32)
            krb = Buf("krraw")
            wkr, wkrb = ws.fetch(w2d, COL["d_kr"], 64)
            for (tok0, n) in chunks:
                parts = []
                for j in range(4):
                    pt, pb = self.ps_p.next()
                    tiles = [self.bigb[i] for i in range(tok0 // 128, (tok0 + n) // 128)]
                    for k in range(16):
                        S.op("pe", lambda e, k=k, j=j, pt=pt: e.matmul(pt[:, 0:n], lhsT=wck[j][0][:, k, :],
                                                                   rhs=self.big[:, k, tok0:tok0 + n], start=(k == 0), stop=(k == 15)),
                             reads=[wck[j][1]] + tiles, writes=[pb], inc=(k == 15))
                    pt, pb = self.evac(pt, pb, 128, n)
                    parts.append((pt[0:128, 0:n], 128, pb))
                rstd, rb = self.rstd_of(parts, 512.0, n)
                for j in range(4):
                    src, _, pb = parts[j]
                    S.op("dve", lambda e, j=j, src=src: e.scalar_tensor_tensor(out=ckvn[:, j, tok0:tok0 + n], in0=src,
                                                                           scalar=self.gn[:, G0 + 10 + j:G0 + 11 + j], in1=rstd[:, 0:n],
                                                                           op0=ALU.mult, op1=ALU.mult),
                         reads=[pb, rb, self.gnb], writes=[ckb])
                pr, prb = feat_group(wkr, wkrb, 0, 64, tok0, n)
                S.op("act", lambda e: e.activation(out=kr[:, tok0:tok0 + n], in_=pr[0:64, 0:n], func=AF.Copy), reads=[prb], writes=[krb])
            wsu = Prog.WS(self, st, 4, nwb=8, conv=("pool",))
            vout = Ring(nc, st, "voutd", [128, 128], BF16, 4)
            wks = [wsu.fetch(self.w_uk[l], h * 128, 128) for h in range(4)]
            wvs = [wsu.fetch(self.w_uv[l], h * 128, 128) for h in range(4)]
            jobs = []
            for h in range(4):
                for ci, (tok0, n) in enumerate(chunks):
                    d = {}

                    def A(d=d, h=h, tok0=tok0, n=n):
                        wk, wkb = wks[h]
                        pt, pb = self.ps_p.next()
                        for k in range(4):
                            S.op("pe", lambda e, k=k: e.matmul(pt[:, 0:n], lhsT=wk[:, k, :], rhs=ckvn[:, k, tok0:tok0 + n],
                                                              start=(k == 0), stop=(k == 3)),
                                 reads=[wkb, ckb], writes=[pb], inc=(k == 3))
                        d["kf"], d["kfb"] = self.evac(pt, pb, 128, n)
                        d["sqn"] = self.sq_part(d["kf"][0:128, 0:n], 128, d["kfb"], n)
                        d["sqr_"] = self.sq_part(kr[0:64, tok0:tok0 + n], 64, krb, n)

                    def B(d=d, tok0=tok0, n=n):
                        rstd, rb = self.rstd_from([d["sqn"], d["sqr_"]], 192.0, n)
                        d["a"] = self.norm_apply(d["kf"][0:128, 0:n], 128, d["kfb"], self.gn[:, G0 + 8:G0 + 9], rstd, rb, n)
                        d["b"] = self.norm_apply(kr[0:64, tok0:tok0 + n], 64, krb, self.gn[:, G0 + 9:G0 + 10], rstd, rb, n)

                    def C(d=d, h=h, tok0=tok0, n=n):
                        self.rope_store(d["a"][0], d["a"][1], 128, False, None, None, tok0, n,
                                        self.sq[SQR["d_kn"] + h * 128:SQR["d_kn"] + (h + 1) * 128, tok0:tok0 + n], 128)
                        self.rope_store(d["b"][0], d["b"][1], 64, tok0 < SEQ, tab, tabb, tok0, n,
                                        self.sq[SQR["d_kr"] + h * 64:SQR["d_kr"] + (h + 1) * 64, tok0:tok0 + n], 64)
                    jobs.append((A, B, C))
            self.run_pipe(jobs)
            for h in range(4):
                wv, wvb = wvs[h]
                v0 = SVC["d_v"] + h * 128
                for t in range(NT):
                    pt, pb = self.ps_p.next()
                    for k in range(4):
                        S.op("pe", lambda e, k=k, t=t: e.matmul(pt[:, 0:128], lhsT=ckvn[:, k, t * 128:(t + 1) * 128], rhs=wv[:, k, :],
                                                              start=(k == 0), stop=(k == 3)),
                             reads=[wvb, ckb], writes=[pb], inc=(k == 3))
                    vo, vob = vout.next()
                    S.op("dve", lambda e: e.tensor_copy(vo[:], pt[:, 0:128]), reads=[pb], writes=[vob])
                    S.dma([lambda e, t=t: e.dma_start(out=self.sv[t * 128:(t + 1) * 128, v0:v0 + 128], in_=vo[:])], vob, reads=[vob])
            S.barrier()

    def stage_attn(self, l, last):
        S, nc = self.S, self.nc
        with ExitStack() as st:
            qr_ = Ring(nc, st, "aq", [128, T], BF16, 2)
            kr_ = Ring(nc, st, "ak", [128, T], BF16, 2)
            q2_ = Ring(nc, st, "aq2", [64, T], BF16, 2)
            k2_ = Ring(nc, st, "ak2", [64, T], BF16, 2)
            vr_ = Ring(nc, st, "av", [128, NT, 128], BF16, 2)
            gr_ = Ring(nc, st, "ag", [128, T], BF16, 2)
            self.pS = Ring(nc, st, "pS", [128, 512], F32, 4, psum=True)
            self.pO = Ring(nc, st, "pO", [128, 512], F32, 2, psum=True)
            self.pN = Ring(nc, st, "pN", [128, 512], F32, 2, psum=True)
            self.pr = Ring(nc, st, "pr", [128, 512], BF16, 6)
            self.tr = Ring(nc, st, "tr", [128, 512], F32, 6)
            pt_, ptb = self.load(st, "ptab", [128, 4 * NPT * 64], F32, self.rpbx[l])
            with ExitStack() as st2:
                mb, mbb = self.load(st2, "mbt", [128, NPT * 64], F32, self.maskb)
                for h in range(4):
                    S.op("dve", lambda e, h=h: e.tensor_tensor(pt_[:, h * NPT * 64:(h + 1) * NPT * 64], pt_[:, h * NPT * 64:(h + 1) * NPT * 64],
                                                               mb[:], ALU.add), reads=[ptb, mbb], writes=[ptb])
                S.barrier()
            mc, mcb = self.load(st, "mct", [128, 256], F32, self.maskc)
            sk, skb = self.load(st, "sk", [128, DEPTH * 4], F32, self.sink[0].partition_broadcast(128))
            S.op("act", lambda e: e.activation(out=sk[:], in_=sk[:], func=AF.Exp), reads=[skb], writes=[skb])

            heads = []
            for h in range(4):
                heads.append(dict(kind="glob", q=[(SQR["a_q"] + h * 128, 128)], k=[(SQR["a_k"] + (h // 2) * 128, 128)],
                                  v=SVC["a_v"] + (h // 2) * 128, g=SQR["gate"] + h * 128, mc=h, scale=128 ** -0.5, sink=None))
            for h in range(4):
                heads.append(dict(kind="nbr", q=[(SQR["b_q"] + h * 128, 128)], k=[(SQR["b_k"] + h * 128, 128)],
                                  v=SVC["b_v"] + h * 128, g=SQR["gate"] + (4 + h) * 128, mc=4 + h, scale=128 ** -0.5, sink=None, h=h))
            for h in range(4):
                heads.append(dict(kind="win", q=[(SQR["c_q"] + h * 128, 128)], k=[(SQR["c_k"] + (h // 2) * 128, 128)],
                                  v=SVC["c_v"] + (h // 2) * 128, g=SQR["gate"] + (8 + h) * 128, mc=8 + h, scale=128 ** -0.5,
                                  sink=l * 4 + h))
            for h in range(4):
                heads.append(dict(kind="glob", q=[(SQR["d_qn"] + h * 128, 128), (SQR["d_qr"] + h * 64, 64)],
                                  k=[(SQR["d_kn"] + h * 128, 128), (SQR["d_kr"] + h * 64, 64)],
                                  v=SVC["d_v"] + h * 128, g=SQR["gate"] + (12 + h) * 128, mc=12 + h, scale=192 ** -0.5, sink=None))
            if self.debug and self.debug[0].startswith("mix") and len(self.debug) > 3:
                heads = [heads[i] for i in self.debug[3]]

            def loadhead(hd):
                r = {}
                q, qb = qr_.next()
                k, kb = kr_.next()
                v, vb = vr_.next()
                g, gb = gr_.next()
                r0, _ = hd["q"][0]
                S.dma([lambda e: e.dma_start(out=q[:], in_=self.sq[r0:r0 + 128, :])], qb, writes=[qb])
                k0, _ = hd["k"][0]
                S.dma([lambda e: e.dma_start(out=k[:], in_=self.sq[k0:k0 + 128, :])], kb, writes=[kb])
                v0 = hd["v"]
                S.dma([lambda e: e.dma_start(out=v[:], in_=self.sv[:, v0:v0 + 128].rearrange("(j p) c -> p j c", p=128))], vb, writes=[vb])
                g0 = hd["g"]
                S.dma([lambda e: e.dma_start(out=g[:], in_=self.sq[g0:g0 + 128, :])], gb, writes=[gb])
                r["qp"] = [(q, 128, qb)]
                r["kp"] = [(k, 128, kb)]
                if len(hd["q"]) > 1:
                    q2, q2b = q2_.next()
                    k2, k2b = k2_.next()
                    r1, _ = hd["q"][1]
                    k1, _ = hd["k"][1]
                    S.dma([lambda e: e.dma_start(out=q2[:], in_=self.sq[r1:r1 + 64, :])], q2b, writes=[q2b])
                    S.dma([lambda e: e.dma_start(out=k2[:], in_=self.sq[k1:k1 + 64, :])], k2b, writes=[k2b])
                    r["qp"].append((q2, 64, q2b))
                    r["kp"].append((k2, 64, k2b))
                r["v"], r["vb"], r["g"], r["gb"] = v, vb, g, gb
                return r

            nxt = loadhead(heads[0])
            for hi, hd in enumerate(heads):
                cur = nxt
                if hi + 1 < len(heads):
                    nxt = loadhead(heads[hi + 1])
                esink = sk[:, hd["sink"]:hd["sink"] + 1] if hd["sink"] is not None else None
                if hd["kind"] == "glob":
                    for qb_ in range(4):
                        self.attn_block(cur, hd, qb_ * 512, 512, [(j, None, None) for j in range(NT)], esink, skb)
                elif hd["kind"] == "win":
                    prev = None
                    for i in range(NXT):
                        groups = [([16, 17, i], None, None)]
                        if i == 0:
                            groups.append(([i + 1], mc[:, 128:256], mcb))
                        elif i == NXT - 1:
                            groups.append(([i - 1], mc[:, 0:128], mcb))
                        else:
                            groups.append(([i - 1, i + 1], mc[:, 0:256], mcb))
                        stt_ = self.attn_b1(cur, hd, i * 128, 128, groups)
                        if prev is not None:
                            self.attn_b2(cur, hd, prev, esink, skb)
                        prev = stt_
                    self.attn_b2(cur, hd, prev, esink, skb)
                else:
                    h = hd["h"]
                    prev = None
                    for qrow in range(32):
                        rs = min(max(qrow - 4, 0), 24)
                        if rs % 2 == 0:
                            a0 = rs - qrow + 7
                            p0 = a0 // 2 if a0 % 2 == 0 else 7 + (a0 - 1) // 2
                            js = [rs // 2 + c for c in range(4)]
                        else:
                            p0 = 14
                            js = [(rs - 1) // 2 + c for c in range(5)]
                        b0 = (h * NPT + p0) * 64
                        groups = [(js, pt_[:, b0:b0 + len(js) * 64], ptb), ([16, 17], None, None)]
                        stt_ = self.attn_b1(cur, hd, qrow * 64, 64, groups)
                        if prev is not None:
                            self.attn_b2(cur, hd, prev, esink, skb)
                        prev = stt_
                    self.attn_b2(cur, hd, prev, esink, skb)
                if not last:
                    self.attn_block(cur, hd, SEQ, CTXL, [(16, None, None), (17, None, None)], esink, skb)
            S.end_stage()

    def attn_block(self, cur, hd, q0, nq, chunks, esink, skb):
        S = self.S
        scale = hd["scale"]
        po, pob = self.pO.next()
        pn, pnb = self.pN.next()
        v, vb = cur["v"], cur["vb"]
        pend = None
        nch = len(chunks)

        def pv(idx, j, p, pb_):
            S.op("pe", lambda e: e.matmul(po[:, 0:nq], lhsT=v[:, j, :], rhs=p[:, 0:nq], start=(idx == 0), stop=(idx == nch - 1)),
                 reads=[vb, pb_], writes=[pob], inc=False)
            S.op("pe", lambda e: e.matmul(pn[:, 0:nq], lhsT=self.ones[:, :], rhs=p[:, 0:nq], start=(idx == 0), stop=(idx == nch - 1)),
                 reads=[self.cb, pb_], writes=[pnb])

        for idx, (j, bias, biasb) in enumerate(chunks):
            ps, psb = self.pS.next()
            np_ = len(cur["qp"])
            for pi in range(np_):
                qt, nr, qb_ = cur["qp"][pi]
                kt, _, kb_ = cur["kp"][pi]
                S.op("pe", lambda e, pi=pi, qt=qt, kt=kt, nr=nr: e.matmul(ps[:, 0:nq], lhsT=kt[0:nr, j * 128:(j + 1) * 128],
                                                                         rhs=qt[0:nr, q0:q0 + nq], start=(pi == 0), stop=(pi == np_ - 1)),
                     reads=[qb_, kb_], writes=[psb], inc=(pi == np_ - 1))
            p, pb_ = self.pr.next()
            if bias is None:
                S.op("act", lambda e: e.activation(out=p[:, 0:nq], in_=ps[:, 0:nq], func=AF.Exp, scale=scale), reads=[psb], writes=[pb_])
            else:
                t, tb = self.tr.next()
                S.op("dve", lambda e: e.scalar_tensor_tensor(out=t[:, 0:nq], in0=ps[:, 0:nq], scalar=scale, in1=bias,
                                                             op0=ALU.mult, op1=ALU.add), reads=[psb, biasb], writes=[tb])
                S.op("act", lambda e: e.activation(out=p[:, 0:nq], in_=t[:, 0:nq], func=AF.Exp), reads=[tb], writes=[pb_])
            if pend is not None:
                pv(*pend)
            pend = (idx, j, p, pb_)
        pv(*pend)
        ri, rib = self.tr.next()
        if esink is not None:
            S.op("dve", lambda e: e.tensor_scalar(ri[:, 0:nq], pn[:, 0:nq], esink, None, op0=ALU.add), reads=[pnb, skb], writes=[rib])
            S.op("dve", lambda e: e.reciprocal(ri[:, 0:nq], ri[:, 0:nq]), reads=[rib], writes=[rib])
        else:
            S.op("dve", lambda e: e.reciprocal(ri[:, 0:nq], pn[:, 0:nq]), reads=[pnb], writes=[rib])
        o, ob = self.tr.next()
        S.op("dve", lambda e: e.tensor_tensor(o[:, 0:nq], po[:, 0:nq], ri[:, 0:nq], ALU.mult), reads=[pob, rib], writes=[ob])
        tiles = [self.bigb[i] for i in range(q0 // 128, (q0 + nq + 127) // 128)]
        g, gb = cur["g"], cur["gb"]
        S.op("pool", lambda e: e.tensor_tensor(self.big[:, hd["mc"], q0:q0 + nq], o[:, 0:nq], g[:, q0:q0 + nq], ALU.mult),
             reads=[ob, gb], writes=tiles)

    def attn_b1(self, cur, hd, q0, nq, groups):
        S = self.S
        scale = hd["scale"]
        pend = []
        for (js, bias, biasb) in groups:
            ps, psb = self.pS.next()
            w = len(js) * nq
            np_ = len(cur["qp"])
            for c, j in enumerate(js):
                for pi in range(np_):
                    qt, nr, qb_ = cur["qp"][pi]
                    kt, _, kb_ = cur["kp"][pi]
                    S.op("pe", lambda e: e.matmul(ps[:, c * nq:(c + 1) * nq], lhsT=kt[0:nr, j * 128:(j + 1) * 128],
                                                  rhs=qt[0:nr, q0:q0 + nq], start=(pi == 0), stop=(pi == np_ - 1)),
                         reads=[qb_, kb_], writes=[psb], inc=(pi == np_ - 1 and c == len(js) - 1))
            p, pb_ = self.pr.next()
            if bias is None:
                S.op("act", lambda e: e.activation(out=p[:, 0:w], in_=ps[:, 0:w], func=AF.Exp, scale=scale), reads=[psb], writes=[pb_])
            else:
                t, tb = self.tr.next()
                S.op("dve", lambda e: e.scalar_tensor_tensor(out=t[:, 0:w], in0=ps[:, 0:w], scalar=scale, in1=bias,
                                                             op0=ALU.mult, op1=ALU.add), reads=[psb, biasb], writes=[tb])
                S.op("act", lambda e: e.activation(out=p[:, 0:w], in_=t[:, 0:w], func=AF.Exp), reads=[tb], writes=[pb_])
            for c, j in enumerate(js):
                pend.append((j, p, pb_, c))
        return (q0, nq, pend)

    def attn_b2(self, cur, hd, state, esink, skb):
        S = self.S
        q0, nq, pend = state
        po, pob = self.pO.next()
        pn, pnb = self.pN.next()
        v, vb = cur["v"], cur["vb"]
        total = len(pend)
        for idx, (j, p, pb_, c) in enumerate(pend):
            S.op("pe", lambda e: e.matmul(po[:, 0:nq], lhsT=v[:, j, :], rhs=p[:, c * nq:(c + 1) * nq],
                                          start=(idx == 0), stop=(idx == total - 1)), reads=[vb, pb_], writes=[pob], inc=False)
            S.op("pe", lambda e: e.matmul(pn[:, 0:nq], lhsT=self.ones[:, :], rhs=p[:, c * nq:(c + 1) * nq],
                                          start=(idx == 0), stop=(idx == total - 1)), reads=[self.cb, pb_], writes=[pnb])
        self.attn_fin(cur, hd, q0, nq, po, pob, pn, pnb, esink, skb)

    def attn_fin(self, cur, hd, q0, nq, po, pob, pn, pnb, esink, skb):
        S = self.S
        ri, rib = self.tr.next()
        if esink is not None:
            S.op("dve", lambda e: e.tensor_scalar(ri[:, 0:nq], pn[:, 0:nq], esink, None, op0=ALU.add), reads=[pnb, skb], writes=[rib])
            S.op("dve", lambda e: e.reciprocal(ri[:, 0:nq], ri[:, 0:nq]), reads=[rib], writes=[rib])
        else:
            S.op("dve", lambda e: e.reciprocal(ri[:, 0:nq], pn[:, 0:nq]), reads=[pnb], writes=[rib])
        o, ob = self.tr.next()
        S.op("dve", lambda e: e.tensor_tensor(o[:, 0:nq], po[:, 0:nq], ri[:, 0:nq], ALU.mult), reads=[pob, rib], writes=[ob])
        tiles = [self.bigb[i] for i in range(q0 // 128, (q0 + nq + 127) // 128)]
        g, gb = cur["g"], cur["gb"]
        S.op("pool", lambda e: e.tensor_tensor(self.big[:, hd["mc"], q0:q0 + nq], o[:, 0:nq], g[:, q0:q0 + nq], ALU.mult),
             reads=[ob, gb], writes=tiles)

    def stage_out(self, l, last):
        S, nc = self.S, self.nc
        w2d = self.w_out[l]
        ntile = NXT if last else NT
        with ExitStack() as st:
            ws = Prog.WS(self, st, 16, nwb=8, conv=("act", "pool"))
            psO = Ring(nc, st, "psO", [128, 512], F32, 3, psum=True)
            xr = Ring(nc, st, "xo", [128, 512], F32, 4)
            yr = Ring(nc, st, "yo", [128, 512], F32, 4)
            gt = self.sb(st, "gtx", [128, D], F32)
            gtb = Buf("gtx")
            gc = self.sb(st, "gtc", [128, D], F32)
            gcb = Buf("gtc")
            S.dma([lambda e: e.dma_start(out=gt[:], in_=self.modv[l, 0, 2 * D:3 * D].partition_broadcast(128))], gtb, writes=[gtb])
            S.dma([lambda e: e.dma_start(out=gc[:], in_=self.modv[l, 1, 2 * D:3 * D].partition_broadcast(128))], gcb, writes=[gcb])
            xsrc = self.x if l == 0 else self.x1
            csrc = self.ctx if l == 0 else self.hc1
            xdst = self.out if last else self.x1
            slabs = [ws.fetch(w2d, j * 128, 128) for j in range(4)]
            for cb_ in range(4):
                cur = slabs
                if cb_ + 1 < 4:
                    slabs = [ws.fetch(w2d, (cb_ + 1) * 512 + j * 128, 128) for j in range(4)]
                c0 = cb_ * 512
                def tile_io(t):
                    if t < NXT:
                        return (xsrc[t * 128:(t + 1) * 128, c0:c0 + 512], xdst[t * 128:(t + 1) * 128, c0:c0 + 512], gt, gtb)
                    return (csrc[(t - NXT) * 128:(t - NXT + 1) * 128, c0:c0 + 512],
                            self.hc1[(t - NXT) * 128:(t - NXT + 1) * 128, c0:c0 + 512], gc, gcb)

                def xload(t):
                    xt, xb = xr.next()
                    src = tile_io(t)[0]
                    S.dma([lambda e: e.dma_start(out=xt[:], in_=src)], xb, writes=[xb])
                    return xt, xb

                xq = [xload(0), xload(1)]
                for t in range(ntile):
                    _, dst, gg, ggb = tile_io(t)
                    xt, xb = xq.pop(0)
                    if t + 2 < ntile:
                        xq.append(xload(t + 2))
                    pt, pb = psO.next()
                    for j in range(4):
                        wb, bw = cur[j]
                        for k in range(16):
                            S.op("pe", lambda e: e.matmul(pt[:, j * 128:(j + 1) * 128], lhsT=self.big[:, k, t * 128:(t + 1) * 128],
                                                          rhs=wb[:, k, :], start=(k == 0), stop=(k == 15)),
                                 reads=[bw, self.bigb[t]], writes=[pb], inc=(k == 15))
                    y, yb = yr.next()
                    S.op("dve", lambda e: e.tensor_tensor(y[:], pt[:], gg[:, c0:c0 + 512], ALU.mult), reads=[pb, ggb], writes=[yb])
                    S.op("dve", lambda e: e.tensor_tensor(y[:], y[:], xt[:], ALU.add), reads=[yb, xb], writes=[yb])
                    S.dma([lambda e: e.dma_start(out=dst, in_=y[:])], yb, reads=[yb])
            S.end_stage()


def _rope_tables(rot_dim):
    t = np.arange(SEQ)
    row = (t // 64).astype(np.float32)
    col = (t % 64).astype(np.float32)
    nf = rot_dim // 4
    inv = (10000.0 ** (-np.arange(nf, dtype=np.float32) / nf)).astype(np.float32)
    ang = np.concatenate([row[:, None] * inv, col[:, None] * inv], axis=-1).astype(np.float32)
    cos = np.cos(ang).astype(np.float32).T
    sin = np.sin(ang).astype(np.float32).T
    return np.ascontiguousarray(np.concatenate([np.concatenate([cos, cos], 0), np.concatenate([sin, sin], 0)], axis=1))


def _rmat():
    r = np.zeros((128, 192), np.float32)
    for m in range(64):
        r[m + 64, m] = -1.0
        r[m, m + 64] = 1.0
    for m in range(32):
        r[m + 32, 128 + m] = -1.0
        r[m, 128 + m + 32] = 1.0
    return r


def _nbr_entries():
    ent = [(a, a + 1) for a in range(0, 14, 2)] + [(a, a + 1) for a in range(1, 14, 2)]
    ent += [(None, 3), (4, 5), (6, 7), (8, 9), (10, None)]
    return ent


def _nbr_tables(rpb):
    ent = _nbr_entries()
    kcol = np.arange(64)[:, None]
    qcol = np.arange(64)[None, :]
    cs = np.clip(qcol - 8, 0, 48)
    colvalid = (kcol >= cs) & (kcol < cs + 16)
    dc = np.clip(kcol - qcol + 15, 0, 30)
    mask = np.full((NPT, 128, 64), NEG, np.float32)
    idx_a = np.zeros((NPT, 128), np.int64)
    blk_ok = np.zeros((NPT, 128), bool)
    for e, (at, ab) in enumerate(ent):
        for half, a in ((0, at), (1, ab)):
            sl = slice(half * 64, half * 64 + 64)
            if a is None:
                continue
            idx_a[e, sl] = a
            blk_ok[e, sl] = True
            mask[e, sl, :] = np.where(colvalid, 0.0, NEG)
    dcf = np.concatenate([dc, dc], 0)
    g = rpb[:, :, idx_a[:, :, None], dcf[None, :, :]]
    g = np.where(blk_ok[None, None, :, :, None], g, np.float32(0.0)).astype(np.float32)
    rpbx = np.ascontiguousarray(g.transpose(0, 3, 1, 2, 4).reshape(DEPTH, 128, 4 * NPT * 64))
    maskb = np.ascontiguousarray(mask.transpose(1, 0, 2).reshape(128, NPT * 64))
    return rpbx, maskb


def _maskc():
    p = np.arange(128)[:, None]
    f = np.arange(128)[None, :]
    prev = np.where(f <= p, 0.0, NEG).astype(np.float32)
    nxt = np.where(p <= f, 0.0, NEG).astype(np.float32)
    return np.ascontiguousarray(np.concatenate([prev, nxt], axis=1))


def _col(v, n=128):
    o = np.zeros((128,), np.float32)
    o[:len(v)] = v
    return o


def make_in_maps(inp):
    f = lambda a: np.ascontiguousarray(np.asarray(a, dtype=np.float32))
    gains = np.zeros((128, DEPTH * NGAIN), np.float32)
    for l in range(DEPTH):
        cols = [inp["a_q_g"][l], inp["a_k_g"][l], inp["b_q_g"][l], inp["b_k_g"][l], inp["c_q_g"][l], inp["c_k_g"][l],
                inp["d_q_g"][l][:128], inp["d_q_g"][l][128:], inp["d_k_g"][l][:128], inp["d_k_g"][l][128:]]
        cols += [inp["d_kv_g"][l][j * 128:(j + 1) * 128] for j in range(4)]
        for j, c in enumerate(cols):
            gains[:, l * NGAIN + j] = _col(np.asarray(c, np.float32))
    rpbx, maskb = _nbr_tables(f(inp["b_rpb"]))
    shared = dict(norm_g=f(inp["norm_g"]), b_ada=f(inp["b_ada"]), w_ada=f(inp["w_ada"]), w_in=f(inp["w_in"]),
                  w_out=f(inp["w_out"]), w_uk=f(inp["d_w_uk"]), w_uv=f(inp["d_w_uv"]), gains=gains,
                  sink=f(inp["c_sink"]).reshape(1, DEPTH * 4), rpbx=rpbx, maskb=maskb, maskc=_maskc(),
                  ident=np.eye(128, dtype=np.float32), rmat=_rmat(), rope_h=_rope_tables(128), rope_r=_rope_tables(64))
    cctx = f(inp["c_ctx"]).reshape(16, 128).T
    maps = []
    for b in range(NCORES):
        cf = np.ascontiguousarray(np.concatenate([f(inp["c"][b]).reshape(16, 128).T, cctx], axis=1))
        m = dict(shared)
        m.update(x=f(inp["x"][b]), ctx=f(inp["ctx"][b]), cfm=cf)
        maps.append(m)
    return maps


_PROG = {}


def kernel(**inputs):
    if "p" not in _PROG:
        _PROG["p"] = Prog()
    prog = _PROG["p"]
    maps = make_in_maps(inputs)
    res = run_bass_kernel_spmd(prog.nc, maps, core_ids=list(range(NCORES)))
    return np.stack([np.asarray(r["out"], dtype=np.float32) for r in res.results], axis=0)
```
